# Optimizing a Trainium2 kernel written in Bass

```python
import math
import jax, jax.numpy as jnp
from jax import lax
import numpy as np

D_MODEL = 2048
BATCH = 2
SEQ = 4096
DEPTH = 2

RWKV_HEAD_DIM = 64
RWKV_WIDTH = D_MODEL // 2
RWKV_HEADS = RWKV_WIDTH // RWKV_HEAD_DIM
DECAY_LORA = 64
ICLR_LORA = 64
VRES_LORA = 32
GATE_LORA = 160
LNX_EPS = 64e-5

ATTN_GROUPS = ((128, 1), (512, 4), (2048, 16))
ATTN_HEADS_PER_GROUP = 4
ATTN_HEAD_DIM = D_MODEL // 16
ATTN_HEADS = len(ATTN_GROUPS) * ATTN_HEADS_PER_GROUP
ATTN_OUT_WIDTH = ATTN_HEADS_PER_GROUP * ATTN_HEAD_DIM

SGU_CHUNK = 128
SGU_GROUPS = 4
SGU_WIDTH = D_MODEL // 4
SGU_GROUP_DIM = SGU_WIDTH // SGU_GROUPS

FFN_DIM = ((8 * D_MODEL // 3 + 127) // 128) * 128
RMS_EPS = 1e-6
LN_EPS = 1e-5
N_BRANCHES = 3

N_SHIFT = 3 * RWKV_WIDTH + DECAY_LORA + ICLR_LORA + GATE_LORA
N_QKV = 3 * ATTN_HEADS * ATTN_HEAD_DIM
N_SGU = 2 * SGU_WIDTH
N_GATE = N_BRANCHES * D_MODEL
N_IN = N_SHIFT + N_QKV + N_SGU + N_GATE

kernel_name = "hybrid_rwkv7_dilated_alibi_sgu_macaron"


def rms_norm(x, g):
    xf = x.astype(jnp.float32)
    y = xf * lax.rsqrt(jnp.mean(xf * xf, axis=-1, keepdims=True) + RMS_EPS)
    return (y * g.astype(jnp.float32)).astype(x.dtype)


def token_shift(p):
    return jnp.pad(p, ((0, 0), (1, 0), (0, 0)))[:, :-1]


def swiglu_ffn(x, w_gu, w_down):
    gate, up = jnp.split(x @ w_gu, 2, axis=-1)
    return (jax.nn.silu(gate) * up) @ w_down


def alibi_slopes(n_heads):
    def geometric(n):
        start = 2.0 ** (-8.0 / n)
        return [start ** (i + 1) for i in range(n)]
    closest = 2 ** int(math.floor(math.log2(n_heads)))
    slopes = geometric(closest)
    if closest < n_heads:
        slopes += geometric(2 * closest)[0::2][: n_heads - closest]
    return np.array(sorted(slopes, reverse=True), dtype=np.float32)


def wkv7_scan(r, w, k, v, a, b):
    B, T, H, N = r.shape

    def step(S, inp):
        r_t, w_t, k_t, v_t, a_t, b_t = inp
        sa = jnp.einsum('bhij,bhj->bhi', S, a_t)
        S = (S * w_t[:, :, None, :] + sa[:, :, :, None] * b_t[:, :, None, :]
             + v_t[:, :, :, None] * k_t[:, :, None, :])
        return S, jnp.einsum('bhij,bhj->bhi', S, r_t)

    S0 = jnp.zeros((B, H, N, N), jnp.float32)
    xs = tuple(jnp.moveaxis(z, 1, 0) for z in (r, w, k, v, a, b))
    _, ys = lax.scan(step, S0, xs)
    return jnp.moveaxis(ys, 0, 1)


def rwkv7_time_mix(xr, xk, v, xw, xa, xg, decay_w0, decay_w2, iclr_a0, iclr_a2, gate_g2,
                   k_k, k_a, r_k, lnx_w, lnx_b):
    B, T, _ = xr.shape
    H, N = RWKV_HEADS, RWKV_HEAD_DIM
    f32 = jnp.float32
    heads = lambda z: z.astype(f32).reshape(B, T, H, N)
    w_log = -jax.nn.softplus(-(decay_w0 + jnp.tanh(xw) @ decay_w2).astype(f32)) - 0.5
    decay = jnp.exp(-jnp.exp(w_log))
    a = jax.nn.sigmoid((iclr_a0 + xa @ iclr_a2).astype(f32))
    g = jax.nn.sigmoid(xg) @ gate_g2
    k = xk.astype(f32)
    kk = heads(k * k_k)
    kk = kk / jnp.maximum(jnp.linalg.norm(kk, axis=-1, keepdims=True), 1e-12)
    k = k * (1.0 + (a - 1.0) * k_a)
    r_h, k_h, v_h, a_h = heads(xr), heads(k), heads(v), heads(a)
    y = wkv7_scan(r_h, heads(decay), k_h, v_h, -kk, kk * a_h)
    mu = jnp.mean(y, axis=-1, keepdims=True)
    var = jnp.mean(jnp.square(y - mu), axis=-1, keepdims=True)
    y = ((y - mu) * lax.rsqrt(var + LNX_EPS)).reshape(B, T, H * N) * lnx_w + lnx_b
    bonus = jnp.sum(r_h * k_h * r_k, axis=-1, keepdims=True) * v_h
    y = (y + bonus.reshape(B, T, H * N)) * g
    return y.astype(xr.dtype)


def dilated_attention_group(q, k, v, window, dilation, slopes):
    B, T, Hg, E = q.shape
    blk = window // dilation
    span = blk * dilation
    t_pad = -(-T // span) * span
    L = t_pad // dilation
    nb = L // blk

    def to_sub(z):
        z = jnp.pad(z, ((0, 0), (0, t_pad - T), (0, 0), (0, 0)))
        z = z.reshape(B, L, dilation, Hg, E).transpose(0, 2, 3, 1, 4)
        return z.reshape(B, dilation, Hg, nb, blk, E)

    def with_prev(z):
        prev = jnp.pad(z, ((0, 0), (0, 0), (0, 0), (1, 0), (0, 0), (0, 0)))[:, :, :, :-1]
        return jnp.concatenate([prev, z], axis=4)

    qs = to_sub(q)
    kc, vc = with_prev(to_sub(k)), with_prev(to_sub(v))
    s = jnp.einsum('brhnqe,brhnke->brhnqk', qs, kc).astype(jnp.float32) * (E ** -0.5)
    i = jnp.arange(blk)[:, None]
    j = jnp.arange(2 * blk)[None, :]
    rel = blk + i - j
    blk_idx = jnp.arange(nb)[:, None, None]
    mask = (rel >= 0) & (rel <= blk) & ((blk_idx > 0) | (j >= blk))
    dist = (rel * dilation).astype(jnp.float32)
    bias = -slopes[:, None, None] * dist
    s = jnp.where(mask, s + bias[None, None, :, None], -jnp.inf)
    lse = jax.nn.logsumexp(s, axis=-1)
    p = jnp.exp(s - lse[..., None])
    o = jnp.einsum('brhnqk,brhnke->brhnqe', p.astype(v.dtype), vc)
    o = o.reshape(B, dilation, Hg, L, E).transpose(0, 3, 1, 2, 4).reshape(B, t_pad, Hg, E)[:, :T]
    lse = lse.reshape(B, dilation, Hg, L).transpose(0, 3, 1, 2).reshape(B, t_pad, Hg)[:, :T]
    return o, lse


def spatial_gating(u, v, ln_g, ln_b, w_s, b_s):
    B, T, C = v.shape
    vf = v.astype(jnp.float32)
    mu = jnp.mean(vf, axis=-1, keepdims=True)
    var = jnp.mean(jnp.square(vf - mu), axis=-1, keepdims=True)
    vn = ((vf - mu) * lax.rsqrt(var + LN_EPS) * ln_g + ln_b).astype(v.dtype)
    vn = vn.reshape(B, T // SGU_CHUNK, SGU_CHUNK, SGU_GROUPS, SGU_GROUP_DIM)
    causal = jnp.tril(jnp.ones((SGU_CHUNK, SGU_CHUNK), dtype=bool))
    w = jnp.where(causal, w_s, jnp.zeros_like(w_s))
    mixed = jnp.einsum('gts,bcsgd->bctgd', w, vn) + b_s.T[:, :, None]
    return u * mixed.reshape(B, T, C)


def setup_inputs(seed: int = 0) -> dict:
    key = jax.random.key(seed)
    ks = iter(jax.random.split(key, 48))
    L, D, F, W = DEPTH, D_MODEL, FFN_DIM, RWKV_WIDTH

    def normal(shape, scale):
        return jax.random.normal(next(ks), shape, jnp.float32) * scale

    def uniform(shape, lo, hi):
        return jax.random.uniform(next(ks), shape, jnp.float32, lo, hi)

    def gain(shape):
        return 1.0 + normal(shape, 0.02)

    return {
        "x": normal((BATCH, SEQ, D), 1.0),
        "ffn1_pre_g": gain((L, D)),
        "ffn1_w_gu": normal((L, D, 2 * F), D ** -0.5),
        "ffn1_w_down": normal((L, F, D), F ** -0.5),
        "ffn1_post_g": gain((L, D)),
        "mix_pre_g": gain((L, D)),
        "w_in": normal((L, D, N_IN), D ** -0.5),
        "shift_mu": uniform((L, N_SHIFT), 0.0, 1.0),
        "decay_w0": uniform((L, W), -6.0, 1.0),
        "decay_w2": normal((L, DECAY_LORA, W), 0.5 * DECAY_LORA ** -0.5),
        "iclr_a0": normal((L, W), 0.5),
        "iclr_a2": normal((L, ICLR_LORA, W), ICLR_LORA ** -0.5),
        "gate_g2": normal((L, GATE_LORA, W), GATE_LORA ** -0.5),
        "k_k": uniform((L, W), 0.7, 1.0),
        "k_a": uniform((L, W), 0.8, 1.2),
        "r_k": normal((L, RWKV_HEADS, RWKV_HEAD_DIM), 0.1),
        "lnx_w": gain((L, W)),
        "lnx_b": normal((L, W), 0.02),
        "vres_w1": normal((L - 1, D, VRES_LORA), D ** -0.5),
        "vres_mu": uniform((L - 1, VRES_LORA), 0.0, 1.0),
        "vres_v0": normal((L - 1, W), 0.5),
        "vres_w2": normal((L - 1, VRES_LORA, W), VRES_LORA ** -0.5),
        "sgu_ln_g": gain((L, SGU_WIDTH)),
        "sgu_ln_b": normal((L, SGU_WIDTH), 0.02),
        "sgu_w_s": normal((L, SGU_GROUPS, SGU_CHUNK, SGU_CHUNK), 0.5 * SGU_CHUNK ** -0.5),
        "sgu_b": 1.0 + normal((L, SGU_GROUPS, SGU_CHUNK), 0.1),
        "w_b_rwkv": normal((L, W, D), W ** -0.5),
        "w_b_attn": normal((L, ATTN_OUT_WIDTH, D), ATTN_OUT_WIDTH ** -0.5),
        "w_b_sgu": normal((L, SGU_WIDTH, D), SGU_WIDTH ** -0.5),
        "w_out": normal((L, D, D), D ** -0.5),
        "mix_post_g": gain((L, D)),
        "ffn2_pre_g": gain((L, D)),
        "ffn2_w_gu": normal((L, D, 2 * F), D ** -0.5),
        "ffn2_w_down": normal((L, F, D), F ** -0.5),
        "ffn2_post_g": gain((L, D)),
    }


def reference(x, ffn1_pre_g, ffn1_w_gu, ffn1_w_down, ffn1_post_g, mix_pre_g, w_in, shift_mu,
              decay_w0, decay_w2, iclr_a0, iclr_a2, gate_g2, k_k, k_a, r_k, lnx_w, lnx_b,
              vres_w1, vres_mu, vres_v0, vres_w2, sgu_ln_g, sgu_ln_b, sgu_w_s, sgu_b,
              w_b_rwkv, w_b_attn, w_b_sgu, w_out, mix_post_g,
              ffn2_pre_g, ffn2_w_gu, ffn2_w_down, ffn2_post_g):
    B, T, D = x.shape
    W = RWKV_WIDTH
    slopes = jnp.asarray(alibi_slopes(ATTN_HEADS))
    shift_split = [W, 2 * W, 3 * W, 3 * W + DECAY_LORA, 3 * W + DECAY_LORA + ICLR_LORA]
    in_split = [N_SHIFT, N_SHIFT + N_QKV, N_SHIFT + N_QKV + N_SGU]
    h = x
    v_first = None
    for l in range(DEPTH):
        h = h + 0.5 * rms_norm(swiglu_ffn(rms_norm(h, ffn1_pre_g[l]), ffn1_w_gu[l], ffn1_w_down[l]),
                               ffn1_post_g[l])

        xn = rms_norm(h, mix_pre_g[l])
        proj = xn @ w_in[l]
        p_shift, p_qkv, p_sgu, p_gate = jnp.split(proj, in_split, axis=-1)

        p_shift = p_shift + (token_shift(p_shift) - p_shift) * shift_mu[l]
        xr, xk, xv, xw, xa, xg = jnp.split(p_shift, shift_split, axis=-1)
        if l == 0:
            v = xv
            v_first = xv
        else:
            pv = xn @ vres_w1[l - 1]
            pv = pv + (token_shift(pv) - pv) * vres_mu[l - 1]
            v = xv + (v_first - xv) * jax.nn.sigmoid(vres_v0[l - 1] + pv @ vres_w2[l - 1])
        y_rwkv = rwkv7_time_mix(xr, xk, v, xw, xa, xg, decay_w0[l], decay_w2[l], iclr_a0[l],
                                iclr_a2[l], gate_g2[l], k_k[l], k_a[l], r_k[l], lnx_w[l], lnx_b[l])

        q, k, va = (z.reshape(B, T, ATTN_HEADS, ATTN_HEAD_DIM) for z in jnp.split(p_qkv, 3, axis=-1))
        outs, lses = [], []
        for gi, (window, dilation) in enumerate(ATTN_GROUPS):
            hs = slice(gi * ATTN_HEADS_PER_GROUP, (gi + 1) * ATTN_HEADS_PER_GROUP)
            o_g, lse_g = dilated_attention_group(q[:, :, hs], k[:, :, hs], va[:, :, hs],
                                                 window, dilation, slopes[hs])
            outs.append(o_g)
            lses.append(lse_g)
        wts = jax.nn.softmax(jnp.stack(lses, axis=0), axis=0)
        y_attn = jnp.sum(wts[..., None] * jnp.stack(outs, axis=0).astype(jnp.float32), axis=0)
        y_attn = y_attn.reshape(B, T, ATTN_OUT_WIDTH).astype(x.dtype)

        u_s, v_s = jnp.split(jax.nn.gelu(p_sgu), 2, axis=-1)
        y_sgu = spatial_gating(u_s, v_s, sgu_ln_g[l], sgu_ln_b[l], sgu_w_s[l], sgu_b[l])

        gates = jax.nn.sigmoid(p_gate).reshape(B, T, N_BRANCHES, D)
        merged = (gates[:, :, 0] * (y_rwkv @ w_b_rwkv[l])
                  + gates[:, :, 1] * (y_attn @ w_b_attn[l])
                  + gates[:, :, 2] * (y_sgu @ w_b_sgu[l]))
        h = h + rms_norm(merged @ w_out[l], mix_post_g[l])

        h = h + 0.5 * rms_norm(swiglu_ffn(rms_norm(h, ffn2_pre_g[l]), ffn2_w_gu[l], ffn2_w_down[l]),
                               ffn2_post_g[l])
    return h
```

```python
import contextlib
import numpy as np
import concourse.bass as bass
import concourse.mybir as mybir

F32 = mybir.dt.float32
BF16 = mybir.dt.bfloat16
ALU = mybir.AluOpType
AF = mybir.ActivationFunctionType
AX = mybir.AxisListType


class Res:
    __slots__ = ("name", "lw", "rd", "ov")

    def __init__(self, name=""):
        self.name = name
        self.lw = {}
        self.rd = {}
        self.ov = []


class Region:
    def __init__(self, t, nbytes, name):
        self.t = t
        self.nbytes = nbytes
        self.name = name
        self.views = []
        self.dead = []

    def kill_from(self, offset):
        keep = []
        for v in self.views:
            if v.lo >= offset:
                self.dead.append(v)
            else:
                keep.append(v)
        self.views = keep

    def view(self, off, shape, dtype, name=None, nbytes=None):
        esz = {F32: 4, BF16: 2, mybir.dt.int32: 4, mybir.dt.uint8: 1}[dtype]
        n = 1
        for d in shape[1:]:
            n *= d
        nb = n * esz
        assert off % 4 == 0 and off + nb <= self.nbytes, (off, nb, self.nbytes)
        ap = self.t[0:shape[0], off:off + nb].bitcast(dtype)
        if len(shape) == 3:
            ap = ap.rearrange("p (a b) -> p a b", a=shape[1])
        elif len(shape) == 4:
            ap = ap.rearrange("p (a b c) -> p a b c", a=shape[1], b=shape[2])
        v = Tile(ap, name or f"{self.name}@{off}")
        v.lo, v.hi = off, off + nb
        for o in self.dead:
            if o.lo < v.hi and v.lo < o.hi:
                for q in o.all_res():
                    for tok in list(q.lw.values()) + list(q.rd.values()):
                        k = id(tok[0])
                        if k not in v.r.rd or v.r.rd[k][1] < tok[1]:
                            v.r.rd[k] = tok
        for o in self.views:
            if o.lo < v.hi and v.lo < o.hi:
                for q in o.all_res():
                    q.ov.append(v.r)
                    v.r.ov.append(q)
                o.partners.append(v)
                v.partners.append(o)
        self.views.append(v)
        return v


class SubRegion:
    def __init__(self, parent, base, nbytes, name):
        self.parent, self.base, self.nbytes, self.name = parent, base, nbytes, name

    def view(self, off, shape, dtype, name=None):
        return self.parent.view(self.base + off, shape, dtype, name or f"{self.name}@{off}")


class Tile:
    def __init__(self, t, name):
        self.t = t
        self.name = name
        self.r = Res(name)
        self.sub = {}
        self.partners = []

    def __getitem__(self, idx):
        return self.t[idx]

    def all_res(self):
        return [self.r] + list(self.sub.values())

    def res(self, i):
        if i not in self.sub:
            r = Res(f"{self.name}.{i}")
            r.ov.append(self.r)
            self.r.ov.append(r)
            for p in self.partners:
                for q in p.all_res():
                    q.ov.append(r)
                    r.ov.append(q)
            self.sub[i] = r
        return self.sub[i]


class _Eng:
    def __init__(self, name):
        self.name = name
        self.q = []
        self.sem = None
        self.count = 0
        self.seen = {}
        self.pend_r = []
        self.pend_w = []
        self.nsem = 0


class Sched:
    EPOCH = 30000

    def __init__(self, nc, same_engine_sync=False):
        self.nc = nc
        self.stack = contextlib.ExitStack()
        self.E = {n: _Eng(n) for n in ("pe", "dve", "act", "pool", "sp")}
        self.same_engine_sync = same_engine_sync
        self.nsem = 0
        self.dma_sems = {}
        self.ntile = 0
        self.arena = None
        self.bump = 0
        self.peak = 0
        self.nops = 0
        self.PB = None

    def new_sem(self, name):
        self.nsem += 1
        return self.stack.enter_context(self.nc.semaphore(f"{name}_{self.nsem}"))

    ARENA_BYTES = 212800

    def sbuf(self, shape, dtype, name=None):
        self.ntile += 1
        name = name or f"t{self.ntile}"
        if self.arena is None:
            self.arena = self.region_raw(self.ARENA_BYTES, "arena")
            self.bump = 0
        esz = {F32: 4, BF16: 2, mybir.dt.int32: 4, mybir.dt.uint8: 1}[dtype]
        n = 1
        for d in shape[1:]:
            n *= d
        nb = (n * esz + 31) // 32 * 32
        off = self.bump
        self.bump += nb
        assert self.bump <= self.ARENA_BYTES, f"arena overflow allocating {name}: {self.bump}"
        self.peak = max(self.peak, self.bump)
        return self.arena.view(off, list(shape), dtype, name, nbytes=n * esz)

    def phase_mark(self):
        return self.bump

    def phase_release(self, mark):
        self.bump = mark
        self.arena.kill_from(mark)

    def region(self, nbytes, name):
        off = self.bump if self.arena is not None else 0
        if self.arena is None:
            self.arena = self.region_raw(self.ARENA_BYTES, "arena")
            self.bump = 0
            off = 0
        self.bump += (nbytes + 31) // 32 * 32
        assert self.bump <= self.ARENA_BYTES, f"arena overflow allocating region {name}: {self.bump}"
        self.peak = max(self.peak, self.bump)
        return SubRegion(self.arena, off, nbytes, name)

    def region_raw(self, nbytes, name):
        self.ntile += 1
        t = self.stack.enter_context(self.nc.sbuf_tensor(f"{name}_{self.ntile}", [128, nbytes], mybir.dt.uint8))
        return Region(t, nbytes, name)

    def psum(self, shape, dtype, name=None):
        self.ntile += 1
        name = name or f"p{self.ntile}"
        t = self.stack.enter_context(self.nc.psum_tensor(f"{name}_{self.ntile}", list(shape), dtype))
        return Tile(t, name)

    @staticmethod
    def _resl(x):
        out = []
        for r in x:
            if isinstance(r, Tile):
                out.append(r.r)
            elif r is not None:
                out.append(r)
        return out

    @staticmethod
    def _put(d, tok):
        k = id(tok[0])
        if k not in d or d[k][1] < tok[1]:
            d[k] = tok

    def _collect(self, E, reads, writes, is_dma, wadd=()):
        waits = {}

        def need(tok):
            sem, val, src, tdma = tok
            if (not tdma) and src == E.name and not self.same_engine_sync:
                return
            k = id(sem)
            if E.seen.get(k, 0) >= val:
                return
            if k not in waits or waits[k][1] < val:
                waits[k] = (sem, val)

        for r in reads:
            for tok in r.lw.values():
                need(tok)
            for o in r.ov:
                for tok in o.lw.values():
                    need(tok)
        for r in writes:
            for tok in r.lw.values():
                need(tok)
            for tok in r.rd.values():
                need(tok)
            for o in r.ov:
                for tok in o.lw.values():
                    need(tok)
                for tok in o.rd.values():
                    need(tok)
        for r in wadd:
            for tok in r.rd.values():
                need(tok)
            for o in r.ov:
                for tok in o.lw.values():
                    need(tok)
                for tok in o.rd.values():
                    need(tok)
        for k, (sem, val) in waits.items():
            E.seen[k] = val
        return list(waits.values())

    def op(self, eng, fn, reads=(), writes=(), inc=True):
        E = self.E[eng]
        self.nops += 1
        reads = self._resl(reads)
        writes = self._resl(writes)
        waits = self._collect(E, reads, writes, False)
        if not inc:
            E.q.append((waits, fn, None))
            E.pend_r += reads
            E.pend_w += writes
            return
        if E.sem is None or E.count >= self.EPOCH:
            E.sem = self.new_sem(f"s_{eng}")
            E.count = 0
        E.count += 1
        tok = (E.sem, E.count, E.name, False)
        E.q.append((waits, fn, (E.sem, 1)))
        for r in reads + E.pend_r:
            self._put(r.rd, tok)
        for r in writes + E.pend_w:
            r.lw = {id(tok[0]): tok}
            r.rd = {}
        E.pend_r = []
        E.pend_w = []

    def o(self, eng, method, *args, reads=(), writes=(), inc=True, **kwargs):
        return self.op(eng, (lambda e: getattr(e, method)(*args, **kwargs)), reads=reads, writes=writes, inc=inc)

    DMA_POOL = 24

    def dma(self, eng, out, in_, reads=(), writes=(), key=None, wadd=(), **kw):
        E = self.E[eng]
        reads = self._resl(reads)
        writes = self._resl(writes)
        wadd = self._resl(wadd)
        self.nops += 1
        waits = self._collect(E, reads, writes, True, wadd)
        pool = self.dma_sems.setdefault(eng, {"sems": [], "pos": 0})
        if len(pool["sems"]) < self.DMA_POOL:
            pool["sems"].append([self.new_sem(f"s_dma_{eng}"), 0])
        ds = pool["sems"][pool["pos"] % self.DMA_POOL] if len(pool["sems"]) == self.DMA_POOL and pool["pos"] >= self.DMA_POOL else pool["sems"][-1]
        pool["pos"] += 1
        if ds[1] > 0:
            k = id(ds[0])
            if E.seen.get(k, 0) < ds[1]:
                waits.append((ds[0], ds[1]))
                E.seen[k] = ds[1]
        if ds[1] + 16 > 16 * 2000:
            ds[0] = self.new_sem(f"s_dma_{eng}")
            ds[1] = 0
        ds[1] += 16
        tok = (ds[0], ds[1], "dma", True)
        E.q.append((waits, (lambda e: e.dma_start(out=out, in_=in_, **kw)), (ds[0], 16)))
        for r in reads:
            self._put(r.rd, tok)
        for r in writes:
            r.lw = {id(tok[0]): tok}
            r.rd = {}
        for r in wadd:
            self._put(r.lw, tok)
        return tok

    def collective(self, kind, alu, groups, in_ap, out_ap, reads=(), writes=(), wadd=()):
        E = self.E["pool"]
        reads = self._resl(reads)
        writes = self._resl(writes)
        wadd = self._resl(wadd)
        waits = self._collect(E, reads, writes, True, wadd)
        sem = self.new_sem("s_cc")
        tok = (sem, 1, "dma", True)
        E.q.append((waits, (lambda e: e.collective_compute(kind, alu, replica_groups=groups, ins=[in_ap], outs=[out_ap])), (sem, 1)))
        for r in reads:
            self._put(r.rd, tok)
        for r in writes:
            r.lw = {id(tok[0]): tok}
            r.rd = {}
        for r in wadd:
            self._put(r.lw, tok)
        return tok

    def wait_all(self, eng, toks):
        E = self.E[eng]
        waits = [(t[0], t[1]) for t in toks]
        E.q.append((waits, None, None))

    def emit(self):
        nc = self.nc
        with nc.Block() as block:
            def run(E):
                def body(e):
                    for waits, fn, inc in E.q:
                        for sem, val in waits:
                            e.wait_ge(sem, val)
                        if fn is None:
                            continue
                        ins = fn(e)
                        if inc is not None:
                            ins.then_inc(inc[0], inc[1])
                return body
            if self.E["sp"].q:
                block.sync(run(self.E["sp"]))
            if self.E["pe"].q:
                block.tensor(run(self.E["pe"]))
            if self.E["dve"].q:
                block.vector(run(self.E["dve"]))
            if self.E["act"].q:
                block.scalar(run(self.E["act"]))
            if self.E["pool"].q:
                block.gpsimd(run(self.E["pool"]))

    def close(self):
        self.stack.close()


D = 2048
F = 5504
DC = 16
FC = 43
RMS_EPS = 1e-6


def alloc_common(S, NT):
    C = {}
    C["NT"] = NT
    C["hid"] = S.region(FC * NT * 2, "hid")
    C["A"] = S.region(DC * NT * 4, "A")
    C["ring"] = S.region(32 * 1024, "ring")
    C["PB"] = S.PB
    C["ones"] = S.sbuf([128, 128], F32, "ones")
    S.o("pool", "memset", C["ones"][:], 1.0, writes=[C["ones"]])
    C["eps"] = S.sbuf([128, 1], F32, "eps")
    S.o("pool", "memset", C["eps"][:], RMS_EPS, writes=[C["eps"]])
    EPS_T[0] = C["eps"]
    C["sil"] = [S.sbuf([128, NT], F32, f"sil{i}") for i in range(2)]
    C["rstd"] = S.sbuf([128, NT], F32, "rstd")
    C["hld"] = [S.sbuf([128, NT], F32, f"hld{i}") for i in range(2)]
    C["xo"] = [C["ring"].view(i * NT * 2, [128, NT], BF16, f"xo{i}") for i in range(2)]
    return C


def rstd_from_ss(S, out_t, out_ap, ss_tiles, ss_aps, n, eps_t=None):
    for i, (st, sa) in enumerate(zip(ss_tiles, ss_aps)):
        w = sa.shape[-1]
        o = out_ap[:, i * 512:i * 512 + w]
        S.o("act", "activation", out=o, in_=sa, func=AF.Sqrt, bias=EPS_T[0][:, 0:1], scale=1.0 / D,
             reads=[st, EPS_T[0]], writes=[out_t])
    S.o("dve", "reciprocal", out=out_ap, in_=out_ap, reads=[out_t], writes=[out_t])


EPS_T = [None]


def prenorm_from_hbm(S, C, hT, gcol_ap, gcol_t, xn, hres=None):
    NT = C["NT"]
    TP = 256
    hv = hT.rearrange("(c p) t -> p c t", p=128)
    hs = [C["hid"].view(i * DC * TP * 4, [128, DC, TP], F32, f"hs{i}") for i in range(2)]
    sq = C["hid"].view(2 * DC * TP * 4, [128, DC, TP], F32, "sq")
    ones = C["ones"]
    for pi in range(NT // TP):
        h = hs[pi % 2]
        ps = C["PB"][pi % 2]
        tsl = slice(pi * TP, (pi + 1) * TP)
        S.dma("sp", h[:], hv[:, :, tsl], reads=list(hres) if hres is not None else [], writes=[h])
        S.o("act", "activation", out=sq[:], in_=h[:], func=AF.Square, reads=[h], writes=[sq])
        for c in range(DC):
            S.o("pe", "matmul", ps[:, 0:TP], lhsT=ones[:], rhs=sq[:, c, :], start=(c == 0), stop=(c == DC - 1),
                 reads=[ones, sq], writes=[ps], inc=(c == DC - 1))
        rs = C["rstd"]
        rstd_from_ss(S, rs, rs[:, 0:TP], [ps], [ps[:, 0:TP]], TP)
        S.o("dve", "tensor_tensor", out=sq[:], in0=h[:], in1=rs[:, 0:TP].unsqueeze(1).to_broadcast([128, DC, TP]), op=ALU.mult,
             reads=[h, rs], writes=[sq])
        S.o("pool", "tensor_tensor", out=xn[:, :, tsl], in0=sq[:], in1=gcol_ap.unsqueeze(2).to_broadcast([128, DC, TP]), op=ALU.mult,
             reads=[sq, gcol_t], writes=[xn])


def ffn_core(S, C, xn, w_gu, w_down, yT):
    NT = C["NT"]
    NH = NT // 512
    PB = C["PB"]
    hid = C["hid"].view(0, [128, FC, NT], BF16, "hidden")
    ring = C["ring"]
    wgv = w_gu.rearrange("(c p) m -> p c m", p=128)
    GW = 2
    wg = [ring.view(i * 16384, [128, DC, GW * 128], BF16, f"wg{i}") for i in range(2)]
    wu = [ring.view(8192 + i * 16384, [128, DC, GW * 128], BF16, f"wu{i}") for i in range(2)]
    ngrp = (FC + GW - 1) // GW

    def load_gu(gi):
        nch = min(GW, FC - gi * GW)
        c0 = gi * GW * 128
        S.dma("pool", wg[gi % 2][:, :, 0:nch * 128], wgv[:, :, c0:c0 + nch * 128], writes=[wg[gi % 2]])
        S.dma("pool", wu[gi % 2][:, :, 0:nch * 128], wgv[:, :, F + c0:F + c0 + nch * 128], writes=[wu[gi % 2]])

    load_gu(0)
    for gi in range(ngrp):
        if gi + 1 < ngrp:
            load_gu(gi + 1)
        nch = min(GW, FC - gi * GW)
        for jc in range(nch):
            j = gi * GW + jc
            s = j % 2
            G = PB[4 * s:4 * s + NH]
            U = PB[4 * s + 2:4 * s + 2 + NH]
            for c in range(DC):
                for (wt, PS) in ((wg[gi % 2], G), (wu[gi % 2], U)):
                    for hf in range(NH):
                        last = (c == DC - 1)
                        S.o("pe", "matmul",
                            PS[hf][:], lhsT=wt[:, c, jc * 128:(jc + 1) * 128], rhs=xn[:, c, hf * 512:(hf + 1) * 512],
                            start=(c == 0), stop=(c == DC - 1),
                            reads=[wt, xn], writes=[PS[hf]], inc=last)
            sil = C["sil"][s]
            for hf in range(NH):
                S.o("act", "activation", out=sil[:, hf * 512:(hf + 1) * 512], in_=G[hf][:], func=AF.Silu,
                     reads=[G[hf]], writes=[sil])
            for hf in range(NH):
                S.o("dve", "tensor_tensor", out=hid[:, j, hf * 512:(hf + 1) * 512], in0=sil[:, hf * 512:(hf + 1) * 512],
                                                                                 in1=U[hf][:], op=ALU.mult,
                     reads=[sil, U[hf]], writes=[hid])

    wdv = w_down.rearrange("(j p) d -> p j d", p=128)
    wd = [ring.view(i * 11264, [128, FC, 128], BF16, f"wd{i}") for i in range(2)]
    ones = C["ones"]
    SS = PB[4:4 + NH]

    def load_d(dc):
        S.dma("pool", wd[dc % 2][:], wdv[:, :, dc * 128:(dc + 1) * 128], writes=[wd[dc % 2]])

    load_d(0)
    for dc in range(DC):
        if dc + 1 < DC:
            load_d(dc + 1)
        Y = PB[2 * (dc % 2):2 * (dc % 2) + NH]
        w = wd[dc % 2]
        for j in range(FC):
            for hf in range(NH):
                S.o("pe", "matmul", Y[hf][:], lhsT=w[:, j, :], rhs=hid[:, j, hf * 512:(hf + 1) * 512],
                                                                    start=(j == 0), stop=(j == FC - 1),
                     reads=[w, hid], writes=[Y[hf]], inc=(j == FC - 1))
        evac_y_tile(S, C, Y, dc, yT, SS, NH)
    return SS


def evac_y_tile(S, C, Y, dc, yT, SS, NH):
    ones = C["ones"]
    sq2 = C["sil"][dc % 2]
    for hf in range(NH):
        S.o("act", "activation", out=yT[:, dc, hf * 512:(hf + 1) * 512], in_=Y[hf][:], func=AF.Copy, reads=[Y[hf]], writes=[yT.res(dc)])
        S.o("dve", "tensor_tensor", out=sq2[:, hf * 512:(hf + 1) * 512], in0=yT[:, dc, hf * 512:(hf + 1) * 512],
            in1=yT[:, dc, hf * 512:(hf + 1) * 512], op=ALU.mult, reads=[yT.res(dc)], writes=[sq2])
    for hf in range(NH):
        S.o("pe", "matmul", SS[hf][:], lhsT=ones[:], rhs=sq2[:, hf * 512:(hf + 1) * 512], start=(dc == 0), stop=(dc == DC - 1),
            reads=[ones, sq2], writes=[SS[hf]])


def residual_epilogue(S, C, yT, SS, hT_in, hres, post_g_scaled, hT_out, hres_out, next_g=None, xnT_out=None, xres_out=None):
    NT = C["NT"]
    NH = NT // 512
    PB = C["PB"]
    rs = C["rstd"]
    rstd_from_ss(S, rs, rs[:], SS, [t[:] for t in SS], NT)
    hv_in = hT_in.rearrange("(c p) t -> p c t", p=128)
    hv_out = hT_out.rearrange("(c p) t -> p c t", p=128)
    ones = C["ones"]
    SS3 = PB[6:6 + NH]
    out_toks = []
    for dc in range(DC):
        hl = C["hld"][dc % 2]
        S.dma("sp", hl[:], hv_in[:, dc, :], reads=[hres[dc]] if hres is not None else [], writes=[hl])
        S.o("dve", "scalar_tensor_tensor", out=yT[:, dc, :], in0=yT[:, dc, :], scalar=post_g_scaled[:, dc:dc + 1], in1=rs[:],
            op0=ALU.mult, op1=ALU.mult, reads=[yT.res(dc), post_g_scaled, rs], writes=[yT.res(dc)])
        S.o("pool", "tensor_tensor", out=yT[:, dc, :], in0=yT[:, dc, :], in1=hl[:], op=ALU.add, reads=[yT.res(dc), hl], writes=[yT.res(dc)])
        out_toks.append(S.dma("act", hv_out[:, dc, :], yT[:, dc, :], reads=[yT.res(dc)], writes=[hres_out[dc]] if hres_out is not None else []))
        if next_g is not None:
            sq3 = C["sil"][dc % 2]
            S.o("act", "activation", out=sq3[:], in_=yT[:, dc, :], func=AF.Square, reads=[yT.res(dc)], writes=[sq3])
            for hf in range(NH):
                S.o("pe", "matmul", SS3[hf][:], lhsT=ones[:], rhs=sq3[:, hf * 512:(hf + 1) * 512], start=(dc == 0), stop=(dc == DC - 1),
                    reads=[ones, sq3], writes=[SS3[hf]])
    if next_g is not None:
        rstd_from_ss(S, rs, rs[:], SS3, [t[:] for t in SS3], NT)
        for dc in range(DC):
            xo = C["xo"][dc % 2]
            S.o("dve", "scalar_tensor_tensor", out=xo[:], in0=yT[:, dc, :], scalar=next_g[:, dc:dc + 1], in1=rs[:],
                op0=ALU.mult, op1=ALU.mult, reads=[yT.res(dc), next_g, rs], writes=[xo])
            out_toks.append(S.dma("sp", xnT_out[dc // 4][(dc % 4) * 128:(dc % 4 + 1) * 128, :], xo[:], reads=[xo], wadd=[xres_out] if xres_out is not None else []))
    return out_toks


def ffn_stage(S, C, hT_in, w_gu, w_down, pre_g, post_g_half, hT_out, next_g=None, xnT_out=None, hres=None, hres_out=None, xres_out=None):
    NT = C["NT"]
    xn = C["A"].view(0, [128, DC, NT], BF16, "xn")
    prenorm_from_hbm(S, C, hT_in, pre_g[:], pre_g, xn, hres)
    yT = C["A"].view(0, [128, DC, NT], F32, "yT")
    SS = ffn_core(S, C, xn, w_gu, w_down, yT)
    return residual_epilogue(S, C, yT, SS, hT_in, hres, post_g_half, hT_out, hres_out, next_g, xnT_out, xres_out)

import numpy as np

HN = 64
LNX_EPS = 64e-5


def rwkv_consts_host():
    n8 = 32
    sel = np.zeros((2 * n8, n8, 128), np.float32)
    for h in range(2):
        for c in range(n8):
            sel[h * n8 + c, c, h * 64:(h + 1) * 64] = 1.0
    ident2 = np.zeros((128, 64), np.float32)
    for h in range(2):
        ident2[h * 64 + np.arange(64), np.arange(64)] = 1.0
    bo = np.zeros((128, 128), np.float32)
    bo[:64, :64] = 1.0
    bo[64:, 64:] = 1.0
    hmask = np.zeros((128, 2), np.float32)
    hmask[:64, 0] = 1.0
    hmask[64:, 1] = 1.0
    return {"c_sel": sel, "c_ident2": ident2, "c_bo": bo, "c_hmask": hmask}


def np_rwkv_ref(p, prm, v_first=None):
    f8 = np.float64
    T = p["r"].shape[0]

    def shift(z, mu):
        zp = np.concatenate([np.zeros_like(z[:1]), z[:-1]], 0)
        return z + (zp - z) * mu

    sig = lambda z: 1 / (1 + np.exp(-z))
    xr, xk, xv = shift(p["r"], prm["mu_r"]), shift(p["k"], prm["mu_k"]), shift(p["v"], prm["mu_v"])
    xw, xa, xg = shift(p["xw"], prm["mu_w"]), shift(p["xa"], prm["mu_a"]), shift(p["xg"], prm["mu_g"])
    H = xr.shape[1] // 64
    z = prm["w0"] + np.tanh(xw) @ prm["w2"]
    w_log = -np.log1p(np.exp(-z)) - 0.5
    decay = np.exp(-np.exp(w_log))
    a = sig(prm["a0"] + xa @ prm["a2"])
    g = sig(xg) @ prm["g2"]
    if v_first is None:
        v = xv
    else:
        pv = shift(p["pv"], prm["mu_pv"])
        v = xv + (v_first - xv) * sig(prm["v0"] + pv @ prm["vw2"])
    kk = (xk * prm["k_k"]).reshape(T, H, 64)
    kk = kk / np.maximum(np.linalg.norm(kk, axis=-1, keepdims=True), 1e-12)
    k = xk * (1 + (a - 1) * prm["k_a"])
    hd = lambda z: z.reshape(T, H, 64)
    r_h, k_h, v_h, a_h, w_h = hd(xr), hd(k), hd(v), hd(a), hd(decay)
    aa, bb = -kk, kk * a_h
    Sst = np.zeros((H, 64, 64), f8)
    ys = np.zeros((T, H, 64), f8)
    for t in range(T):
        sa = np.einsum('hij,hj->hi', Sst, aa[t])
        Sst = Sst * w_h[t][:, None, :] + sa[:, :, None] * bb[t][:, None, :] + v_h[t][:, :, None] * k_h[t][:, None, :]
        ys[t] = np.einsum('hij,hj->hi', Sst, r_h[t])
    mu = ys.mean(-1, keepdims=True)
    var = ((ys - mu) ** 2).mean(-1, keepdims=True)
    y = ((ys - mu) / np.sqrt(var + LNX_EPS)).reshape(T, H * 64) * prm["lnx_w"] + prm["lnx_b"]
    bonus = (r_h * k_h * prm["r_k"].reshape(H, 64)).sum(-1, keepdims=True) * v_h
    y = (y + bonus.reshape(T, H * 64)) * g
    return y, xv


TS = 256
NC8 = TS // 8
MU_R, MU_K, MU_V, W0, A0, K_K, K_A, R_K, LNX_W, LNX_B, V0 = range(11)


def rwkv_program(S, nc, ntile, T, layer1, dr):
    f = lambda shape, name, dt=F32: S.sbuf(shape, dt, name)
    nseg = T // TS
    RB, KB_, VB = 0, ntile * 128, 2 * ntile * 128
    XWB = 3 * ntile * 128
    XAB, XGB, PVB = XWB + 64, XWB + 128, XWB + 288
    sel = f([64, NC8, 128], "sel")
    ident2 = f([128, 64], "ident2")
    bo = f([128, 128], "bo")
    hmask = f([128, 2], "hmask")
    pc = f([128, ntile, 11], "pc")
    pd = f([128, ntile, 4], "pd")
    lmu = f([128, 5], "lmu"); lmu1 = f([128, 5], "lmu1")
    w2 = f([64, ntile * 128], "w2")
    a2 = f([64, ntile * 128], "a2")
    g2a = f([128, ntile * 128], "g2a")
    g2b = f([32, ntile * 128], "g2b")
    vw2 = f([32, ntile * 128], "vw2")
    epsl = f([128, 1], "epsl")
    S.dma("sp", sel[:], dr["c_sel"], writes=[sel])
    S.dma("sp", ident2[:], dr["c_ident2"], writes=[ident2])
    S.dma("sp", bo[:], dr["c_bo"], writes=[bo])
    S.dma("sp", hmask[:], dr["c_hmask"], writes=[hmask])
    S.dma("act", pc[:], dr["pc"], writes=[pc])
    S.dma("act", lmu[:], dr["lmu"], writes=[lmu])
    S.dma("act", w2[:], dr["w2"], writes=[w2])
    S.dma("act", a2[:], dr["a2"], writes=[a2])
    S.dma("act", g2a[:], dr["g2"][0:128, :], writes=[g2a])
    S.dma("act", g2b[:], dr["g2"][128:160, :], writes=[g2b])
    if layer1:
        S.dma("act", vw2[:], dr["vw2"], writes=[vw2])
    S.o("pool", "memset", epsl[:], LNX_EPS, writes=[epsl])
    for i, src in enumerate((MU_R, MU_K, MU_V, K_A)):
        S.o("pool", "tensor_scalar", out=pd[:, :, i], in0=pc[:, :, src], scalar1=-1.0, scalar2=1.0, op0=ALU.mult, op1=ALU.add,
             reads=[pc], writes=[pd])
    S.o("pool", "tensor_scalar", out=lmu1[:], in0=lmu[:], scalar1=-1.0, scalar2=1.0, op0=ALU.mult, op1=ALU.add,
         reads=[lmu], writes=[lmu1])
    state = [f([128, 64], f"state{t}") for t in range(ntile)]
    sa = [f([128, 1], f"sa{t}") for t in range(ntile)]
    junk = [f([128, 64], f"junk{t}") for t in range(ntile)]
    for t in range(ntile):
        S.o("dve", "memset", state[t][:], 0.0, writes=[state[t]])
    PB = S.PB
    NRING = 4
    ring = [f([128, 5, 512], f"ring{i}") for i in range(NRING)]
    ringpos = [0]
    W1 = TS + 1
    raw_main = [[f([128, W1], f"raw{t}_{i}") for i in range(3)] for t in range(1)]
    raw_l = {"xw": f([64, W1], "raw_xw"), "xa": f([64, W1], "raw_xa"), "xg0": f([128, W1], "raw_xg0"),
             "xg1": f([32, W1], "raw_xg1"), "pv": f([32, W1], "raw_pv")}
    sh_l = {"xw": f([64, TS], "sh_xw"), "xa": f([64, TS], "sh_xa"), "xg0": f([128, TS], "sh_xg0"),
            "xg1": f([32, TS], "sh_xg1"), "pv": f([32, TS], "sh_pv")}
    xr = f([128, TS], "xr"); xk = f([128, TS], "xk")
    dw = f([128, TS], "dw"); aic = f([128, TS], "aic"); kk = f([128, TS], "kk"); a_s = f([128, TS], "a_s")
    b_s = f([128, TS], "b_s"); kmod = f([128, TS], "kmod"); tmp1 = f([128, TS], "tmp1"); tmp2 = f([128, TS], "tmp2")
    vf = f([128, TS], "vf")
    xm = [f([128, 2, TS], f"xm{i}") for i in range(2)]
    xmpos = [0]
    xtok = [[[f([64, 512], f"xtok{b}_{t}_{x}") for x in range(5)] for t in range(ntile)] for b in range(2)]
    vbuf = [[f([128, TS], f"v{b}_{t}") for t in range(ntile)] for b in range(2)]
    gbuf = [[f([128, TS], f"g{b}_{t}") for t in range(ntile)] for b in range(2)]
    bonus = [[f([128, TS], f"bonus{b}_{t}") for t in range(ntile)] for b in range(2)]
    ybuf = [[f([128, TS], f"y{b}_{t}") for t in range(ntile)] for b in range(2)]
    yo = [f([128, TS], f"yo{i}", BF16) for i in range(2)]
    pt1 = f([128, TS], "pt1"); pt2 = f([128, TS], "pt2")
    proj = dr["proj"]
    out_toks = []
    dq = ["sp", "act"]
    dqi = [0]

    def q():
        dqi[0] += 1
        return dq[dqi[0] % 2]

    def load_raw(dst, rows, r0, s):
        t0 = s * TS
        pres = dr["res"]["proj"]
        if s == 0:
            S.o("pool", "memset", dst[0:rows, 0:1], 0.0, writes=[dst])
            S.dma(q(), dst[0:rows, 1:W1], proj[r0:r0 + rows, 0:TS], reads=[pres[0]], writes=[dst])
        else:
            S.dma(q(), dst[0:rows, :], proj[r0:r0 + rows, t0 - 1:t0 + TS], reads=list({id(x): x for x in (pres[(t0 - 1) // 512], pres[t0 // 512])}.values()), writes=[dst])

    def shift_lerp(dst, dst_ap, raw, rows, mu_ap, omu_ap, mu_t):
        S.o("act", "activation", out=dst_ap, in_=raw[0:rows, 1:W1], func=AF.Copy, scale=omu_ap, reads=[raw, mu_t], writes=[dst])
        S.o("pool", "tensor_scalar_mul", out=pt1[0:rows, :], in0=raw[0:rows, 0:TS], scalar1=mu_ap, reads=[raw, mu_t], writes=[pt1])
        S.o("pool", "tensor_add", out=dst_ap, in0=dst_ap, in1=pt1[0:rows, :], reads=[dst, pt1], writes=[dst])

    def head_sum(dst, dst_ap, src, src_ap, scale, ps):
        S.o("pe", "matmul", ps[:, 0:TS], lhsT=bo[:], rhs=src_ap, start=True, stop=True, reads=[bo, src], writes=[ps])
        S.o("act", "activation", out=dst_ap, in_=ps[:, 0:TS], func=AF.Copy, scale=scale, reads=[ps], writes=[dst])

    def prep(s):
        b = s % 2
        lrows = {"xw": (64, XWB, 0), "xa": (64, XAB, 1), "xg0": (128, XGB, 2), "xg1": (32, XGB + 128, 3), "pv": (32, PVB, 4)}
        for nm, (rows, r0, li) in lrows.items():
            if nm == "pv" and not layer1:
                continue
            load_raw(raw_l[nm], rows, r0, s)
            shift_lerp(sh_l[nm], sh_l[nm][:], raw_l[nm], rows, lmu[0:rows, li:li + 1], lmu1[0:rows, li:li + 1], lmu1)
        S.o("act", "activation", out=sh_l["xw"][:], in_=sh_l["xw"][:], func=AF.Tanh, reads=[sh_l["xw"]], writes=[sh_l["xw"]])
        S.o("act", "activation", out=sh_l["xg0"][:], in_=sh_l["xg0"][:], func=AF.Sigmoid, reads=[sh_l["xg0"]], writes=[sh_l["xg0"]])
        S.o("act", "activation", out=sh_l["xg1"][:], in_=sh_l["xg1"][:], func=AF.Sigmoid, reads=[sh_l["xg1"]], writes=[sh_l["xg1"]])
        for t in range(ntile):
            fs = slice(t * 128, (t + 1) * 128)
            rr, rk, rv = raw_main[0]
            load_raw(rr, 128, RB + t * 128, s)
            load_raw(rk, 128, KB_ + t * 128, s)
            load_raw(rv, 128, VB + t * 128, s)
            v = vbuf[b][t]
            shift_lerp(xr, xr[:], rr, 128, pc[:, t, MU_R:MU_R + 1], pd[:, t, 0:1], pc)
            shift_lerp(xk, xk[:], rk, 128, pc[:, t, MU_K:MU_K + 1], pd[:, t, 1:2], pc)
            shift_lerp(v, v[:], rv, 128, pc[:, t, MU_V:MU_V + 1], pd[:, t, 2:3], pc)
            if not layer1:
                out_toks.append(S.dma(q(), dr["xv"][fs, s * TS:(s + 1) * TS], v[:], reads=[v], wadd=[dr["xv_res"]]))
            ps = PB[0]
            S.o("pe", "matmul", ps[:, 0:TS], lhsT=w2[:, fs], rhs=sh_l["xw"][:], start=True, stop=True, reads=[w2, sh_l["xw"]], writes=[ps])
            S.o("act", "activation", out=dw[:], in_=ps[:, 0:TS], func=AF.Sigmoid, bias=pc[:, t, W0:W0 + 1], reads=[ps, pc], writes=[dw])
            S.o("act", "activation", out=dw[:], in_=dw[:], func=AF.Exp, scale=-float(np.exp(-0.5)), reads=[dw], writes=[dw])
            ps = PB[1]
            S.o("pe", "matmul", ps[:, 0:TS], lhsT=a2[:, fs], rhs=sh_l["xa"][:], start=True, stop=True, reads=[a2, sh_l["xa"]], writes=[ps])
            S.o("act", "activation", out=aic[:], in_=ps[:, 0:TS], func=AF.Sigmoid, bias=pc[:, t, A0:A0 + 1], reads=[ps, pc], writes=[aic])
            ps = PB[2]
            g = gbuf[b][t]
            S.o("pe", "matmul", ps[:, 0:TS], lhsT=g2a[:, fs], rhs=sh_l["xg0"][:], start=True, stop=False, reads=[g2a, sh_l["xg0"]], writes=[ps], inc=False)
            S.o("pe", "matmul", ps[:, 0:TS], lhsT=g2b[:, fs], rhs=sh_l["xg1"][:], start=False, stop=True, reads=[g2b, sh_l["xg1"]], writes=[ps])
            S.o("act", "activation", out=g[:], in_=ps[:, 0:TS], func=AF.Copy, reads=[ps], writes=[g])
            if layer1:
                ps = PB[3]
                S.dma(q(), vf[:], dr["vfirst"][fs, s * TS:(s + 1) * TS], reads=[dr["xv_res"]], writes=[vf])
                S.o("pe", "matmul", ps[:, 0:TS], lhsT=vw2[:, fs], rhs=sh_l["pv"][:], start=True, stop=True, reads=[vw2, sh_l["pv"]], writes=[ps])
                S.o("act", "activation", out=tmp1[:], in_=ps[:, 0:TS], func=AF.Sigmoid, bias=pc[:, t, V0:V0 + 1], reads=[ps, pc], writes=[tmp1])
                S.o("pool", "tensor_sub", out=vf[:], in0=vf[:], in1=v[:], reads=[vf, v], writes=[vf])
                S.o("pool", "tensor_mul", out=vf[:], in0=vf[:], in1=tmp1[:], reads=[vf, tmp1], writes=[vf])
                S.o("pool", "tensor_add", out=v[:], in0=v[:], in1=vf[:], reads=[vf, v], writes=[v])
            S.o("pool", "tensor_scalar_mul", out=kk[:], in0=xk[:], scalar1=pc[:, t, K_K:K_K + 1], reads=[xk, pc], writes=[kk])
            S.o("pool", "tensor_mul", out=tmp1[:], in0=kk[:], in1=kk[:], reads=[kk], writes=[tmp1])
            ps = PB[3]
            S.o("pe", "matmul", ps[:, 0:TS], lhsT=bo[:], rhs=tmp1[:], start=True, stop=True, reads=[bo, tmp1], writes=[ps])
            S.o("act", "activation", out=tmp2[:], in_=ps[:, 0:TS], func=AF.Sqrt, reads=[ps], writes=[tmp2])
            S.o("pool", "tensor_scalar_max", out=tmp2[:], in0=tmp2[:], scalar1=1e-12, reads=[tmp2], writes=[tmp2])
            S.o("dve", "reciprocal", out=tmp2[:], in_=tmp2[:], reads=[tmp2], writes=[tmp2])
            S.o("pool", "tensor_mul", out=kk[:], in0=kk[:], in1=tmp2[:], reads=[kk, tmp2], writes=[kk])
            S.o("pool", "tensor_scalar_mul", out=a_s[:], in0=kk[:], scalar1=-1.0, reads=[kk], writes=[a_s])
            S.o("pool", "tensor_mul", out=b_s[:], in0=kk[:], in1=aic[:], reads=[kk, aic], writes=[b_s])
            S.o("pool", "tensor_scalar", out=tmp1[:], in0=aic[:], scalar1=pc[:, t, K_A:K_A + 1], scalar2=pd[:, t, 3:4], op0=ALU.mult, op1=ALU.add,
                 reads=[aic, pc, pd], writes=[tmp1])
            S.o("pool", "tensor_mul", out=kmod[:], in0=xk[:], in1=tmp1[:], reads=[xk, tmp1], writes=[kmod])
            S.o("pool", "tensor_mul", out=tmp1[:], in0=xr[:], in1=kmod[:], reads=[xr, kmod], writes=[tmp1])
            S.o("pool", "tensor_scalar_mul", out=tmp1[:], in0=tmp1[:], scalar1=pc[:, t, R_K:R_K + 1], reads=[tmp1, pc], writes=[tmp1])
            ps = PB[0]
            S.o("pe", "matmul", ps[:, 0:TS], lhsT=bo[:], rhs=tmp1[:], start=True, stop=True, reads=[bo, tmp1], writes=[ps])
            S.o("act", "activation", out=tmp2[:], in_=ps[:, 0:TS], func=AF.Copy, reads=[ps], writes=[tmp2])
            bn = bonus[b][t]
            S.o("pool", "tensor_mul", out=bn[:], in0=tmp2[:], in1=v[:], reads=[tmp2, v], writes=[bn])
            for xi, X in enumerate((dw, a_s, b_s, kmod, xr)):
                m = xm[xmpos[0] % 2]
                xmpos[0] += 1
                for h in range(2):
                    S.o("pool", "tensor_scalar_mul", out=m[:, h, :], in0=X[:], scalar1=hmask[:, h:h + 1], reads=[X, hmask], writes=[m])
                ps = PB[1 + (xi % 2)]
                for tau in range(8):
                    lh = m[:, :, tau:TS:8]
                    S.o("pe", "matmul", ps[0:2 * NC8, tau * 64:(tau + 1) * 64], lhsT=lh, rhs=ident2[:], start=True, stop=True,
                         reads=[m, ident2], writes=[ps], inc=(tau == 7))
                xt = xtok[b][t][xi]
                S.o("act", "activation", out=xt[:], in_=ps[0:2 * NC8, :], func=AF.Copy, reads=[ps], writes=[xt])

    def scan(s):
        b = s % 2
        for c in range(NC8):
            slots = []
            for t in range(ntile):
                slot = ring[ringpos[0] % NRING]
                ringpos[0] += 1
                slots.append(slot)
                for xi in range(5):
                    ps = PB[3 + xi]
                    S.o("pe", "matmul", ps[:], lhsT=sel[:, c, :], rhs=xtok[b][t][xi][:], start=True, stop=True,
                         reads=[sel, xtok[b][t][xi]], writes=[ps])
                    S.o("act", "activation", out=slot[:, xi, :], in_=ps[:], func=AF.Copy, reads=[ps], writes=[slot])
            for tau in range(8):
                tt = c * 8 + tau
                sl = slice(tau * 64, (tau + 1) * 64)
                for t in range(ntile):
                    S.o("dve", "scalar_tensor_tensor", out=junk[t][:], in0=state[t][:], scalar=1.0, in1=slots[t][:, 1, sl], op0=ALU.mult, op1=ALU.mult, accum_out=sa[t][:],
                         reads=[state[t], slots[t]], writes=[junk[t], sa[t]])
                for t in range(ntile):
                    S.o("dve", "tensor_tensor", out=state[t][:], in0=state[t][:], in1=slots[t][:, 0, sl], op=ALU.mult,
                         reads=[state[t], slots[t]], writes=[state[t]])
                for t in range(ntile):
                    S.o("dve", "scalar_tensor_tensor", out=state[t][:], in0=slots[t][:, 2, sl], scalar=sa[t][:, 0:1], in1=state[t][:], op0=ALU.mult, op1=ALU.add,
                         reads=[state[t], slots[t], sa[t]], writes=[state[t]])
                for t in range(ntile):
                    S.o("dve", "scalar_tensor_tensor", out=state[t][:], in0=slots[t][:, 3, sl], scalar=vbuf[b][t][:, tt:tt + 1], in1=state[t][:], op0=ALU.mult, op1=ALU.add,
                         reads=[state[t], slots[t], vbuf[b][t]], writes=[state[t]])
                for t in range(ntile):
                    S.o("dve", "scalar_tensor_tensor", out=junk[t][:], in0=state[t][:], scalar=1.0, in1=slots[t][:, 4, sl], op0=ALU.mult, op1=ALU.mult, accum_out=ybuf[b][t][:, tt:tt + 1],
                         reads=[state[t], slots[t]], writes=[junk[t], ybuf[b][t]])

    def post(s):
        b = s % 2
        for t in range(ntile):
            fs = slice(t * 128, (t + 1) * 128)
            y = ybuf[b][t]
            ps = PB[0]
            S.o("pe", "matmul", ps[:, 0:TS], lhsT=bo[:], rhs=y[:], start=True, stop=True, reads=[bo, y], writes=[ps])
            S.o("act", "activation", out=pt2[:], in_=ps[:, 0:TS], func=AF.Copy, scale=1.0 / 64, reads=[ps], writes=[pt2])
            S.o("pool", "tensor_sub", out=y[:], in0=y[:], in1=pt2[:], reads=[y, pt2], writes=[y])
            S.o("pool", "tensor_mul", out=pt2[:], in0=y[:], in1=y[:], reads=[y], writes=[pt2])
            ps = PB[1]
            S.o("pe", "matmul", ps[:, 0:TS], lhsT=bo[:], rhs=pt2[:], start=True, stop=True, reads=[bo, pt2], writes=[ps])
            S.o("act", "activation", out=pt2[:], in_=ps[:, 0:TS], func=AF.Sqrt, bias=epsl[:, 0:1], scale=1.0 / 64, reads=[ps, epsl], writes=[pt2])
            S.o("dve", "reciprocal", out=pt2[:], in_=pt2[:], reads=[pt2], writes=[pt2])
            S.o("pool", "tensor_mul", out=y[:], in0=y[:], in1=pt2[:], reads=[y, pt2], writes=[y])
            S.o("pool", "tensor_scalar", out=y[:], in0=y[:], scalar1=pc[:, t, LNX_W:LNX_W + 1], scalar2=pc[:, t, LNX_B:LNX_B + 1], op0=ALU.mult, op1=ALU.add,
                 reads=[y, pc], writes=[y])
            S.o("pool", "tensor_add", out=y[:], in0=y[:], in1=bonus[b][t][:], reads=[y, bonus[b][t]], writes=[y])
            o = yo[t % 2]
            S.o("pool", "tensor_mul", out=o[:], in0=y[:], in1=gbuf[b][t][:], reads=[y, gbuf[b][t]], writes=[o])
            out_toks.append(S.dma(q(), dr["y"][t][:, s * TS:(s + 1) * TS], o[:], reads=[o], wadd=[dr["y_res"]]))

    prep(0)
    for s in range(nseg):
        if s + 1 < nseg:
            prep(s + 1)
        scan(s)
        post(s)
    return out_toks


DCH = 16


def proj_program(S, nc, T, layer1, dr):
    PB = S.PB
    xall = [a.rearrange("(r c p) t -> r p c t", r=4, p=128) for a in dr["xn_all"]]
    wv = dr["W"].rearrange("(c p) m -> p c m", p=128)
    NTT = T // 512
    mt = []
    for i in range(6):
        mt.append((i * 128, 128, "proj", i * 128))
    mt += [(768, 64, "proj", 768), (832, 64, "proj", 832), (896, 128, "proj", 896), (1024, 32, "proj", 1024)]
    if layer1:
        mt.append((1056, 32, "proj", 1056))
    for g in range(3):
        mt.append((1088 + g * 128, 128, "q", g * 128))
    for g in range(3):
        mt.append((1472 + g * 128, 128, "k", g * 128))
    MG = 3
    groups = [mt[i:i + MG] for i in range(0, len(mt), MG)]
    wring = [S.sbuf([128, DCH, 384], BF16, f"wring{i}") for i in range(2)]
    xring = [S.sbuf([128, DCH, 512], BF16, f"xring{i}") for i in range(3)]
    stg = [S.sbuf([128, 512], F32, f"stg{i}") for i in range(4)]
    stgb = [S.sbuf([128, 512], BF16, f"stgb{i}") for i in range(4)]
    toks = []
    xi = [0]
    si = [0]
    ei = [0]

    def load_x(tt):
        x = xring[xi[0] % 3]
        xi[0] += 1
        for k4 in range(4):
            S.dma(("sp", "act")[(xi[0] + k4) % 2], x[:, k4 * 4:(k4 + 1) * 4, :], xall[k4][tt // 2][:, :, (tt % 2) * 512:(tt % 2) * 512 + 512], reads=[dr["xn_all_res"]],
                  writes=[x.res(k4)])
        return x

    def load_w(gi):
        grp = groups[gi]
        w = wring[gi % 2]
        off = 0
        for (c0, rows, kind, orow) in grp:
            S.dma("pool", w[:, :, off:off + rows], wv[:, :, c0:c0 + rows], writes=[w.res(off)])
            off += 128
        return w

    wnext = load_w(0)
    for gi, grp in enumerate(groups):
        w = wnext
        if gi + 1 < len(groups):
            wnext = load_w(gi + 1)
        for tt in range(NTT):
            x = load_x(tt)
            for j, (c0, rows, kind, orow) in enumerate(grp):
                ps = PB[(tt % 2) * MG + j]
                for c in range(DCH):
                    S.o("pe", "matmul", ps[0:rows, :], lhsT=w[:, c, j * 128:j * 128 + rows], rhs=x[:, c, :], start=(c == 0), stop=(c == DCH - 1),
                        reads=[w.res(j * 128), x.res(c // 4)], writes=[ps], inc=(c == DCH - 1))
                eng = ("act", "dve")[ei[0] % 2]
                ei[0] += 1
                if kind == "proj":
                    st = stg[si[0] % 4]
                    dst = dr["proj"][orow:orow + rows, tt * 512:(tt + 1) * 512]
                else:
                    st = stgb[si[0] % 4]
                    dst = dr["qT" if kind == "q" else "kT"][orow:orow + rows, tt * 512:(tt + 1) * 512]
                si[0] += 1
                if eng == "act":
                    S.o("act", "activation", out=st[0:rows, :], in_=ps[0:rows, :], func=AF.Copy, reads=[ps], writes=[st])
                else:
                    S.o("dve", "tensor_copy", out=st[0:rows, :], in_=ps[0:rows, :], reads=[ps], writes=[st])
                toks.append(S.dma(("sp", "act")[si[0] % 2], dst, st[0:rows, :], reads=[st], wadd=[dr["res"][kind][tt]]))
    wvv = S.sbuf([128, DCH, 384], BF16, "wvv")
    S.dma("pool", wvv[:], wv[:, :, 1856:2240], writes=[wvv])
    vst = [S.sbuf([128, 384], BF16, f"vst{i}") for i in range(4)]
    for tt in range(NTT):
        x = load_x(tt)
        for bl in range(4):
            ps = PB[6 + bl % 2]
            for c in range(DCH):
                S.o("pe", "matmul", ps[:, 0:384], lhsT=x[:, c, bl * 128:(bl + 1) * 128], rhs=wvv[:, c, :], start=(c == 0), stop=(c == DCH - 1),
                    reads=[wvv, x.res(c // 4)], writes=[ps], inc=(c == DCH - 1))
            st = vst[bl]
            if bl % 2 == 0:
                S.o("act", "activation", out=st[:], in_=ps[:, 0:384], func=AF.Copy, reads=[ps], writes=[st])
            else:
                S.o("dve", "tensor_copy", out=st[:], in_=ps[:, 0:384], reads=[ps], writes=[st])
            r0 = tt * 512 + bl * 128
            toks.append(S.dma(("sp", "act")[bl % 2], dr["vtok"][r0:r0 + 128, :], st[:], reads=[st], wadd=[dr["res"]["v"][tt]]))
    return toks

import numpy as np
import math

ATT_GROUPS = ((128, 1), (512, 4), (2048, 16))
E = 128
NEG = -1.0e30


def alibi_slopes(n_heads):
    def geometric(n):
        start = 2.0 ** (-8.0 / n)
        return [start ** (i + 1) for i in range(n)]
    closest = 2 ** int(math.floor(math.log2(n_heads)))
    slopes = geometric(closest)
    if closest < n_heads:
        slopes += geometric(2 * closest)[0::2][: n_heads - closest]
    return np.array(sorted(slopes, reverse=True), dtype=np.float32)


def attn_bias_host(slot):
    sl = alibi_slopes(12)
    out = np.zeros((3, 2, 128, 256), np.float32)
    i = np.arange(128)[:, None]
    j = np.arange(256)[None, :]
    rel = 128 + i - j
    for g, (win, dil) in enumerate(ATT_GROUPS):
        slope = sl[g * 4 + slot]
        base = -slope * (rel * dil).astype(np.float32)
        m = (rel >= 0) & (rel <= 128)
        out[g, 0] = np.where(m, base, NEG)
        out[g, 1] = np.where(m & (j >= 128), base, NEG)
    return np.ascontiguousarray(out.transpose(2, 0, 1, 3).reshape(128, 6, 256))


def attn_program(S, nc, T, dr):
    PB = S.PB[0:6]
    PT = []
    for i in range(2):
        pt_ = Tile(S.PB[6 + i][:, 0:128].bitcast(BF16).rearrange("p (a b) -> p a b", a=2), f"pt{i}")
        pt_.r = S.PB[6 + i].r
        PT.append(pt_)
    bias = S.sbuf([128, 6, 256], F32, "bias")
    ident = S.sbuf([128, 128], BF16, "ident")
    S.dma("sp", bias[:], dr["bias"], writes=[bias])
    S.dma("sp", ident[:], dr["ident"], writes=[ident])
    q = [S.sbuf([128, T], BF16, f"q{i}") for i in range(2)]
    k = [S.sbuf([128, T], BF16, f"k{i}") for i in range(2)]
    v = [S.sbuf([128, 32, 128], BF16, f"v{i}") for i in range(2)]
    og = [S.sbuf([128, 32, 128], F32, f"og{i}") for i in range(2)]
    lse = [S.sbuf([128, 32], F32, f"lse{i}") for i in range(2)]
    ssb = [S.sbuf([128, 256], F32, f"ssb{i}") for i in range(2)]
    pb16 = [S.sbuf([128, 256], BF16, f"p16_{i}") for i in range(2)]
    ptsb = [S.sbuf([128, 2, 128], BF16, f"ptsb{i}") for i in range(2)]
    mx = [S.sbuf([128, 1], F32, f"mx{i}") for i in range(2)]
    nmx = [S.sbuf([128, 1], F32, f"nmx{i}") for i in range(2)]
    den = [S.sbuf([128, 1], F32, f"den{i}") for i in range(2)]
    rden = [S.sbuf([128, 1], F32, f"rden{i}") for i in range(2)]
    lnd = [S.sbuf([128, 1], F32, f"lnd{i}") for i in range(2)]
    og_res = {}
    lse_res = {}
    NB = T // 128
    bi = 0
    for g, (win, d) in enumerate(ATT_GROUPS):
        gb = g % 2
        S.dma("sp", q[gb][:], dr["qT"][g * 128:(g + 1) * 128, :], reads=dr["res"]["q"], writes=[q[gb]])
        S.dma("act", k[gb][:], dr["kT"][g * 128:(g + 1) * 128, :], reads=dr["res"]["k"], writes=[k[gb]])
        nbr = NB // d
        vsrc = dr["vtok"][:, g * 128:(g + 1) * 128]
        for r in range(d):
            src = vsrc[r:T:d, :].rearrange("(n j) e -> j n e", j=128)
            S.dma(("sp", "act")[r % 2], v[gb][:, r * nbr:(r + 1) * nbr, :], src, reads=dr["res"]["v"], writes=[v[gb].res(r)])
        for r in range(d):
            for n in range(nbr):
                blk = r * nbr + n
                b2 = bi % 2
                bi += 1
                first = (n == 0)
                kc0 = 128 if first else 0
                ps = PB[b2]
                qs = q[gb][:, r + d * 128 * n: r + d * 128 * n + d * 127 + 1: d]
                k0 = r + d * 128 * (n - 1) if not first else r
                nk = 128 if first else 256
                ks = k[gb][:, k0: k0 + d * (nk - 1) + 1: d]
                S.o("pe", "matmul", ps[:, kc0:256], lhsT=qs, rhs=ks, start=True, stop=True, reads=[q[gb], k[gb]], writes=[ps])
                s_ = ssb[b2]
                S.o("dve", "scalar_tensor_tensor", out=s_[:, kc0:256], in0=ps[:, kc0:256], scalar=float(E ** -0.5), in1=bias[:, g * 2 + (1 if first else 0), kc0:256],
                    op0=ALU.mult, op1=ALU.add, reads=[ps, bias], writes=[s_])
                S.o("dve", "tensor_reduce", out=mx[b2][:], in_=s_[:, kc0:256], axis=AX.X, op=ALU.max, reads=[s_], writes=[mx[b2]])
                S.o("dve", "tensor_scalar_mul", out=nmx[b2][:], in0=mx[b2][:], scalar1=-1.0, reads=[mx[b2]], writes=[nmx[b2]])
                p_ = pb16[b2]
                S.o("act", "activation", out=p_[:, kc0:256], in_=s_[:, kc0:256], func=AF.Exp, bias=nmx[b2][:, 0:1], accum_out=den[b2][:],
                    reads=[s_, nmx[b2]], writes=[p_, den[b2]])
                pt = PT[b2]
                for hb in range(2 if not first else 1):
                    kb = hb if not first else 1
                    S.o("pe", "transpose", pt[:, kb, :], p_[:, kb * 128:(kb + 1) * 128], ident[:], reads=[p_, ident], writes=[pt], inc=(hb == (0 if first else 1)))
                pts = ptsb[b2]
                if first:
                    S.o("act", "activation", out=pts[:, 1, :], in_=pt[:, 1, :], func=AF.Copy, reads=[pt], writes=[pts])
                else:
                    S.o("act", "activation", out=pts[:, 0, :], in_=pt[:, 0, :], func=AF.Copy, reads=[pt], writes=[pts])
                    S.o("dve", "tensor_copy", out=pts[:, 1, :], in_=pt[:, 1, :], reads=[pt], writes=[pts])
                po = PB[2 + b2]
                if first:
                    S.o("pe", "matmul", po[:, 0:128], lhsT=pts[:, 1, :], rhs=v[gb][:, blk, :], start=True, stop=True, reads=[pts, v[gb].res(r)], writes=[po])
                else:
                    S.o("pe", "matmul", po[:, 0:128], lhsT=pts[:, 0, :], rhs=v[gb][:, blk - 1, :], start=True, stop=False, reads=[pts, v[gb].res(r)], writes=[po], inc=False)
                    S.o("pe", "matmul", po[:, 0:128], lhsT=pts[:, 1, :], rhs=v[gb][:, blk, :], start=False, stop=True, reads=[pts, v[gb].res(r)], writes=[po])
                S.o("dve", "reciprocal", out=rden[b2][:], in_=den[b2][:], reads=[den[b2]], writes=[rden[b2]])
                S.o("act", "activation", out=og[gb][:, blk, :], in_=po[:, 0:128], func=AF.Copy, scale=rden[b2][:, 0:1], reads=[po, rden[b2]], writes=[og[gb]])
                S.o("act", "activation", out=lnd[b2][:], in_=den[b2][:], func=AF.Ln, reads=[den[b2]], writes=[lnd[b2]])
                S.o("dve", "tensor_add", out=lse[gb][:, blk:blk + 1], in0=lnd[b2][:], in1=mx[b2][:], reads=[lnd[b2], mx[b2]], writes=[lse[gb]])
        for r in range(d):
            for n in range(nbr):
                blk = r * nbr + n
                t0 = r + d * 128 * n
                dst = dr["og"][g, t0:t0 + d * 127 + 1:d, :]
                S.dma(("sp", "act")[blk % 2], dst, og[gb][:, blk, :], reads=[og[gb]], writes=[og_res.setdefault((g, r, n), Res("ogd"))])
                dl = dr["lse"][g, t0:t0 + d * 127 + 1:d].rearrange("(j o) -> j o", o=1)
                S.dma(("sp", "act")[(blk + 1) % 2], dl, lse[gb][:, blk:blk + 1], reads=[lse[gb]], writes=[lse_res.setdefault((g, r, n), Res("lsed"))], allow_slow_non_contiguous=True)
    toks = []
    ogm = [S.sbuf([128, 3, 128], F32, f"ogm{i}") for i in range(2)]
    lsm = [S.sbuf([128, 3], F32, f"lsm{i}") for i in range(2)]
    wm = [S.sbuf([128, 3], F32, f"wm{i}") for i in range(2)]
    mm = [S.sbuf([128, 1], F32, f"mm{i}") for i in range(2)]
    zz = [S.sbuf([128, 1], F32, f"zz{i}") for i in range(2)]
    acc = [S.sbuf([128, 128], F32, f"acc{i}") for i in range(2)]
    yo = [S.sbuf([128, 128], BF16, f"yo{i}") for i in range(2)]
    yoT = [S.sbuf([128, 128], BF16, f"yoT{i}") for i in range(2)]
    for nb in range(NB):
        b2 = nb % 2
        t0 = nb * 128
        rdo = [og_res[(g, r, nb // d)] for g, (_w, d) in enumerate(ATT_GROUPS) for r in range(d)]
        rdl = [lse_res[(g, r, nb // d)] for g, (_w, d) in enumerate(ATT_GROUPS) for r in range(d)]
        S.dma("sp", ogm[b2][:], dr["og"][:, t0:t0 + 128, :].rearrange("g t e -> t g e"), reads=rdo, writes=[ogm[b2]])
        S.dma("act", lsm[b2][:], dr["lse"][:, t0:t0 + 128].rearrange("g t -> t g"), reads=rdl, writes=[lsm[b2]], allow_slow_non_contiguous=True)
        S.o("dve", "tensor_reduce", out=mm[b2][:], in_=lsm[b2][:], axis=AX.X, op=ALU.max, reads=[lsm[b2]], writes=[mm[b2]])
        S.o("dve", "tensor_scalar_mul", out=mm[b2][:], in0=mm[b2][:], scalar1=-1.0, reads=[mm[b2]], writes=[mm[b2]])
        S.o("act", "activation", out=wm[b2][:], in_=lsm[b2][:], func=AF.Exp, bias=mm[b2][:, 0:1], accum_out=zz[b2][:], reads=[lsm[b2], mm[b2]], writes=[wm[b2], zz[b2]])
        S.o("dve", "reciprocal", out=zz[b2][:], in_=zz[b2][:], reads=[zz[b2]], writes=[zz[b2]])
        S.o("dve", "tensor_scalar_mul", out=wm[b2][:], in0=wm[b2][:], scalar1=zz[b2][:, 0:1], reads=[wm[b2], zz[b2]], writes=[wm[b2]])
        S.o("dve", "tensor_scalar_mul", out=acc[b2][:], in0=ogm[b2][:, 0, :], scalar1=wm[b2][:, 0:1], reads=[ogm[b2], wm[b2]], writes=[acc[b2]])
        S.o("dve", "scalar_tensor_tensor", out=acc[b2][:], in0=ogm[b2][:, 1, :], scalar=wm[b2][:, 1:2], in1=acc[b2][:], op0=ALU.mult, op1=ALU.add,
            reads=[ogm[b2], wm[b2], acc[b2]], writes=[acc[b2]])
        S.o("dve", "scalar_tensor_tensor", out=yo[b2][:], in0=ogm[b2][:, 2, :], scalar=wm[b2][:, 2:3], in1=acc[b2][:], op0=ALU.mult, op1=ALU.add,
            reads=[ogm[b2], wm[b2], acc[b2]], writes=[yo[b2]])
        ptt = PT[b2]
        S.o("pe", "transpose", ptt[:, 0, :], yo[b2][:], ident[:], reads=[yo[b2], ident], writes=[ptt])
        S.o("act", "activation", out=yoT[b2][:], in_=ptt[:, 0, :], func=AF.Copy, reads=[ptt], writes=[yoT[b2]])
        toks.append(S.dma("sp", dr["yT"][:, t0:t0 + 128], yoT[b2][:], reads=[yoT[b2]], wadd=[dr["y_res"]]))
    return toks


def np_attn_ref(q, k, v, slot):
    T = q.shape[0]
    sl = alibi_slopes(12)
    outs, lses = [], []
    for g, (win, d) in enumerate(ATT_GROUPS):
        slope = float(sl[g * 4 + slot])
        o = np.zeros((T, 128))
        ls = np.zeros(T)
        for t in range(T):
            ks = t - d * np.arange(0, 129)
            ks = ks[ks >= 0]
            s = (k[ks, g] @ q[t, g]) / np.sqrt(128.0) - slope * (t - ks)
            m = s.max()
            p = np.exp(s - m)
            ls[t] = m + np.log(p.sum())
            o[t] = (p / p.sum()) @ v[ks, g]
        outs.append(o)
        lses.append(ls)
    L = np.stack(lses, 0)
    w = np.exp(L - L.max(0))
    w /= w.sum(0)
    return sum(w[g][:, None] * outs[g] for g in range(3))


N_SHIFT = 3360
N_QKV = 4608
SGU0 = N_SHIFT + N_QKV
GATE0 = SGU0 + 1024
LN_EPS = 1e-5
GELU_C = 1.5957691216057308


def gelu_from_psum(S, ps_t, ps_ap, sb, out_t, out_ap):
    S.o("act", "activation", out=sb[1], in_=ps_ap, func=AF.Square, reads=[ps_t], writes=[sb[0]])
    S.o("dve", "tensor_scalar", out=sb[1], in0=sb[1], scalar1=0.044715, scalar2=1.0, op0=ALU.mult, op1=ALU.add, reads=[sb[0]], writes=[sb[0]])
    S.o("dve", "tensor_tensor", out=sb[1], in0=sb[1], in1=ps_ap, op=ALU.mult, reads=[sb[0], ps_t], writes=[sb[0]])
    S.o("act", "activation", out=sb[1], in_=sb[1], func=AF.Sigmoid, scale=GELU_C, reads=[sb[0]], writes=[sb[0]])
    S.o("dve", "tensor_tensor", out=out_ap, in0=sb[1], in1=ps_ap, op=ALU.mult, reads=[sb[0], ps_t], writes=[out_t])


def merge_stage(S, C, dr, P):
    NT = C["NT"]
    NH = NT // 512
    NCH = NT // 128
    PB = C["PB"]
    hidR, AR, ring = C["hid"], C["A"], C["ring"]
    xn = AR.view(0, [128, DC, NT], BF16, "xn_m")
    yall = hidR.view(0, [128, 16, NT], BF16, "yall")
    merged = hidR.view(32 * NT, [128, 16, NT], BF16, "merged")
    vn = hidR.view(64 * NT, [128, NCH, 512], BF16, "vn")
    u = hidR.view(72 * NT, [128, 4, NT], BF16, "u_sgu")
    acc = hidR.view(80 * NT, [128, NT], F32, "acc")
    tmpa = C["hld"][0]
    f = lambda shape, name, dt=F32: S.sbuf(shape, dt, name)
    ab = 32 * NT
    gsc = []
    for i in range(2):
        tl = Tile(C["sil"][i][:, 0:512], f"gsc{i}")
        tl.r = C["sil"][i].r
        gsc.append(tl)
    ge = Tile(C["rstd"][:, 0:512], "ge")
    ge.r = C["rstd"].r
    lng = AR.view(ab, [128, 512], F32, "lng"); lnb = AR.view(ab + 2048, [128, 512], F32, "lnb")
    bsb = AR.view(ab + 4096, [128, 4, 128], F32, "bsb")
    wst = AR.view(ab + 6144, [128, 4, 128], F32, "wst"); cm = AR.view(ab + 8192, [128, 128], F32, "cm")
    wtm = AR.view(ab + 8704, [128, 4, 128], BF16, "wtm")
    st1 = f([128, 4], "st1")
    lneps = f([128, 1], "lneps")
    S.o("pool", "memset", lneps[:], LN_EPS, writes=[lneps])
    for c4 in range(4):
        S.dma(("sp", "act")[c4 % 2], xn[:, c4 * 4:(c4 + 1) * 4, :], dr["xnT"][c4].rearrange("(c p) t -> p c t", p=128), reads=[dr["xn_res"]], writes=[xn.res(c4)])
    xn_all = [xn.res(i) for i in range(4)]
    qm = P["qmask"]
    cand = [AR.view(ab + 12288 + i * 8192, [128, 4, NT], BF16, f"cand{i}") for i in range(2)]
    ya_ = dr["y_all"]
    for ci in range(12):
        if ci < 8:
            ysrc = ya_[ci % 2][(ci // 2) * 128:(ci // 2 + 1) * 128, :]
        else:
            ysrc = ya_[2][(ci - 8) * 128:(ci - 7) * 128, :]
        cd = cand[ci % 2]
        S.dma(("sp", "act")[ci % 2], cd[:], ysrc.rearrange("p (q t) -> p q t", q=4), reads=[dr["y_all_res"]], writes=[cd])
        yr_ = yall.res(ci)
        S.o("dve", "tensor_scalar_mul", out=yall[:, ci, :], in0=cd[:, 0, :], scalar1=qm[:, 0:1], reads=[cd, qm], writes=[yr_])
        for qq in range(1, 4):
            S.o("dve", "scalar_tensor_tensor", out=yall[:, ci, :], in0=cd[:, qq, :], scalar=qm[:, qq:qq + 1], in1=yall[:, ci, :], op0=ALU.mult, op1=ALU.add,
                reads=[cd, qm, yr_], writes=[yr_])
    S.dma("act", lng[:], dr["lng"], writes=[lng])
    S.dma("act", lnb[:], dr["lnb"], writes=[lnb])
    S.dma("act", bsb[:], dr["sgub"], writes=[bsb])
    S.dma("sp", wst[:], dr["wsT"], writes=[wst])
    S.dma("sp", cm[:], dr["cmask"], writes=[cm])
    S.o("dve", "tensor_tensor", out=wtm[:], in0=wst[:], in1=cm[:].unsqueeze(1).to_broadcast([128, 4, 128]), op=ALU.mult, reads=[wst, cm], writes=[wtm])
    wiv = dr["w_in"].rearrange("(c p) m -> p c m", p=128)
    wv_ = ring.view(0, [128, DC, 512], BF16, "w_sguv")
    wu_ = ring.view(16384, [128, DC, 512], BF16, "w_sguu")
    S.dma("pool", wv_[:], wiv[:, :, SGU0 + 512:SGU0 + 1024], writes=[wv_])
    S.dma("pool", wu_[:], wiv[:, :, SGU0:SGU0 + 512], writes=[wu_])
    for ch in range(NCH):
        ps = PB[ch % 2]
        for c in range(DC):
            S.o("pe", "matmul", ps[:], lhsT=xn[:, c, ch * 128:(ch + 1) * 128], rhs=wv_[:, c, :], start=(c == 0), stop=(c == DC - 1),
                reads=[xn.res(c // 4), wv_], writes=[ps], inc=(c == DC - 1))
        sc = gsc[ch % 2]
        gelu_from_psum(S, ps, ps[:], (sc, sc[:]), ge, ge[:])
        S.o("dve", "tensor_reduce", out=st1[:, 0:1], in_=ge[:], axis=AX.X, op=ALU.add, reads=[ge], writes=[st1])
        S.o("dve", "tensor_scalar_mul", out=st1[:, 0:1], in0=st1[:, 0:1], scalar1=-1.0 / 512, reads=[st1], writes=[st1])
        S.o("act", "activation", out=ge[:], in_=ge[:], func=AF.Identity, bias=st1[:, 0:1], reads=[ge, st1], writes=[ge])
        S.o("dve", "scalar_tensor_tensor", out=sc[:], in0=ge[:], scalar=1.0, in1=ge[:], op0=ALU.mult, op1=ALU.mult, accum_out=st1[:, 1:2],
            reads=[ge], writes=[sc, st1])
        S.o("act", "activation", out=st1[:, 2:3], in_=st1[:, 1:2], func=AF.Sqrt, bias=lneps[:, 0:1], scale=1.0 / 512, reads=[st1, lneps], writes=[st1])
        S.o("dve", "reciprocal", out=st1[:, 3:4], in_=st1[:, 2:3], reads=[st1], writes=[st1])
        S.o("dve", "scalar_tensor_tensor", out=ge[:], in0=ge[:], scalar=st1[:, 3:4], in1=lng[:], op0=ALU.mult, op1=ALU.mult, reads=[ge, st1, lng], writes=[ge])
        S.o("dve", "tensor_tensor", out=vn[:, ch, :], in0=ge[:], in1=lnb[:], op=ALU.add, reads=[ge, lnb], writes=[vn])
    for mtile in range(4):
        for hf in range(NH):
            ps = PB[2 + (mtile * NH + hf) % 2]
            for c in range(DC):
                S.o("pe", "matmul", ps[:], lhsT=wu_[:, c, mtile * 128:(mtile + 1) * 128], rhs=xn[:, c, hf * 512:(hf + 1) * 512], start=(c == 0), stop=(c == DC - 1),
                    reads=[xn.res(c // 4), wu_], writes=[ps], inc=(c == DC - 1))
            sc = gsc[(mtile * NH + hf) % 2]
            gelu_from_psum(S, ps, ps[:], (sc, sc[:]), u, u[:, mtile, hf * 512:(hf + 1) * 512])
    for ch in range(NCH):
        ps = PB[4 + ch % 2]
        for g in range(4):
            S.o("pe", "matmul", ps[:, g * 128:(g + 1) * 128], lhsT=vn[:, ch, g * 128:(g + 1) * 128], rhs=wtm[:, g, :], start=True, stop=True,
                reads=[vn, wtm], writes=[ps], inc=(g == 3))
        sc = gsc[ch % 2]
        S.o("dve", "tensor_tensor", out=sc[:], in0=ps[:], in1=bsb[:].rearrange("p g t -> p (g t)"), op=ALU.add, reads=[ps, bsb], writes=[sc])
        S.o("dve", "tensor_tensor", out=yall[:, 12:16, ch * 128:(ch + 1) * 128], in0=sc[:].rearrange("p (g t) -> p g t", g=4),
            in1=u[:, :, ch * 128:(ch + 1) * 128], op=ALU.mult, reads=[sc, u], writes=[yall.res(99)])
    wbv = [dr["w_b_rwkv"].rearrange("(c p) m -> p c m", p=128), dr["w_b_attn"].rearrange("(c p) m -> p c m", p=128),
           dr["w_b_sgu"].rearrange("(c p) m -> p c m", p=128)]
    kcs = [8, 4, 4]
    yoff = [0, 8, 12]
    yres = [[yall.res(i) for i in range(8)], [yall.res(8 + i) for i in range(4)], [yall.res(99)]]
    slabs = [ring.view(i * 16384, [128, 64, 128], BF16, f"mslab{i}") for i in range(2)]

    def load_slab(dt):
        sl = slabs[dt % 2]
        for br in range(3):
            c0 = GATE0 + br * 2048 + dt * 128
            S.dma("pool", sl[:, br * 16:(br + 1) * 16, :], wiv[:, :, c0:c0 + 128], writes=[sl.res(br)])
        o = 48
        for br in range(3):
            S.dma("pool", sl[:, o:o + kcs[br], :], wbv[br][:, :, dt * 128:(dt + 1) * 128], writes=[sl.res(3 + br)])
            o += kcs[br]
        return sl

    nxt = load_slab(0)
    for dt in range(DC):
        sl = nxt
        if dt + 1 < DC:
            nxt = load_slab(dt + 1)
        for br in range(3):
            i = dt * 3 + br
            s = i % 2
            G = PB[4 * s:4 * s + NH]
            Pj = PB[4 * s + 2:4 * s + 2 + NH]
            for c in range(DC):
                for hf in range(NH):
                    S.o("pe", "matmul", G[hf][:], lhsT=sl[:, br * 16 + c, :], rhs=xn[:, c, hf * 512:(hf + 1) * 512], start=(c == 0), stop=(c == DC - 1),
                        reads=[sl.res(br), xn.res(c // 4)], writes=[G[hf]], inc=(c == DC - 1))
            o = 48 + sum(kcs[:br])
            for kc in range(kcs[br]):
                for hf in range(NH):
                    S.o("pe", "matmul", Pj[hf][:], lhsT=sl[:, o + kc, :], rhs=yall[:, yoff[br] + kc, hf * 512:(hf + 1) * 512], start=(kc == 0), stop=(kc == kcs[br] - 1),
                        reads=[sl.res(3 + br)] + yres[br], writes=[Pj[hf]], inc=(kc == kcs[br] - 1))
            gs = C["sil"][s]
            for hf in range(NH):
                hs = slice(hf * 512, (hf + 1) * 512)
                S.o("act", "activation", out=gs[:, hs], in_=G[hf][:], func=AF.Sigmoid, reads=[G[hf]], writes=[gs])
                if br == 0:
                    S.o("dve", "tensor_tensor", out=acc[:, hs], in0=gs[:, hs], in1=Pj[hf][:], op=ALU.mult, reads=[gs, Pj[hf]], writes=[acc])
                elif br == 1:
                    S.o("dve", "tensor_tensor", out=tmpa[:, hs], in0=gs[:, hs], in1=Pj[hf][:], op=ALU.mult, reads=[gs, Pj[hf]], writes=[tmpa])
                    S.o("pool", "tensor_tensor", out=acc[:, hs], in0=acc[:, hs], in1=tmpa[:, hs], op=ALU.add, reads=[acc, tmpa], writes=[acc])
                else:
                    S.o("dve", "tensor_tensor", out=tmpa[:, hs], in0=gs[:, hs], in1=Pj[hf][:], op=ALU.mult, reads=[gs, Pj[hf]], writes=[tmpa])
                    S.o("pool", "tensor_tensor", out=merged[:, dt, hs], in0=acc[:, hs], in1=tmpa[:, hs], op=ALU.add, reads=[acc, tmpa], writes=[merged])
    yT = AR.view(0, [128, DC, NT], F32, "yT_m")
    wov = dr["w_out"].rearrange("(c p) m -> p c m", p=128)
    wo = [ring.view(i * 4096, [128, DC, 128], BF16, f"wo{i}") for i in range(2)]
    SS = PB[4:4 + NH]

    def load_wo(dt):
        S.dma("pool", wo[dt % 2][:], wov[:, :, dt * 128:(dt + 1) * 128], writes=[wo[dt % 2]])

    load_wo(0)
    for dt in range(DC):
        if dt + 1 < DC:
            load_wo(dt + 1)
        Y = PB[2 * (dt % 2):2 * (dt % 2) + NH]
        w = wo[dt % 2]
        for c in range(DC):
            for hf in range(NH):
                S.o("pe", "matmul", Y[hf][:], lhsT=w[:, c, :], rhs=merged[:, c, hf * 512:(hf + 1) * 512], start=(c == 0), stop=(c == DC - 1),
                    reads=[w, merged], writes=[Y[hf]], inc=(c == DC - 1))
        evac_y_tile(S, C, Y, dt, yT, SS, NH)
    return residual_epilogue(S, C, yT, SS, dr["hT"], P.get("hres"), P["post_g"], dr["hT_out"], P.get("hres_out"), P.get("next_g"), dr.get("xnT_out"))


import ml_dtypes
from concourse.bass_utils import run_bass_kernel_spmd

NCORES = 8
SEQ = 4096
NTOK = 1024
BF = ml_dtypes.bfloat16
DEPTH = 2
RGROUPS = [[0, 1, 2, 3], [4, 5, 6, 7]]
_PROG = {}


def _col16(v):
    return np.ascontiguousarray(np.asarray(v, np.float32).reshape(16, 128).T)


def build_fused():
    nc = bass.Bass("TRN2", target_bir_lowering=False)
    di = lambda n, sh, dt=F32: nc.dram_tensor(n, sh, dt, kind="ExternalInput").ap()
    dint = lambda n, sh, dt=F32: nc.dram_tensor(n, sh, dt).ap()
    L = DEPTH
    I = {}
    I["hT0"] = di("hT0", [D, NTOK])
    for nm in ("ffn1_w_gu", "ffn2_w_gu"):
        I[nm] = di(nm, [L, D, 2 * F])
    for nm in ("ffn1_w_down", "ffn2_w_down"):
        I[nm] = di(nm, [L, F, D])
    I["w_in"] = di("w_in", [L, D, 15136])
    I["w_b_rwkv"] = di("w_b_rwkv", [L, 1024, D]); I["w_b_attn"] = di("w_b_attn", [L, 512, D]); I["w_b_sgu"] = di("w_b_sgu", [L, 512, D])
    I["w_out"] = di("w_out", [L, D, D])
    I["gs"] = di("gs", [128, 6 * L, 16])
    I["Wkp"] = di("Wkp", [L, D, 2240])
    I["pc"] = di("pc", [L, 128, 2, 11]); I["lmu"] = di("lmu", [L, 128, 5])
    I["w2"] = di("w2", [L, 64, 256]); I["a2"] = di("a2", [L, 64, 256]); I["g2"] = di("g2", [L, 160, 256]); I["vw2"] = di("vw2", [32, 256])
    for nm, sh in (("c_sel", [64, 32, 128]), ("c_ident2", [128, 64]), ("c_bo", [128, 128]), ("c_hmask", [128, 2])):
        I[nm] = di(nm, sh)
    I["bias"] = di("bias", [128, 6, 256]); I["ident"] = di("ident", [128, 128], BF16)
    I["qmask"] = di("qmask", [128, 4])
    I["wsT"] = di("wsT", [L, 128, 4, 128]); I["cmask"] = di("cmask", [128, 128])
    I["lng"] = di("lng", [L, 128, 512]); I["lnb"] = di("lnb", [L, 128, 512]); I["sgub"] = di("sgub", [L, 128, 4, 128])
    out = nc.dram_tensor("hT_out", [D, NTOK], F32, kind="ExternalOutput").ap()

    S = Sched(nc, same_engine_sync=True)
    S.PB = [S.psum([128, 512], F32, f"pb{i}") for i in range(8)]
    gs = S.sbuf([128, 6 * L, 16], F32, "gs_sb")
    S.dma("sp", gs[:], I["gs"], writes=[gs])
    gt = []
    for i in range(6 * L):
        t = Tile(gs[:, i, :], f"g{i}")
        t.r = gs.r
        gt.append(t)
    gh = S.sbuf([128, 2 * L, 16], F32, "gs_half")
    ght = []
    for l in range(L):
        for j, src in enumerate((1, 5)):
            S.o("dve", "tensor_scalar_mul", out=gh[:, 2 * l + j, :], in0=gs[:, 6 * l + src, :], scalar1=0.5, reads=[gs], writes=[gh])
            t = Tile(gh[:, 2 * l + j, :], f"gh{l}{j}")
            t.r = gh.r
            ght.append(t)
    qm = S.sbuf([128, 4], F32, "qmask_sb")
    S.dma("sp", qm[:], I["qmask"], writes=[qm])

    hs = [dint(f"h_scr{i}", [D, NTOK]) for i in range(5)]
    hres = [[Res(f"h{i}_{d}") for d in range(16)] for i in range(5)]
    xn_own = [[dint(f"xn_own{l}_{k}", [512, NTOK], BF16) for k in range(4)] for l in range(L)]
    xn_own_res = [Res(f"xn_own_res{l}") for l in range(L)]
    xn_all = [[dint(f"xn_all{l}_{k}", [4 * 512, NTOK], BF16) for k in range(4)] for l in range(L)]
    xn_all_res = [Res(f"xn_all_res{l}") for l in range(L)]
    proj = [dint(f"proj{l}", [1088, SEQ]) for l in range(L)]
    qT = [dint(f"qT{l}", [384, SEQ], BF16) for l in range(L)]
    kT = [dint(f"kT{l}", [384, SEQ], BF16) for l in range(L)]
    vtok = [dint(f"vtok{l}", [SEQ, 384], BF16) for l in range(L)]
    og = [dint(f"og{l}", [3, SEQ, 128]) for l in range(L)]
    lse = [dint(f"lse{l}", [3, SEQ]) for l in range(L)]
    y_own = [[dint(f"y_own{l}_{k}", [128, SEQ], BF16) for k in range(3)] for l in range(L)]
    y_own_res = [Res(f"y_own_res{l}") for l in range(L)]
    y_all = [[dint(f"y_all{l}_{k}", [4 * 128, SEQ], BF16) for k in range(3)] for l in range(L)]
    y_all_res = [Res(f"y_all_res{l}") for l in range(L)]
    xv = dint("xv_first", [256, SEQ])
    xv_res = Res("xv_res")

    m0 = S.phase_mark()
    C = alloc_common(S, NTOK)
    ffn_stage(S, C, I["hT0"], I["ffn1_w_gu"][0], I["ffn1_w_down"][0], gt[0], ght[0], hs[0], gt[2], xn_own[0], hres=None, hres_out=hres[0], xres_out=xn_own_res[0])
    S.phase_release(m0)
    hcur = 0
    toks = None
    for l in range(L):
        layer1 = l > 0
        last = l == L - 1
        for k in range(4):
            S.collective("AllGather", ALU.bypass, RGROUPS, xn_own[l][k].opt(), xn_all[l][k].opt(), reads=[xn_own_res[l]], wadd=[xn_all_res[l]])
        res = {"proj": [Res(f"pr{l}_{t}") for t in range(8)], "q": [Res(f"q{l}_{t}") for t in range(8)],
               "k": [Res(f"k{l}_{t}") for t in range(8)], "v": [Res(f"v{l}_{t}") for t in range(8)]}
        m = S.phase_mark()
        proj_program(S, nc, SEQ, layer1, {"xn_all": xn_all[l], "xn_all_res": xn_all_res[l], "W": I["Wkp"][l], "proj": proj[l], "qT": qT[l], "kT": kT[l],
                                          "vtok": vtok[l], "res": res})
        S.phase_release(m)
        m = S.phase_mark()
        drr = {"proj": proj[l], "res": res, "pc": I["pc"][l], "lmu": I["lmu"][l], "w2": I["w2"][l], "a2": I["a2"][l], "g2": I["g2"][l],
               "c_sel": I["c_sel"], "c_ident2": I["c_ident2"], "c_bo": I["c_bo"], "c_hmask": I["c_hmask"],
               "y": y_own[l][0:2], "y_res": y_own_res[l], "xv": xv, "xv_res": xv_res, "vfirst": xv, "vw2": I["vw2"]}
        rwkv_program(S, nc, 2, SEQ, layer1, drr)
        S.phase_release(m)
        m = S.phase_mark()
        attn_program(S, nc, SEQ, {"qT": qT[l], "kT": kT[l], "vtok": vtok[l], "res": res, "bias": I["bias"], "ident": I["ident"], "og": og[l], "lse": lse[l],
                                  "yT": y_own[l][2], "y_res": y_own_res[l]})
        S.phase_release(m)
        for k in range(3):
            S.collective("AllGather", ALU.bypass, RGROUPS, y_own[l][k].opt(), y_all[l][k].opt(), reads=[y_own_res[l]], wadd=[y_all_res[l]])
        m = S.phase_mark()
        C = alloc_common(S, NTOK)
        drm = {"xnT": xn_own[l], "xn_res": xn_own_res[l], "hT": hs[hcur], "y_all": y_all[l], "y_all_res": y_all_res[l],
               "w_in": I["w_in"][l], "w_b_rwkv": I["w_b_rwkv"][l], "w_b_attn": I["w_b_attn"][l], "w_b_sgu": I["w_b_sgu"][l], "w_out": I["w_out"][l],
               "wsT": I["wsT"][l], "cmask": I["cmask"], "lng": I["lng"][l], "lnb": I["lnb"][l], "sgub": I["sgub"][l], "hT_out": hs[hcur + 1]}
        merge_stage(S, C, drm, {"post_g": gt[6 * l + 3], "hres": hres[hcur], "hres_out": hres[hcur + 1], "qmask": qm})
        hcur += 1
        if last:
            toks = ffn_stage(S, C, hs[hcur], I["ffn2_w_gu"][l], I["ffn2_w_down"][l], gt[6 * l + 4], ght[2 * l + 1], out, hres=hres[hcur])
        else:
            ffn_stage(S, C, hs[hcur], I["ffn2_w_gu"][l], I["ffn2_w_down"][l], gt[6 * l + 4], ght[2 * l + 1], hs[hcur + 1], hres=hres[hcur], hres_out=hres[hcur + 1])
            hcur += 1
            ffn_stage(S, C, hs[hcur], I["ffn1_w_gu"][l + 1], I["ffn1_w_down"][l + 1], gt[6 * (l + 1)], ght[2 * (l + 1)], hs[hcur + 1], gt[6 * (l + 1) + 2], xn_own[l + 1],
                      hres=hres[hcur], hres_out=hres[hcur + 1], xres_out=xn_own_res[l + 1])
            hcur += 1
        S.phase_release(m)
    S.wait_all("sp", toks)
    S.emit()
    S.close()
    return nc


def kernel(x, ffn1_pre_g, ffn1_w_gu, ffn1_w_down, ffn1_post_g, mix_pre_g, w_in, shift_mu,
           decay_w0, decay_w2, iclr_a0, iclr_a2, gate_g2, k_k, k_a, r_k, lnx_w, lnx_b,
           vres_w1, vres_mu, vres_v0, vres_w2, sgu_ln_g, sgu_ln_b, sgu_w_s, sgu_b,
           w_b_rwkv, w_b_attn, w_b_sgu, w_out, mix_post_g,
           ffn2_pre_g, ffn2_w_gu, ffn2_w_down, ffn2_post_g):
    f32 = lambda a: np.ascontiguousarray(np.asarray(a, dtype=np.float32))
    x = f32(x)
    L = DEPTH
    if "nc" not in _PROG:
        _PROG["nc"] = build_fused()
    nc = _PROG["nc"]
    shared = {"ffn1_w_gu": f32(ffn1_w_gu), "ffn2_w_gu": f32(ffn2_w_gu), "ffn1_w_down": f32(ffn1_w_down), "ffn2_w_down": f32(ffn2_w_down),
              "w_in": f32(w_in), "w_b_rwkv": f32(w_b_rwkv), "w_b_attn": f32(w_b_attn), "w_b_sgu": f32(w_b_sgu), "w_out": f32(w_out)}
    glist = []
    for l in range(L):
        glist += [ffn1_pre_g[l], ffn1_post_g[l], mix_pre_g[l], mix_post_g[l], ffn2_pre_g[l], ffn2_post_g[l]]
    shared["gs"] = np.ascontiguousarray(np.stack([_col16(g) for g in glist], axis=1))
    shared.update(rwkv_consts_host())
    shared["ident"] = np.eye(128, dtype=np.float32).astype(BF)
    shared["cmask"] = np.triu(np.ones((128, 128), np.float32))
    shared["wsT"] = np.ascontiguousarray(f32(sgu_w_s).transpose(0, 3, 1, 2))
    shared["lng"] = np.ascontiguousarray(np.broadcast_to(f32(sgu_ln_g)[:, None, :], (L, 128, 512)))
    shared["lnb"] = np.ascontiguousarray(np.broadcast_to(f32(sgu_ln_b)[:, None, :], (L, 128, 512)))
    shared["sgub"] = np.ascontiguousarray(np.broadcast_to(f32(sgu_b)[:, None, :, :], (L, 128, 4, 128)))
    wl = shared["w_in"]
    ims = []
    for c in range(NCORES):
        b, q = c // 4, c % 4
        im = dict(shared)
        im["hT0"] = np.ascontiguousarray(x[b, q * NTOK:(q + 1) * NTOK, :].T)
        fc = np.arange(256 * q, 256 * q + 256)
        heads = [q, 4 + q, 8 + q]
        cols = np.concatenate([fc, 1024 + fc, 2048 + fc, np.arange(3072, 3360)])
        Wkp = np.zeros((L, D, 2240), np.float32)
        pcs, lmus = [], []
        for l in range(L):
            Wkp[l, :, 0:1056] = wl[l][:, cols]
            if l > 0:
                Wkp[l, :, 1056:1088] = f32(vres_w1[l - 1])
            for gi, hd in enumerate(heads):
                Wkp[l, :, 1088 + gi * 128:1088 + (gi + 1) * 128] = wl[l][:, N_SHIFT + hd * 128:N_SHIFT + (hd + 1) * 128]
                Wkp[l, :, 1472 + gi * 128:1472 + (gi + 1) * 128] = wl[l][:, N_SHIFT + 1536 + hd * 128:N_SHIFT + 1536 + (hd + 1) * 128]
                Wkp[l, :, 1856 + gi * 128:1856 + (gi + 1) * 128] = wl[l][:, N_SHIFT + 3072 + hd * 128:N_SHIFT + 3072 + (hd + 1) * 128]
            sm = f32(shift_mu[l])
            pcl = [sm[fc], sm[1024 + fc], sm[2048 + fc], f32(decay_w0[l])[fc], f32(iclr_a0[l])[fc], f32(k_k[l])[fc], f32(k_a[l])[fc],
                   f32(r_k[l]).reshape(1024)[fc], f32(lnx_w[l])[fc], f32(lnx_b[l])[fc],
                   (f32(vres_v0[l - 1])[fc] if l > 0 else np.zeros(256, np.float32))]
            pcs.append(np.stack([p.reshape(2, 128).T for p in pcl], axis=-1))
            lmu = np.zeros((128, 5), np.float32)
            lmu[:64, 0] = sm[3072:3136]; lmu[:64, 1] = sm[3136:3200]; lmu[:128, 2] = sm[3200:3328]; lmu[:32, 3] = sm[3328:3360]
            if l > 0:
                lmu[:32, 4] = f32(vres_mu[l - 1])
            lmus.append(lmu)
        im["Wkp"] = Wkp
        im["pc"] = np.ascontiguousarray(np.stack(pcs, 0)); im["lmu"] = np.ascontiguousarray(np.stack(lmus, 0))
        im["w2"] = np.ascontiguousarray(f32(decay_w2)[:, :, fc]); im["a2"] = np.ascontiguousarray(f32(iclr_a2)[:, :, fc])
        im["g2"] = np.ascontiguousarray(f32(gate_g2)[:, :, fc]); im["vw2"] = np.ascontiguousarray(f32(vres_w2)[0][:, fc])
        im["bias"] = attn_bias_host(q)
        qmk = np.zeros((128, 4), np.float32); qmk[:, q] = 1.0
        im["qmask"] = qmk
        ims.append(im)
    res = run_bass_kernel_spmd(nc, ims, core_ids=list(range(NCORES)))
    out = np.zeros((2, SEQ, D), np.float32)
    for c in range(NCORES):
        b, q = c // 4, c % 4
        out[b, q * NTOK:(q + 1) * NTOK, :] = np.asarray(res.results[c]["hT_out"], np.float32).T
    return out
```

```python
import contextlib
import numpy as np
import concourse.bass as bass
import concourse.mybir as mybir

F32 = mybir.dt.float32
BF16 = mybir.dt.bfloat16
ALU = mybir.AluOpType
AF = mybir.ActivationFunctionType
AX = mybir.AxisListType


class Res:
    __slots__ = ("name", "lw", "rd", "ov")

    def __init__(self, name=""):
        self.name = name
        self.lw = {}
        self.rd = {}
        self.ov = []


class Region:
    def __init__(self, t, nbytes, name):
        self.t = t
        self.nbytes = nbytes
        self.name = name
        self.views = []
        self.dead = []

    def kill_from(self, offset):
        keep = []
        for v in self.views:
            if v.lo >= offset:
                self.dead.append(v)
            else:
                keep.append(v)
        self.views = keep

    def view(self, off, shape, dtype, name=None, nbytes=None):
        esz = {F32: 4, BF16: 2, mybir.dt.int32: 4, mybir.dt.uint8: 1}[dtype]
        n = 1
        for d in shape[1:]:
            n *= d
        nb = n * esz
        assert off % 4 == 0 and off + nb <= self.nbytes, (off, nb, self.nbytes)
        ap = self.t[0:shape[0], off:off + nb].bitcast(dtype)
        if len(shape) == 3:
            ap = ap.rearrange("p (a b) -> p a b", a=shape[1])
        elif len(shape) == 4:
            ap = ap.rearrange("p (a b c) -> p a b c", a=shape[1], b=shape[2])
        v = Tile(ap, name or f"{self.name}@{off}")
        v.lo, v.hi = off, off + nb
        for o in self.dead:
            if o.lo < v.hi and v.lo < o.hi:
                for q in o.all_res():
                    for tok in list(q.lw.values()) + list(q.rd.values()):
                        k = id(tok[0])
                        if k not in v.r.rd or v.r.rd[k][1] < tok[1]:
                            v.r.rd[k] = tok
        for o in self.views:
            if o.lo < v.hi and v.lo < o.hi:
                for q in o.all_res():
                    q.ov.append(v.r)
                    v.r.ov.append(q)
                o.partners.append(v)
                v.partners.append(o)
        self.views.append(v)
        return v


class SubRegion:
    def __init__(self, parent, base, nbytes, name):
        self.parent, self.base, self.nbytes, self.name = parent, base, nbytes, name

    def view(self, off, shape, dtype, name=None):
        return self.parent.view(self.base + off, shape, dtype, name or f"{self.name}@{off}")


class Tile:
    def __init__(self, t, name):
        self.t = t
        self.name = name
        self.r = Res(name)
        self.sub = {}
        self.partners = []

    def __getitem__(self, idx):
        return self.t[idx]

    def all_res(self):
        return [self.r] + list(self.sub.values())

    def res(self, i):
        if i not in self.sub:
            r = Res(f"{self.name}.{i}")
            r.ov.append(self.r)
            self.r.ov.append(r)
            for p in self.partners:
                for q in p.all_res():
                    q.ov.append(r)
                    r.ov.append(q)
            self.sub[i] = r
        return self.sub[i]


class _Eng:
    def __init__(self, name):
        self.name = name
        self.q = []
        self.sem = None
        self.count = 0
        self.seen = {}
        self.pend_r = []
        self.pend_w = []
        self.nsem = 0


class Sched:
    EPOCH = 30000

    def __init__(self, nc, same_engine_sync=False):
        self.nc = nc
        self.stack = contextlib.ExitStack()
        self.E = {n: _Eng(n) for n in ("pe", "dve", "act", "pool", "sp")}
        self.same_engine_sync = same_engine_sync
        self.nsem = 0
        self.dma_sems = {}
        self.ntile = 0
        self.arena = None
        self.bump = 0
        self.peak = 0
        self.nops = 0
        self.PB = None

    def new_sem(self, name):
        self.nsem += 1
        return self.stack.enter_context(self.nc.semaphore(f"{name}_{self.nsem}"))

    ARENA_BYTES = 212800

    def sbuf(self, shape, dtype, name=None):
        self.ntile += 1
        name = name or f"t{self.ntile}"
        if self.arena is None:
            self.arena = self.region_raw(self.ARENA_BYTES, "arena")
            self.bump = 0
        esz = {F32: 4, BF16: 2, mybir.dt.int32: 4, mybir.dt.uint8: 1}[dtype]
        n = 1
        for d in shape[1:]:
            n *= d
        nb = (n * esz + 31) // 32 * 32
        off = self.bump
        self.bump += nb
        assert self.bump <= self.ARENA_BYTES, f"arena overflow allocating {name}: {self.bump}"
        self.peak = max(self.peak, self.bump)
        return self.arena.view(off, list(shape), dtype, name, nbytes=n * esz)

    def phase_mark(self):
        return self.bump

    def phase_release(self, mark):
        self.bump = mark
        self.arena.kill_from(mark)

    def region(self, nbytes, name):
        off = self.bump if self.arena is not None else 0
        if self.arena is None:
            self.arena = self.region_raw(self.ARENA_BYTES, "arena")
            self.bump = 0
            off = 0
        self.bump += (nbytes + 31) // 32 * 32
        assert self.bump <= self.ARENA_BYTES, f"arena overflow allocating region {name}: {self.bump}"
        self.peak = max(self.peak, self.bump)
        return SubRegion(self.arena, off, nbytes, name)

    def region_raw(self, nbytes, name):
        self.ntile += 1
        t = self.stack.enter_context(self.nc.sbuf_tensor(f"{name}_{self.ntile}", [128, nbytes], mybir.dt.uint8))
        return Region(t, nbytes, name)

    def psum(self, shape, dtype, name=None):
        self.ntile += 1
        name = name or f"p{self.ntile}"
        t = self.stack.enter_context(self.nc.psum_tensor(f"{name}_{self.ntile}", list(shape), dtype))
        return Tile(t, name)

    @staticmethod
    def _resl(x):
        out = []
        for r in x:
            if isinstance(r, Tile):
                out.append(r.r)
            elif r is not None:
                out.append(r)
        return out

    @staticmethod
    def _put(d, tok):
        k = id(tok[0])
        if k not in d or d[k][1] < tok[1]:
            d[k] = tok

    def _collect(self, E, reads, writes, is_dma, wadd=(), nosync=False):
        waits = {}

        def need(tok):
            sem, val, src, tdma = tok
            if (not tdma) and src == E.name and (nosync or not self.same_engine_sync):
                return
            k = id(sem)
            if E.seen.get(k, 0) >= val:
                return
            if k not in waits or waits[k][1] < val:
                waits[k] = (sem, val)

        for r in reads:
            for tok in r.lw.values():
                need(tok)
            for o in r.ov:
                for tok in o.lw.values():
                    need(tok)
        for r in writes:
            for tok in r.lw.values():
                need(tok)
            for tok in r.rd.values():
                need(tok)
            for o in r.ov:
                for tok in o.lw.values():
                    need(tok)
                for tok in o.rd.values():
                    need(tok)
        for r in wadd:
            for tok in r.rd.values():
                need(tok)
            for o in r.ov:
                for tok in o.lw.values():
                    need(tok)
                for tok in o.rd.values():
                    need(tok)
        for k, (sem, val) in waits.items():
            E.seen[k] = val
        return list(waits.values())

    def op(self, eng, fn, reads=(), writes=(), inc=True, nosync=False):
        E = self.E[eng]
        self.nops += 1
        reads = self._resl(reads)
        writes = self._resl(writes)
        waits = self._collect(E, reads, writes, False, nosync=nosync)
        if not inc:
            E.q.append((waits, fn, None))
            E.pend_r += reads
            E.pend_w += writes
            return
        if E.sem is None or E.count >= self.EPOCH:
            E.sem = self.new_sem(f"s_{eng}")
            E.count = 0
        E.count += 1
        tok = (E.sem, E.count, E.name, False)
        E.q.append((waits, fn, (E.sem, 1)))
        for r in reads + E.pend_r:
            self._put(r.rd, tok)
        for r in writes + E.pend_w:
            r.lw = {id(tok[0]): tok}
            r.rd = {}
        E.pend_r = []
        E.pend_w = []

    def o(self, eng, method, *args, reads=(), writes=(), inc=True, nosync=False, **kwargs):
        return self.op(eng, (lambda e: getattr(e, method)(*args, **kwargs)), reads=reads, writes=writes, inc=inc, nosync=nosync)

    DMA_POOL = 24

    def dma(self, eng, out, in_, reads=(), writes=(), key=None, wadd=(), **kw):
        E = self.E[eng]
        reads = self._resl(reads)
        writes = self._resl(writes)
        wadd = self._resl(wadd)
        self.nops += 1
        waits = self._collect(E, reads, writes, True, wadd)
        pool = self.dma_sems.setdefault(eng, {"sems": [], "pos": 0})
        if len(pool["sems"]) < self.DMA_POOL:
            pool["sems"].append([self.new_sem(f"s_dma_{eng}"), 0])
        ds = pool["sems"][pool["pos"] % self.DMA_POOL] if len(pool["sems"]) == self.DMA_POOL and pool["pos"] >= self.DMA_POOL else pool["sems"][-1]
        pool["pos"] += 1
        if ds[1] > 0:
            k = id(ds[0])
            if E.seen.get(k, 0) < ds[1]:
                waits.append((ds[0], ds[1]))
                E.seen[k] = ds[1]
        if ds[1] + 16 > 16 * 2000:
            ds[0] = self.new_sem(f"s_dma_{eng}")
            ds[1] = 0
        ds[1] += 16
        tok = (ds[0], ds[1], "dma", True)
        E.q.append((waits, (lambda e: e.dma_start(out=out, in_=in_, **kw)), (ds[0], 16)))
        for r in reads:
            self._put(r.rd, tok)
        for r in writes:
            r.lw = {id(tok[0]): tok}
            r.rd = {}
        for r in wadd:
            self._put(r.lw, tok)
        return tok

    def collective(self, kind, alu, groups, in_ap, out_ap, reads=(), writes=(), wadd=()):
        E = self.E["pool"]
        reads = self._resl(reads)
        writes = self._resl(writes)
        wadd = self._resl(wadd)
        waits = self._collect(E, reads, writes, True, wadd)
        sem = self.new_sem("s_cc")
        tok = (sem, 1, "dma", True)
        E.q.append((waits, (lambda e: e.collective_compute(kind, alu, replica_groups=groups, ins=[in_ap], outs=[out_ap])), (sem, 1)))
        for r in reads:
            self._put(r.rd, tok)
        for r in writes:
            r.lw = {id(tok[0]): tok}
            r.rd = {}
        for r in wadd:
            self._put(r.lw, tok)
        return tok

    def wait_all(self, eng, toks):
        E = self.E[eng]
        waits = [(t[0], t[1]) for t in toks]
        E.q.append((waits, None, None))

    def emit(self):
        nc = self.nc
        with nc.Block() as block:
            def run(E):
                def body(e):
                    for waits, fn, inc in E.q:
                        for sem, val in waits:
                            e.wait_ge(sem, val)
                        if fn is None:
                            continue
                        ins = fn(e)
                        if inc is not None:
                            ins.then_inc(inc[0], inc[1])
                return body
            if self.E["sp"].q:
                block.sync(run(self.E["sp"]))
            if self.E["pe"].q:
                block.tensor(run(self.E["pe"]))
            if self.E["dve"].q:
                block.vector(run(self.E["dve"]))
            if self.E["act"].q:
                block.scalar(run(self.E["act"]))
            if self.E["pool"].q:
                block.gpsimd(run(self.E["pool"]))

    def close(self):
        self.stack.close()


D = 2048
F = 5504
DC = 16
FC = 43
RMS_EPS = 1e-6


def alloc_common(S, NT):
    C = {}
    C["NT"] = NT
    C["hid"] = S.region(FC * NT * 2, "hid")
    C["A"] = S.region(DC * NT * 4, "A")
    C["ring"] = S.region(32 * 1024, "ring")
    C["PB"] = S.PB
    C["ones"] = S.sbuf([128, 128], F32, "ones")
    S.o("pool", "memset", C["ones"][:], 1.0, writes=[C["ones"]])
    C["eps"] = S.sbuf([128, 1], F32, "eps")
    S.o("pool", "memset", C["eps"][:], RMS_EPS, writes=[C["eps"]])
    EPS_T[0] = C["eps"]
    C["sil"] = [S.sbuf([128, NT], F32, f"sil{i}") for i in range(2)]
    C["rstd"] = S.sbuf([128, NT], F32, "rstd")
    C["hld"] = [S.sbuf([128, NT], F32, f"hld{i}") for i in range(2)]
    C["xo"] = [C["ring"].view(i * NT * 2, [128, NT], BF16, f"xo{i}") for i in range(2)]
    return C


def rstd_from_ss(S, out_t, out_ap, ss_tiles, ss_aps, n, eps_t=None):
    for i, (st, sa) in enumerate(zip(ss_tiles, ss_aps)):
        w = sa.shape[-1]
        o = out_ap[:, i * 512:i * 512 + w]
        S.o("act", "activation", out=o, in_=sa, func=AF.Sqrt, bias=EPS_T[0][:, 0:1], scale=1.0 / D,
             reads=[st, EPS_T[0]], writes=[out_t])
    S.o("dve", "reciprocal", out=out_ap, in_=out_ap, reads=[out_t], writes=[out_t])


EPS_T = [None]


def prenorm_from_hbm(S, C, hT, gcol_ap, gcol_t, xn, hres=None):
    NT = C["NT"]
    TP = 256
    hv = hT.rearrange("(c p) t -> p c t", p=128)
    hs = [C["hid"].view(i * DC * TP * 4, [128, DC, TP], F32, f"hs{i}") for i in range(2)]
    sq = C["hid"].view(2 * DC * TP * 4, [128, DC, TP], F32, "sq")
    ones = C["ones"]
    for pi in range(NT // TP):
        h = hs[pi % 2]
        ps = C["PB"][pi % 2]
        tsl = slice(pi * TP, (pi + 1) * TP)
        S.dma("sp", h[:], hv[:, :, tsl], reads=list(hres) if hres is not None else [], writes=[h])
        S.o("act", "activation", out=sq[:], in_=h[:], func=AF.Square, reads=[h], writes=[sq])
        for c in range(DC):
            S.o("pe", "matmul", ps[:, 0:TP], lhsT=ones[:], rhs=sq[:, c, :], start=(c == 0), stop=(c == DC - 1),
                 reads=[ones, sq], writes=[ps], inc=(c == DC - 1))
        rs = C["rstd"]
        rstd_from_ss(S, rs, rs[:, 0:TP], [ps], [ps[:, 0:TP]], TP)
        S.o("dve", "tensor_tensor", out=sq[:], in0=h[:], in1=rs[:, 0:TP].unsqueeze(1).to_broadcast([128, DC, TP]), op=ALU.mult,
             reads=[h, rs], writes=[sq])
        S.o("pool", "tensor_tensor", out=xn[:, :, tsl], in0=sq[:], in1=gcol_ap.unsqueeze(2).to_broadcast([128, DC, TP]), op=ALU.mult,
             reads=[sq, gcol_t], writes=[xn])


def ffn_core(S, C, xn, w_gu, w_down, yT):
    NT = C["NT"]
    NH = NT // 512
    PB = C["PB"]
    hid = C["hid"].view(0, [128, FC, NT], BF16, "hidden")
    ring = C["ring"]
    wgv = w_gu.rearrange("(c p) m -> p c m", p=128)
    GW = 2
    wg = [ring.view(i * 16384, [128, DC, GW * 128], BF16, f"wg{i}") for i in range(2)]
    wu = [ring.view(8192 + i * 16384, [128, DC, GW * 128], BF16, f"wu{i}") for i in range(2)]
    ngrp = (FC + GW - 1) // GW

    def load_gu(gi):
        nch = min(GW, FC - gi * GW)
        c0 = gi * GW * 128
        S.dma("pool", wg[gi % 2][:, :, 0:nch * 128], wgv[:, :, c0:c0 + nch * 128], writes=[wg[gi % 2]])
        S.dma("pool", wu[gi % 2][:, :, 0:nch * 128], wgv[:, :, F + c0:F + c0 + nch * 128], writes=[wu[gi % 2]])

    load_gu(0)
    for gi in range(ngrp):
        if gi + 1 < ngrp:
            load_gu(gi + 1)
        nch = min(GW, FC - gi * GW)
        for jc in range(nch):
            j = gi * GW + jc
            s = j % 2
            G = PB[4 * s:4 * s + NH]
            U = PB[4 * s + 2:4 * s + 2 + NH]
            for c in range(DC):
                for (wt, PS) in ((wg[gi % 2], G), (wu[gi % 2], U)):
                    for hf in range(NH):
                        last = (c == DC - 1)
                        S.o("pe", "matmul",
                            PS[hf][:], lhsT=wt[:, c, jc * 128:(jc + 1) * 128], rhs=xn[:, c, hf * 512:(hf + 1) * 512],
                            start=(c == 0), stop=(c == DC - 1),
                            reads=[wt, xn], writes=[PS[hf]], inc=last)
            sil = C["sil"][s]
            for hf in range(NH):
                S.o("act", "activation", out=sil[:, hf * 512:(hf + 1) * 512], in_=G[hf][:], func=AF.Silu,
                     reads=[G[hf]], writes=[sil])
            for hf in range(NH):
                S.o("dve", "tensor_tensor", out=hid[:, j, hf * 512:(hf + 1) * 512], in0=sil[:, hf * 512:(hf + 1) * 512],
                                                                                 in1=U[hf][:], op=ALU.mult,
                     reads=[sil, U[hf]], writes=[hid])

    wdv = w_down.rearrange("(j p) d -> p j d", p=128)
    wd = [ring.view(i * 11264, [128, FC, 128], BF16, f"wd{i}") for i in range(2)]
    ones = C["ones"]
    SS = PB[4:4 + NH]

    def load_d(dc):
        S.dma("pool", wd[dc % 2][:], wdv[:, :, dc * 128:(dc + 1) * 128], writes=[wd[dc % 2]])

    load_d(0)
    for dc in range(DC):
        if dc + 1 < DC:
            load_d(dc + 1)
        Y = PB[2 * (dc % 2):2 * (dc % 2) + NH]
        w = wd[dc % 2]
        for j in range(FC):
            for hf in range(NH):
                S.o("pe", "matmul", Y[hf][:], lhsT=w[:, j, :], rhs=hid[:, j, hf * 512:(hf + 1) * 512],
                                                                    start=(j == 0), stop=(j == FC - 1),
                     reads=[w, hid], writes=[Y[hf]], inc=(j == FC - 1))
        evac_y_tile(S, C, Y, dc, yT, SS, NH)
    return SS


def evac_y_tile(S, C, Y, dc, yT, SS, NH):
    ones = C["ones"]
    sq2 = C["sil"][dc % 2]
    for hf in range(NH):
        S.o("act", "activation", out=yT[:, dc, hf * 512:(hf + 1) * 512], in_=Y[hf][:], func=AF.Copy, reads=[Y[hf]], writes=[yT.res(dc)])
        S.o("dve", "tensor_tensor", out=sq2[:, hf * 512:(hf + 1) * 512], in0=yT[:, dc, hf * 512:(hf + 1) * 512],
            in1=yT[:, dc, hf * 512:(hf + 1) * 512], op=ALU.mult, reads=[yT.res(dc)], writes=[sq2])
    for hf in range(NH):
        S.o("pe", "matmul", SS[hf][:], lhsT=ones[:], rhs=sq2[:, hf * 512:(hf + 1) * 512], start=(dc == 0), stop=(dc == DC - 1),
            reads=[ones, sq2], writes=[SS[hf]])


def residual_epilogue(S, C, yT, SS, hT_in, hres, post_g_scaled, hT_out, hres_out, next_g=None, xnT_out=None, xres_out=None):
    NT = C["NT"]
    NH = NT // 512
    PB = C["PB"]
    rs = C["rstd"]
    rstd_from_ss(S, rs, rs[:], SS, [t[:] for t in SS], NT)
    hv_in = hT_in.rearrange("(c p) t -> p c t", p=128)
    hv_out = hT_out.rearrange("(c p) t -> p c t", p=128)
    ones = C["ones"]
    SS3 = PB[6:6 + NH]
    out_toks = []
    for dc in range(DC):
        hl = C["hld"][dc % 2]
        S.dma("sp", hl[:], hv_in[:, dc, :], reads=[hres[dc]] if hres is not None else [], writes=[hl])
        S.o("dve", "scalar_tensor_tensor", out=yT[:, dc, :], in0=yT[:, dc, :], scalar=post_g_scaled[:, dc:dc + 1], in1=rs[:],
            op0=ALU.mult, op1=ALU.mult, reads=[yT.res(dc), post_g_scaled, rs], writes=[yT.res(dc)])
        S.o("pool", "tensor_tensor", out=yT[:, dc, :], in0=yT[:, dc, :], in1=hl[:], op=ALU.add, reads=[yT.res(dc), hl], writes=[yT.res(dc)])
        out_toks.append(S.dma("act", hv_out[:, dc, :], yT[:, dc, :], reads=[yT.res(dc)], writes=[hres_out[dc]] if hres_out is not None else []))
        if next_g is not None:
            sq3 = C["sil"][dc % 2]
            S.o("act", "activation", out=sq3[:], in_=yT[:, dc, :], func=AF.Square, reads=[yT.res(dc)], writes=[sq3])
            for hf in range(NH):
                S.o("pe", "matmul", SS3[hf][:], lhsT=ones[:], rhs=sq3[:, hf * 512:(hf + 1) * 512], start=(dc == 0), stop=(dc == DC - 1),
                    reads=[ones, sq3], writes=[SS3[hf]])
    if next_g is not None:
        rstd_from_ss(S, rs, rs[:], SS3, [t[:] for t in SS3], NT)
        for dc in range(DC):
            xo = C["xo"][dc % 2]
            S.o("dve", "scalar_tensor_tensor", out=xo[:], in0=yT[:, dc, :], scalar=next_g[:, dc:dc + 1], in1=rs[:],
                op0=ALU.mult, op1=ALU.mult, reads=[yT.res(dc), next_g, rs], writes=[xo])
            out_toks.append(S.dma("sp", xnT_out[dc // 4][(dc % 4) * 128:(dc % 4 + 1) * 128, :], xo[:], reads=[xo], wadd=[xres_out] if xres_out is not None else []))
    return out_toks


def ffn_stage(S, C, hT_in, w_gu, w_down, pre_g, post_g_half, hT_out, next_g=None, xnT_out=None, hres=None, hres_out=None, xres_out=None):
    NT = C["NT"]
    xn = C["A"].view(0, [128, DC, NT], BF16, "xn")
    prenorm_from_hbm(S, C, hT_in, pre_g[:], pre_g, xn, hres)
    yT = C["A"].view(0, [128, DC, NT], F32, "yT")
    SS = ffn_core(S, C, xn, w_gu, w_down, yT)
    return residual_epilogue(S, C, yT, SS, hT_in, hres, post_g_half, hT_out, hres_out, next_g, xnT_out, xres_out)

import numpy as np

HN = 64
LNX_EPS = 64e-5


def rwkv_consts_host():
    n8 = 32
    sel = np.zeros((2 * n8, n8, 128), np.float32)
    for h in range(2):
        for c in range(n8):
            sel[h * n8 + c, c, h * 64:(h + 1) * 64] = 1.0
    ident2 = np.zeros((128, 64), np.float32)
    for h in range(2):
        ident2[h * 64 + np.arange(64), np.arange(64)] = 1.0
    bo = np.zeros((128, 128), np.float32)
    bo[:64, :64] = 1.0
    bo[64:, 64:] = 1.0
    hmask = np.zeros((128, 2), np.float32)
    hmask[:64, 0] = 1.0
    hmask[64:, 1] = 1.0
    return {"c_sel": sel, "c_ident2": ident2, "c_bo": bo, "c_hmask": hmask}


def np_rwkv_ref(p, prm, v_first=None):
    f8 = np.float64
    T = p["r"].shape[0]

    def shift(z, mu):
        zp = np.concatenate([np.zeros_like(z[:1]), z[:-1]], 0)
        return z + (zp - z) * mu

    sig = lambda z: 1 / (1 + np.exp(-z))
    xr, xk, xv = shift(p["r"], prm["mu_r"]), shift(p["k"], prm["mu_k"]), shift(p["v"], prm["mu_v"])
    xw, xa, xg = shift(p["xw"], prm["mu_w"]), shift(p["xa"], prm["mu_a"]), shift(p["xg"], prm["mu_g"])
    H = xr.shape[1] // 64
    z = prm["w0"] + np.tanh(xw) @ prm["w2"]
    w_log = -np.log1p(np.exp(-z)) - 0.5
    decay = np.exp(-np.exp(w_log))
    a = sig(prm["a0"] + xa @ prm["a2"])
    g = sig(xg) @ prm["g2"]
    if v_first is None:
        v = xv
    else:
        pv = shift(p["pv"], prm["mu_pv"])
        v = xv + (v_first - xv) * sig(prm["v0"] + pv @ prm["vw2"])
    kk = (xk * prm["k_k"]).reshape(T, H, 64)
    kk = kk / np.maximum(np.linalg.norm(kk, axis=-1, keepdims=True), 1e-12)
    k = xk * (1 + (a - 1) * prm["k_a"])
    hd = lambda z: z.reshape(T, H, 64)
    r_h, k_h, v_h, a_h, w_h = hd(xr), hd(k), hd(v), hd(a), hd(decay)
    aa, bb = -kk, kk * a_h
    Sst = np.zeros((H, 64, 64), f8)
    ys = np.zeros((T, H, 64), f8)
    for t in range(T):
        sa = np.einsum('hij,hj->hi', Sst, aa[t])
        Sst = Sst * w_h[t][:, None, :] + sa[:, :, None] * bb[t][:, None, :] + v_h[t][:, :, None] * k_h[t][:, None, :]
        ys[t] = np.einsum('hij,hj->hi', Sst, r_h[t])
    mu = ys.mean(-1, keepdims=True)
    var = ((ys - mu) ** 2).mean(-1, keepdims=True)
    y = ((ys - mu) / np.sqrt(var + LNX_EPS)).reshape(T, H * 64) * prm["lnx_w"] + prm["lnx_b"]
    bonus = (r_h * k_h * prm["r_k"].reshape(H, 64)).sum(-1, keepdims=True) * v_h
    y = (y + bonus.reshape(T, H * 64)) * g
    return y, xv


TS = 256
NC8 = TS // 8
MU_R, MU_K, MU_V, W0, A0, K_K, K_A, R_K, LNX_W, LNX_B, V0 = range(11)


def rwkv_program(S, nc, ntile, T, layer1, dr):
    f = lambda shape, name, dt=F32: S.sbuf(shape, dt, name)
    nseg = T // TS
    RB, KB_, VB = 0, ntile * 128, 2 * ntile * 128
    XWB = 3 * ntile * 128
    XAB, XGB, PVB = XWB + 64, XWB + 128, XWB + 288
    sel = f([64, NC8, 128], "sel")
    ident2 = f([128, 64], "ident2")
    bo = f([128, 128], "bo")
    hmask = f([128, 2], "hmask")
    pc = f([128, ntile, 11], "pc")
    pd = f([128, ntile, 4], "pd")
    lmu = f([128, 5], "lmu"); lmu1 = f([128, 5], "lmu1")
    w2 = f([64, ntile * 128], "w2")
    a2 = f([64, ntile * 128], "a2")
    g2a = f([128, ntile * 128], "g2a")
    g2b = f([32, ntile * 128], "g2b")
    vw2 = f([32, ntile * 128], "vw2")
    epsl = f([128, 1], "epsl")
    S.dma("sp", sel[:], dr["c_sel"], writes=[sel])
    selb = f([64, NC8, 128], "selb", BF16)
    S.o("act", "activation", out=selb[:], in_=sel[:], func=AF.Copy, reads=[sel], writes=[selb])
    S.dma("sp", ident2[:], dr["c_ident2"], writes=[ident2])
    S.dma("sp", bo[:], dr["c_bo"], writes=[bo])
    S.dma("sp", hmask[:], dr["c_hmask"], writes=[hmask])
    S.dma("act", pc[:], dr["pc"], writes=[pc])
    S.dma("act", lmu[:], dr["lmu"], writes=[lmu])
    S.dma("act", w2[:], dr["w2"], writes=[w2])
    S.dma("act", a2[:], dr["a2"], writes=[a2])
    S.dma("act", g2a[:], dr["g2"][0:128, :], writes=[g2a])
    S.dma("act", g2b[:], dr["g2"][128:160, :], writes=[g2b])
    if layer1:
        S.dma("act", vw2[:], dr["vw2"], writes=[vw2])
    S.o("pool", "memset", epsl[:], LNX_EPS, writes=[epsl])
    for i, src in enumerate((MU_R, MU_K, MU_V, K_A)):
        S.o("pool", "tensor_scalar", out=pd[:, :, i], in0=pc[:, :, src], scalar1=-1.0, scalar2=1.0, op0=ALU.mult, op1=ALU.add,
             reads=[pc], writes=[pd])
    S.o("pool", "tensor_scalar", out=lmu1[:], in0=lmu[:], scalar1=-1.0, scalar2=1.0, op0=ALU.mult, op1=ALU.add,
         reads=[lmu], writes=[lmu1])
    state = [f([128, 64], f"state{t}") for t in range(ntile)]
    sa = [f([128, 1], f"sa{t}") for t in range(ntile)]
    junk = [f([128, 64], f"junk{t}") for t in range(ntile)]
    for t in range(ntile):
        S.o("dve", "memset", state[t][:], 0.0, writes=[state[t]])
    PB = S.PB
    NRING = 4
    ring = [f([128, 5, 512], f"ring{i}") for i in range(NRING)]
    ringpos = [0]
    W1 = TS + 1
    raw_main = [[f([128, W1], f"raw{t}_{i}") for i in range(3)] for t in range(1)]
    raw_l = {"xw": f([64, W1], "raw_xw"), "xa": f([64, W1], "raw_xa"), "xg0": f([128, W1], "raw_xg0"),
             "xg1": f([32, W1], "raw_xg1"), "pv": f([32, W1], "raw_pv")}
    sh_l = {"xw": f([64, TS], "sh_xw"), "xa": f([64, TS], "sh_xa"), "xg0": f([128, TS], "sh_xg0"),
            "xg1": f([32, TS], "sh_xg1"), "pv": f([32, TS], "sh_pv")}
    xr = f([128, TS], "xr"); xk = f([128, TS], "xk")
    dw = f([128, TS], "dw"); aic = f([128, TS], "aic"); kk = f([128, TS], "kk"); a_s = f([128, TS], "a_s")
    b_s = f([128, TS], "b_s"); kmod = f([128, TS], "kmod"); tmp1 = f([128, TS], "tmp1"); tmp2 = f([128, TS], "tmp2")
    vf = f([128, TS], "vf")
    xm = [f([128, 2, TS], f"xm{i}") for i in range(2)]
    xmpos = [0]
    xtok = [[[f([64, 512], f"xtok{b}_{t}_{x}") if x == 0 else f([64, 2, 512], f"xtok{b}_{t}_{x}", BF16) for x in range(5)] for t in range(ntile)] for b in range(2)]
    vbuf = [[f([128, TS], f"v{b}_{t}") for t in range(ntile)] for b in range(2)]
    gbuf = [[f([128, TS], f"g{b}_{t}") for t in range(ntile)] for b in range(2)]
    bonus = [[f([128, TS], f"bonus{b}_{t}") for t in range(ntile)] for b in range(2)]
    ybuf = [[f([128, TS], f"y{b}_{t}") for t in range(ntile)] for b in range(2)]
    yo = [f([128, TS], f"yo{i}", BF16) for i in range(2)]
    pt1 = f([128, TS], "pt1"); pt2 = f([128, TS], "pt2")
    proj = dr["proj"]
    out_toks = []
    dq = ["sp", "act"]
    dqi = [0]

    def q():
        dqi[0] += 1
        return dq[dqi[0] % 2]

    def load_raw(dst, rows, r0, s):
        t0 = s * TS
        pres = dr["res"]["proj"]
        if s == 0:
            S.o("pool", "memset", dst[0:rows, 0:1], 0.0, writes=[dst])
            S.dma(q(), dst[0:rows, 1:W1], proj[r0:r0 + rows, 0:TS], reads=[pres[0]], writes=[dst])
        else:
            S.dma(q(), dst[0:rows, :], proj[r0:r0 + rows, t0 - 1:t0 + TS], reads=list({id(x): x for x in (pres[(t0 - 1) // 512], pres[t0 // 512])}.values()), writes=[dst])

    def shift_lerp(dst, dst_ap, raw, rows, mu_ap, omu_ap, mu_t):
        S.o("act", "activation", out=dst_ap, in_=raw[0:rows, 1:W1], func=AF.Copy, scale=omu_ap, reads=[raw, mu_t], writes=[dst])
        S.o("pool", "tensor_scalar_mul", out=pt1[0:rows, :], in0=raw[0:rows, 0:TS], scalar1=mu_ap, reads=[raw, mu_t], writes=[pt1])
        S.o("pool", "tensor_add", out=dst_ap, in0=dst_ap, in1=pt1[0:rows, :], reads=[dst, pt1], writes=[dst])

    def head_sum(dst, dst_ap, src, src_ap, scale, ps):
        S.o("pe", "matmul", ps[:, 0:TS], lhsT=bo[:], rhs=src_ap, start=True, stop=True, reads=[bo, src], writes=[ps])
        S.o("act", "activation", out=dst_ap, in_=ps[:, 0:TS], func=AF.Copy, scale=scale, reads=[ps], writes=[dst])

    def prep(s):
        b = s % 2
        lrows = {"xw": (64, XWB, 0), "xa": (64, XAB, 1), "xg0": (128, XGB, 2), "xg1": (32, XGB + 128, 3), "pv": (32, PVB, 4)}
        for nm, (rows, r0, li) in lrows.items():
            if nm == "pv" and not layer1:
                continue
            load_raw(raw_l[nm], rows, r0, s)
            shift_lerp(sh_l[nm], sh_l[nm][:], raw_l[nm], rows, lmu[0:rows, li:li + 1], lmu1[0:rows, li:li + 1], lmu1)
        S.o("act", "activation", out=sh_l["xw"][:], in_=sh_l["xw"][:], func=AF.Tanh, reads=[sh_l["xw"]], writes=[sh_l["xw"]])
        S.o("act", "activation", out=sh_l["xg0"][:], in_=sh_l["xg0"][:], func=AF.Sigmoid, reads=[sh_l["xg0"]], writes=[sh_l["xg0"]])
        S.o("act", "activation", out=sh_l["xg1"][:], in_=sh_l["xg1"][:], func=AF.Sigmoid, reads=[sh_l["xg1"]], writes=[sh_l["xg1"]])
        for t in range(ntile):
            fs = slice(t * 128, (t + 1) * 128)
            rr, rk, rv = raw_main[0]
            load_raw(rr, 128, RB + t * 128, s)
            load_raw(rk, 128, KB_ + t * 128, s)
            load_raw(rv, 128, VB + t * 128, s)
            v = vbuf[b][t]
            shift_lerp(xr, xr[:], rr, 128, pc[:, t, MU_R:MU_R + 1], pd[:, t, 0:1], pc)
            shift_lerp(xk, xk[:], rk, 128, pc[:, t, MU_K:MU_K + 1], pd[:, t, 1:2], pc)
            shift_lerp(v, v[:], rv, 128, pc[:, t, MU_V:MU_V + 1], pd[:, t, 2:3], pc)
            if not layer1:
                out_toks.append(S.dma(q(), dr["xv"][fs, s * TS:(s + 1) * TS], v[:], reads=[v], wadd=[dr["xv_res"]]))
            ps = PB[0]
            S.o("pe", "matmul", ps[:, 0:TS], lhsT=w2[:, fs], rhs=sh_l["xw"][:], start=True, stop=True, reads=[w2, sh_l["xw"]], writes=[ps])
            S.o("act", "activation", out=dw[:], in_=ps[:, 0:TS], func=AF.Sigmoid, bias=pc[:, t, W0:W0 + 1], reads=[ps, pc], writes=[dw])
            S.o("act", "activation", out=dw[:], in_=dw[:], func=AF.Exp, scale=-float(np.exp(-0.5)), reads=[dw], writes=[dw])
            ps = PB[1]
            S.o("pe", "matmul", ps[:, 0:TS], lhsT=a2[:, fs], rhs=sh_l["xa"][:], start=True, stop=True, reads=[a2, sh_l["xa"]], writes=[ps])
            S.o("act", "activation", out=aic[:], in_=ps[:, 0:TS], func=AF.Sigmoid, bias=pc[:, t, A0:A0 + 1], reads=[ps, pc], writes=[aic])
            ps = PB[2]
            g = gbuf[b][t]
            S.o("pe", "matmul", ps[:, 0:TS], lhsT=g2a[:, fs], rhs=sh_l["xg0"][:], start=True, stop=False, reads=[g2a, sh_l["xg0"]], writes=[ps], inc=False)
            S.o("pe", "matmul", ps[:, 0:TS], lhsT=g2b[:, fs], rhs=sh_l["xg1"][:], start=False, stop=True, reads=[g2b, sh_l["xg1"]], writes=[ps])
            S.o("act", "activation", out=g[:], in_=ps[:, 0:TS], func=AF.Copy, reads=[ps], writes=[g])
            if layer1:
                ps = PB[3]
                S.dma(q(), vf[:], dr["vfirst"][fs, s * TS:(s + 1) * TS], reads=[dr["xv_res"]], writes=[vf])
                S.o("pe", "matmul", ps[:, 0:TS], lhsT=vw2[:, fs], rhs=sh_l["pv"][:], start=True, stop=True, reads=[vw2, sh_l["pv"]], writes=[ps])
                S.o("act", "activation", out=tmp1[:], in_=ps[:, 0:TS], func=AF.Sigmoid, bias=pc[:, t, V0:V0 + 1], reads=[ps, pc], writes=[tmp1])
                S.o("pool", "tensor_sub", out=vf[:], in0=vf[:], in1=v[:], reads=[vf, v], writes=[vf])
                S.o("pool", "tensor_mul", out=vf[:], in0=vf[:], in1=tmp1[:], reads=[vf, tmp1], writes=[vf])
                S.o("pool", "tensor_add", out=v[:], in0=v[:], in1=vf[:], reads=[vf, v], writes=[v])
            S.o("pool", "tensor_scalar_mul", out=kk[:], in0=xk[:], scalar1=pc[:, t, K_K:K_K + 1], reads=[xk, pc], writes=[kk])
            S.o("pool", "tensor_mul", out=tmp1[:], in0=kk[:], in1=kk[:], reads=[kk], writes=[tmp1])
            ps = PB[3]
            S.o("pe", "matmul", ps[:, 0:TS], lhsT=bo[:], rhs=tmp1[:], start=True, stop=True, reads=[bo, tmp1], writes=[ps])
            S.o("act", "activation", out=tmp2[:], in_=ps[:, 0:TS], func=AF.Sqrt, reads=[ps], writes=[tmp2])
            S.o("pool", "tensor_scalar_max", out=tmp2[:], in0=tmp2[:], scalar1=1e-12, reads=[tmp2], writes=[tmp2])
            S.o("dve", "reciprocal", out=tmp2[:], in_=tmp2[:], reads=[tmp2], writes=[tmp2])
            S.o("pool", "tensor_mul", out=kk[:], in0=kk[:], in1=tmp2[:], reads=[kk, tmp2], writes=[kk])
            S.o("pool", "tensor_scalar_mul", out=a_s[:], in0=kk[:], scalar1=-1.0, reads=[kk], writes=[a_s])
            S.o("pool", "tensor_mul", out=b_s[:], in0=kk[:], in1=aic[:], reads=[kk, aic], writes=[b_s])
            S.o("pool", "tensor_scalar", out=tmp1[:], in0=aic[:], scalar1=pc[:, t, K_A:K_A + 1], scalar2=pd[:, t, 3:4], op0=ALU.mult, op1=ALU.add,
                 reads=[aic, pc, pd], writes=[tmp1])
            S.o("pool", "tensor_mul", out=kmod[:], in0=xk[:], in1=tmp1[:], reads=[xk, tmp1], writes=[kmod])
            S.o("pool", "tensor_mul", out=tmp1[:], in0=xr[:], in1=kmod[:], reads=[xr, kmod], writes=[tmp1])
            S.o("pool", "tensor_scalar_mul", out=tmp1[:], in0=tmp1[:], scalar1=pc[:, t, R_K:R_K + 1], reads=[tmp1, pc], writes=[tmp1])
            ps = PB[0]
            S.o("pe", "matmul", ps[:, 0:TS], lhsT=bo[:], rhs=tmp1[:], start=True, stop=True, reads=[bo, tmp1], writes=[ps])
            S.o("act", "activation", out=tmp2[:], in_=ps[:, 0:TS], func=AF.Copy, reads=[ps], writes=[tmp2])
            bn = bonus[b][t]
            S.o("pool", "tensor_mul", out=bn[:], in0=tmp2[:], in1=v[:], reads=[tmp2, v], writes=[bn])
            for xi, X in enumerate((dw, a_s, b_s, kmod, xr)):
                m = xm[xmpos[0] % 2]
                xmpos[0] += 1
                for h in range(2):
                    S.o("pool", "tensor_scalar_mul", out=m[:, h, :], in0=X[:], scalar1=hmask[:, h:h + 1], reads=[X, hmask], writes=[m])
                ps = PB[1 + (xi % 2)]
                for tau in range(8):
                    lh = m[:, :, tau:TS:8]
                    S.o("pe", "matmul", ps[0:2 * NC8, tau * 64:(tau + 1) * 64], lhsT=lh, rhs=ident2[:], start=True, stop=True,
                         reads=[m, ident2], writes=[ps], inc=(tau == 7))
                xt = xtok[b][t][xi]
                if xi == 0:
                    S.o("act", "activation", out=xt[:], in_=ps[0:2 * NC8, :], func=AF.Copy, reads=[ps], writes=[xt])
                else:
                    S.o("act", "activation", out=xt[:, 0, :], in_=ps[0:2 * NC8, :], func=AF.Copy, reads=[ps], writes=[xt])
                    S.o("dve", "tensor_tensor", out=xt[:, 1, :], in0=ps[0:2 * NC8, :], in1=xt[:, 0, :], op=ALU.subtract, reads=[ps, xt], writes=[xt])

    def scan(s):
        b = s % 2
        for c in range(NC8):
            slots = []
            for t in range(ntile):
                slot = ring[ringpos[0] % NRING]
                ringpos[0] += 1
                slots.append(slot)
                for xi in range(5):
                    ps = PB[3 + xi]
                    if xi == 0:
                        S.o("pe", "matmul", ps[:], lhsT=sel[:, c, :], rhs=xtok[b][t][xi][:], start=True, stop=True,
                            reads=[sel, xtok[b][t][xi]], writes=[ps])
                    else:
                        S.o("pe", "matmul", ps[:], lhsT=selb[:, c, :], rhs=xtok[b][t][xi][:, 0, :], start=True, stop=False,
                            reads=[selb, xtok[b][t][xi]], writes=[ps], inc=False)
                        S.o("pe", "matmul", ps[:], lhsT=selb[:, c, :], rhs=xtok[b][t][xi][:, 1, :], start=False, stop=True,
                            reads=[selb, xtok[b][t][xi]], writes=[ps])
                    S.o("act", "activation", out=slot[:, xi, :], in_=ps[:], func=AF.Copy, reads=[ps], writes=[slot])
            for tau in range(8):
                tt = c * 8 + tau
                sl = slice(tau * 64, (tau + 1) * 64)
                for t in range(ntile):
                    S.o("dve", "scalar_tensor_tensor", out=junk[t][:], in0=state[t][:], scalar=1.0, in1=slots[t][:, 1, sl], op0=ALU.mult, op1=ALU.mult, accum_out=sa[t][:],
                         reads=[state[t], slots[t]], writes=[junk[t], sa[t]], inc=False, nosync=True)
                for t in range(ntile):
                    S.o("dve", "tensor_tensor", out=state[t][:], in0=state[t][:], in1=slots[t][:, 0, sl], op=ALU.mult,
                         reads=[state[t], slots[t]], writes=[state[t]], inc=False, nosync=True)
                for t in range(ntile):
                    S.o("dve", "scalar_tensor_tensor", out=state[t][:], in0=slots[t][:, 2, sl], scalar=sa[t][:, 0:1], in1=state[t][:], op0=ALU.mult, op1=ALU.add,
                         reads=[state[t], slots[t], sa[t]], writes=[state[t]], inc=False, nosync=True)
                for t in range(ntile):
                    S.o("dve", "scalar_tensor_tensor", out=state[t][:], in0=slots[t][:, 3, sl], scalar=vbuf[b][t][:, tt:tt + 1], in1=state[t][:], op0=ALU.mult, op1=ALU.add,
                         reads=[state[t], slots[t], vbuf[b][t]], writes=[state[t]], inc=False, nosync=True)
                for t in range(ntile):
                    S.o("dve", "scalar_tensor_tensor", out=junk[t][:], in0=state[t][:], scalar=1.0, in1=slots[t][:, 4, sl], op0=ALU.mult, op1=ALU.mult, accum_out=ybuf[b][t][:, tt:tt + 1],
                         reads=[state[t], slots[t]], writes=[junk[t], ybuf[b][t]], inc=(tau == 7 and t == ntile - 1), nosync=True)

    def post(s):
        b = s % 2
        for t in range(ntile):
            fs = slice(t * 128, (t + 1) * 128)
            y = ybuf[b][t]
            ps = PB[0]
            S.o("pe", "matmul", ps[:, 0:TS], lhsT=bo[:], rhs=y[:], start=True, stop=True, reads=[bo, y], writes=[ps])
            S.o("act", "activation", out=pt2[:], in_=ps[:, 0:TS], func=AF.Copy, scale=1.0 / 64, reads=[ps], writes=[pt2])
            S.o("pool", "tensor_sub", out=y[:], in0=y[:], in1=pt2[:], reads=[y, pt2], writes=[y])
            S.o("pool", "tensor_mul", out=pt2[:], in0=y[:], in1=y[:], reads=[y], writes=[pt2])
            ps = PB[1]
            S.o("pe", "matmul", ps[:, 0:TS], lhsT=bo[:], rhs=pt2[:], start=True, stop=True, reads=[bo, pt2], writes=[ps])
            S.o("act", "activation", out=pt2[:], in_=ps[:, 0:TS], func=AF.Sqrt, bias=epsl[:, 0:1], scale=1.0 / 64, reads=[ps, epsl], writes=[pt2])
            S.o("dve", "reciprocal", out=pt2[:], in_=pt2[:], reads=[pt2], writes=[pt2])
            S.o("pool", "tensor_mul", out=y[:], in0=y[:], in1=pt2[:], reads=[y, pt2], writes=[y])
            S.o("pool", "tensor_scalar", out=y[:], in0=y[:], scalar1=pc[:, t, LNX_W:LNX_W + 1], scalar2=pc[:, t, LNX_B:LNX_B + 1], op0=ALU.mult, op1=ALU.add,
                 reads=[y, pc], writes=[y])
            S.o("pool", "tensor_add", out=y[:], in0=y[:], in1=bonus[b][t][:], reads=[y, bonus[b][t]], writes=[y])
            o = yo[t % 2]
            S.o("pool", "tensor_mul", out=o[:], in0=y[:], in1=gbuf[b][t][:], reads=[y, gbuf[b][t]], writes=[o])
            out_toks.append(S.dma(q(), dr["y"][t][:, s * TS:(s + 1) * TS], o[:], reads=[o], wadd=[dr["y_res"]]))

    prep(0)
    for s in range(nseg):
        if s + 1 < nseg:
            prep(s + 1)
        scan(s)
        post(s)
    return out_toks


DCH = 16


def proj_program(S, nc, T, layer1, dr):
    PB = S.PB
    xall = [a.rearrange("(r c p) t -> r p c t", r=4, p=128) for a in dr["xn_all"]]
    wv = dr["W"].rearrange("(c p) m -> p c m", p=128)
    NTT = T // 512
    mt = []
    for i in range(6):
        mt.append((i * 128, 128, "proj", i * 128))
    mt += [(768, 64, "proj", 768), (832, 64, "proj", 832), (896, 128, "proj", 896), (1024, 32, "proj", 1024)]
    if layer1:
        mt.append((1056, 32, "proj", 1056))
    for g in range(3):
        mt.append((1088 + g * 128, 128, "q", g * 128))
    for g in range(3):
        mt.append((1472 + g * 128, 128, "k", g * 128))
    MG = 3
    groups = [mt[i:i + MG] for i in range(0, len(mt), MG)]
    wring = [S.sbuf([128, DCH, 384], BF16, f"wring{i}") for i in range(2)]
    xring = [S.sbuf([128, DCH, 512], BF16, f"xring{i}") for i in range(3)]
    stg = [S.sbuf([128, 512], F32, f"stg{i}") for i in range(4)]
    stgb = [S.sbuf([128, 512], BF16, f"stgb{i}") for i in range(4)]
    toks = []
    xi = [0]
    si = [0]
    ei = [0]

    def load_x(tt):
        x = xring[xi[0] % 3]
        xi[0] += 1
        for k4 in range(4):
            S.dma(("sp", "act")[(xi[0] + k4) % 2], x[:, k4 * 4:(k4 + 1) * 4, :], xall[k4][tt // 2][:, :, (tt % 2) * 512:(tt % 2) * 512 + 512], reads=[dr["xn_all_res"]],
                  writes=[x.res(k4)])
        return x

    def load_w(gi):
        grp = groups[gi]
        w = wring[gi % 2]
        off = 0
        for (c0, rows, kind, orow) in grp:
            S.dma("pool", w[:, :, off:off + rows], wv[:, :, c0:c0 + rows], writes=[w.res(off)])
            off += 128
        return w

    wnext = load_w(0)
    for gi, grp in enumerate(groups):
        w = wnext
        if gi + 1 < len(groups):
            wnext = load_w(gi + 1)
        for tt in range(NTT):
            x = load_x(tt)
            for j, (c0, rows, kind, orow) in enumerate(grp):
                ps = PB[(tt % 2) * MG + j]
                for c in range(DCH):
                    S.o("pe", "matmul", ps[0:rows, :], lhsT=w[:, c, j * 128:j * 128 + rows], rhs=x[:, c, :], start=(c == 0), stop=(c == DCH - 1),
                        reads=[w.res(j * 128), x.res(c // 4)], writes=[ps], inc=(c == DCH - 1))
                eng = ("act", "dve")[ei[0] % 2]
                ei[0] += 1
                if kind == "proj":
                    st = stg[si[0] % 4]
                    dst = dr["proj"][orow:orow + rows, tt * 512:(tt + 1) * 512]
                else:
                    st = stgb[si[0] % 4]
                    dst = dr["qT" if kind == "q" else "kT"][orow:orow + rows, tt * 512:(tt + 1) * 512]
                si[0] += 1
                if eng == "act":
                    S.o("act", "activation", out=st[0:rows, :], in_=ps[0:rows, :], func=AF.Copy, reads=[ps], writes=[st])
                else:
                    S.o("dve", "tensor_copy", out=st[0:rows, :], in_=ps[0:rows, :], reads=[ps], writes=[st])
                toks.append(S.dma(("sp", "act")[si[0] % 2], dst, st[0:rows, :], reads=[st], wadd=[dr["res"][kind][tt]]))
    wvv = S.sbuf([128, DCH, 384], BF16, "wvv")
    S.dma("pool", wvv[:], wv[:, :, 1856:2240], writes=[wvv])
    vst = [S.sbuf([128, 384], BF16, f"vst{i}") for i in range(4)]
    for tt in range(NTT):
        x = load_x(tt)
        for bl in range(4):
            ps = PB[6 + bl % 2]
            for c in range(DCH):
                S.o("pe", "matmul", ps[:, 0:384], lhsT=x[:, c, bl * 128:(bl + 1) * 128], rhs=wvv[:, c, :], start=(c == 0), stop=(c == DCH - 1),
                    reads=[wvv, x.res(c // 4)], writes=[ps], inc=(c == DCH - 1))
            st = vst[bl]
            if bl % 2 == 0:
                S.o("act", "activation", out=st[:], in_=ps[:, 0:384], func=AF.Copy, reads=[ps], writes=[st])
            else:
                S.o("dve", "tensor_copy", out=st[:], in_=ps[:, 0:384], reads=[ps], writes=[st])
            r0 = tt * 512 + bl * 128
            toks.append(S.dma(("sp", "act")[bl % 2], dr["vtok"][r0:r0 + 128, :], st[:], reads=[st], wadd=[dr["res"]["v"][tt]]))
    return toks

import numpy as np
import math

ATT_GROUPS = ((128, 1), (512, 4), (2048, 16))
E = 128
NEG = -1.0e30


def alibi_slopes(n_heads):
    def geometric(n):
        start = 2.0 ** (-8.0 / n)
        return [start ** (i + 1) for i in range(n)]
    closest = 2 ** int(math.floor(math.log2(n_heads)))
    slopes = geometric(closest)
    if closest < n_heads:
        slopes += geometric(2 * closest)[0::2][: n_heads - closest]
    return np.array(sorted(slopes, reverse=True), dtype=np.float32)


def attn_bias_host(slot):
    sl = alibi_slopes(12)
    out = np.zeros((3, 2, 128, 256), np.float32)
    i = np.arange(128)[:, None]
    j = np.arange(256)[None, :]
    rel = 128 + i - j
    for g, (win, dil) in enumerate(ATT_GROUPS):
        slope = sl[g * 4 + slot]
        base = -slope * (rel * dil).astype(np.float32)
        m = (rel >= 0) & (rel <= 128)
        out[g, 0] = np.where(m, base, NEG)
        out[g, 1] = np.where(m & (j >= 128), base, NEG)
    return np.ascontiguousarray(out.transpose(2, 0, 1, 3).reshape(128, 6, 256))


def attn_program(S, nc, T, dr):
    PB = S.PB[0:6]
    PT = []
    for i in range(2):
        pt_ = Tile(S.PB[6 + i][:, 0:128].bitcast(BF16).rearrange("p (a b) -> p a b", a=2), f"pt{i}")
        pt_.r = S.PB[6 + i].r
        PT.append(pt_)
    bias = S.sbuf([128, 6, 256], F32, "bias")
    ident = S.sbuf([128, 128], BF16, "ident")
    S.dma("sp", bias[:], dr["bias"], writes=[bias])
    S.dma("sp", ident[:], dr["ident"], writes=[ident])
    q = [S.sbuf([128, T], BF16, f"q{i}") for i in range(2)]
    k = [S.sbuf([128, T], BF16, f"k{i}") for i in range(2)]
    v = [S.sbuf([128, 32, 128], BF16, f"v{i}") for i in range(2)]
    og = [S.sbuf([128, 32, 128], F32, f"og{i}") for i in range(2)]
    lse = [S.sbuf([128, 32], F32, f"lse{i}") for i in range(2)]
    ssb = [S.sbuf([128, 256], F32, f"ssb{i}") for i in range(2)]
    pb16 = [S.sbuf([128, 256], BF16, f"p16_{i}") for i in range(2)]
    ptsb = [S.sbuf([128, 2, 128], BF16, f"ptsb{i}") for i in range(2)]
    mx = [S.sbuf([128, 1], F32, f"mx{i}") for i in range(2)]
    nmx = [S.sbuf([128, 1], F32, f"nmx{i}") for i in range(2)]
    den = [S.sbuf([128, 1], F32, f"den{i}") for i in range(2)]
    rden = [S.sbuf([128, 1], F32, f"rden{i}") for i in range(2)]
    lnd = [S.sbuf([128, 1], F32, f"lnd{i}") for i in range(2)]
    og_res = {}
    lse_res = {}
    NB = T // 128
    bi = 0
    for g, (win, d) in enumerate(ATT_GROUPS):
        gb = g % 2
        S.dma("sp", q[gb][:], dr["qT"][g * 128:(g + 1) * 128, :], reads=dr["res"]["q"], writes=[q[gb]])
        S.dma("act", k[gb][:], dr["kT"][g * 128:(g + 1) * 128, :], reads=dr["res"]["k"], writes=[k[gb]])
        nbr = NB // d
        vsrc = dr["vtok"][:, g * 128:(g + 1) * 128]
        for r in range(d):
            src = vsrc[r:T:d, :].rearrange("(n j) e -> j n e", j=128)
            S.dma(("sp", "act")[r % 2], v[gb][:, r * nbr:(r + 1) * nbr, :], src, reads=dr["res"]["v"], writes=[v[gb].res(r)])
        for r in range(d):
            for n in range(nbr):
                blk = r * nbr + n
                b2 = bi % 2
                bi += 1
                first = (n == 0)
                kc0 = 128 if first else 0
                ps = PB[b2]
                qs = q[gb][:, r + d * 128 * n: r + d * 128 * n + d * 127 + 1: d]
                k0 = r + d * 128 * (n - 1) if not first else r
                nk = 128 if first else 256
                ks = k[gb][:, k0: k0 + d * (nk - 1) + 1: d]
                S.o("pe", "matmul", ps[:, kc0:256], lhsT=qs, rhs=ks, start=True, stop=True, reads=[q[gb], k[gb]], writes=[ps])
                s_ = ssb[b2]
                S.o("dve", "scalar_tensor_tensor", out=s_[:, kc0:256], in0=ps[:, kc0:256], scalar=float(E ** -0.5), in1=bias[:, g * 2 + (1 if first else 0), kc0:256],
                    op0=ALU.mult, op1=ALU.add, reads=[ps, bias], writes=[s_])
                S.o("dve", "tensor_reduce", out=mx[b2][:], in_=s_[:, kc0:256], axis=AX.X, op=ALU.max, reads=[s_], writes=[mx[b2]])
                S.o("dve", "tensor_scalar_mul", out=nmx[b2][:], in0=mx[b2][:], scalar1=-1.0, reads=[mx[b2]], writes=[nmx[b2]])
                p_ = pb16[b2]
                S.o("act", "activation", out=p_[:, kc0:256], in_=s_[:, kc0:256], func=AF.Exp, bias=nmx[b2][:, 0:1], accum_out=den[b2][:],
                    reads=[s_, nmx[b2]], writes=[p_, den[b2]])
                pt = PT[b2]
                for hb in range(2 if not first else 1):
                    kb = hb if not first else 1
                    S.o("pe", "transpose", pt[:, kb, :], p_[:, kb * 128:(kb + 1) * 128], ident[:], reads=[p_, ident], writes=[pt], inc=(hb == (0 if first else 1)))
                pts = ptsb[b2]
                if first:
                    S.o("act", "activation", out=pts[:, 1, :], in_=pt[:, 1, :], func=AF.Copy, reads=[pt], writes=[pts])
                else:
                    S.o("act", "activation", out=pts[:, 0, :], in_=pt[:, 0, :], func=AF.Copy, reads=[pt], writes=[pts])
                    S.o("dve", "tensor_copy", out=pts[:, 1, :], in_=pt[:, 1, :], reads=[pt], writes=[pts])
                po = PB[2 + b2]
                if first:
                    S.o("pe", "matmul", po[:, 0:128], lhsT=pts[:, 1, :], rhs=v[gb][:, blk, :], start=True, stop=True, reads=[pts, v[gb].res(r)], writes=[po])
                else:
                    S.o("pe", "matmul", po[:, 0:128], lhsT=pts[:, 0, :], rhs=v[gb][:, blk - 1, :], start=True, stop=False, reads=[pts, v[gb].res(r)], writes=[po], inc=False)
                    S.o("pe", "matmul", po[:, 0:128], lhsT=pts[:, 1, :], rhs=v[gb][:, blk, :], start=False, stop=True, reads=[pts, v[gb].res(r)], writes=[po])
                S.o("dve", "reciprocal", out=rden[b2][:], in_=den[b2][:], reads=[den[b2]], writes=[rden[b2]])
                S.o("act", "activation", out=og[gb][:, blk, :], in_=po[:, 0:128], func=AF.Copy, scale=rden[b2][:, 0:1], reads=[po, rden[b2]], writes=[og[gb]])
                S.o("act", "activation", out=lnd[b2][:], in_=den[b2][:], func=AF.Ln, reads=[den[b2]], writes=[lnd[b2]])
                S.o("dve", "tensor_add", out=lse[gb][:, blk:blk + 1], in0=lnd[b2][:], in1=mx[b2][:], reads=[lnd[b2], mx[b2]], writes=[lse[gb]])
        for r in range(d):
            for n in range(nbr):
                blk = r * nbr + n
                t0 = r + d * 128 * n
                dst = dr["og"][g, t0:t0 + d * 127 + 1:d, :]
                S.dma(("sp", "act")[blk % 2], dst, og[gb][:, blk, :], reads=[og[gb]], writes=[og_res.setdefault((g, r, n), Res("ogd"))])
                dl = dr["lse"][g, t0:t0 + d * 127 + 1:d].rearrange("(j o) -> j o", o=1)
                S.dma(("sp", "act")[(blk + 1) % 2], dl, lse[gb][:, blk:blk + 1], reads=[lse[gb]], writes=[lse_res.setdefault((g, r, n), Res("lsed"))], allow_slow_non_contiguous=True)
    toks = []
    ogm = [S.sbuf([128, 3, 128], F32, f"ogm{i}") for i in range(2)]
    lsm = [S.sbuf([128, 3], F32, f"lsm{i}") for i in range(2)]
    wm = [S.sbuf([128, 3], F32, f"wm{i}") for i in range(2)]
    mm = [S.sbuf([128, 1], F32, f"mm{i}") for i in range(2)]
    zz = [S.sbuf([128, 1], F32, f"zz{i}") for i in range(2)]
    acc = [S.sbuf([128, 128], F32, f"acc{i}") for i in range(2)]
    yo = [S.sbuf([128, 128], BF16, f"yo{i}") for i in range(2)]
    yoT = [S.sbuf([128, 128], BF16, f"yoT{i}") for i in range(2)]
    for nb in range(NB):
        b2 = nb % 2
        t0 = nb * 128
        rdo = [og_res[(g, r, nb // d)] for g, (_w, d) in enumerate(ATT_GROUPS) for r in range(d)]
        rdl = [lse_res[(g, r, nb // d)] for g, (_w, d) in enumerate(ATT_GROUPS) for r in range(d)]
        S.dma("sp", ogm[b2][:], dr["og"][:, t0:t0 + 128, :].rearrange("g t e -> t g e"), reads=rdo, writes=[ogm[b2]])
        S.dma("act", lsm[b2][:], dr["lse"][:, t0:t0 + 128].rearrange("g t -> t g"), reads=rdl, writes=[lsm[b2]], allow_slow_non_contiguous=True)
        S.o("dve", "tensor_reduce", out=mm[b2][:], in_=lsm[b2][:], axis=AX.X, op=ALU.max, reads=[lsm[b2]], writes=[mm[b2]])
        S.o("dve", "tensor_scalar_mul", out=mm[b2][:], in0=mm[b2][:], scalar1=-1.0, reads=[mm[b2]], writes=[mm[b2]])
        S.o("act", "activation", out=wm[b2][:], in_=lsm[b2][:], func=AF.Exp, bias=mm[b2][:, 0:1], accum_out=zz[b2][:], reads=[lsm[b2], mm[b2]], writes=[wm[b2], zz[b2]])
        S.o("dve", "reciprocal", out=zz[b2][:], in_=zz[b2][:], reads=[zz[b2]], writes=[zz[b2]])
        S.o("dve", "tensor_scalar_mul", out=wm[b2][:], in0=wm[b2][:], scalar1=zz[b2][:, 0:1], reads=[wm[b2], zz[b2]], writes=[wm[b2]])
        S.o("dve", "tensor_scalar_mul", out=acc[b2][:], in0=ogm[b2][:, 0, :], scalar1=wm[b2][:, 0:1], reads=[ogm[b2], wm[b2]], writes=[acc[b2]])
        S.o("dve", "scalar_tensor_tensor", out=acc[b2][:], in0=ogm[b2][:, 1, :], scalar=wm[b2][:, 1:2], in1=acc[b2][:], op0=ALU.mult, op1=ALU.add,
            reads=[ogm[b2], wm[b2], acc[b2]], writes=[acc[b2]])
        S.o("dve", "scalar_tensor_tensor", out=yo[b2][:], in0=ogm[b2][:, 2, :], scalar=wm[b2][:, 2:3], in1=acc[b2][:], op0=ALU.mult, op1=ALU.add,
            reads=[ogm[b2], wm[b2], acc[b2]], writes=[yo[b2]])
        ptt = PT[b2]
        S.o("pe", "transpose", ptt[:, 0, :], yo[b2][:], ident[:], reads=[yo[b2], ident], writes=[ptt])
        S.o("act", "activation", out=yoT[b2][:], in_=ptt[:, 0, :], func=AF.Copy, reads=[ptt], writes=[yoT[b2]])
        toks.append(S.dma("sp", dr["yT"][:, t0:t0 + 128], yoT[b2][:], reads=[yoT[b2]], wadd=[dr["y_res"]]))
    return toks


def np_attn_ref(q, k, v, slot):
    T = q.shape[0]
    sl = alibi_slopes(12)
    outs, lses = [], []
    for g, (win, d) in enumerate(ATT_GROUPS):
        slope = float(sl[g * 4 + slot])
        o = np.zeros((T, 128))
        ls = np.zeros(T)
        for t in range(T):
            ks = t - d * np.arange(0, 129)
            ks = ks[ks >= 0]
            s = (k[ks, g] @ q[t, g]) / np.sqrt(128.0) - slope * (t - ks)
            m = s.max()
            p = np.exp(s - m)
            ls[t] = m + np.log(p.sum())
            o[t] = (p / p.sum()) @ v[ks, g]
        outs.append(o)
        lses.append(ls)
    L = np.stack(lses, 0)
    w = np.exp(L - L.max(0))
    w /= w.sum(0)
    return sum(w[g][:, None] * outs[g] for g in range(3))


N_SHIFT = 3360
N_QKV = 4608
SGU0 = N_SHIFT + N_QKV
GATE0 = SGU0 + 1024
LN_EPS = 1e-5
GELU_C = 1.5957691216057308


def gelu_from_psum(S, ps_t, ps_ap, sb, out_t, out_ap):
    S.o("act", "activation", out=sb[1], in_=ps_ap, func=AF.Square, reads=[ps_t], writes=[sb[0]])
    S.o("dve", "tensor_scalar", out=sb[1], in0=sb[1], scalar1=0.044715, scalar2=1.0, op0=ALU.mult, op1=ALU.add, reads=[sb[0]], writes=[sb[0]])
    S.o("dve", "tensor_tensor", out=sb[1], in0=sb[1], in1=ps_ap, op=ALU.mult, reads=[sb[0], ps_t], writes=[sb[0]])
    S.o("act", "activation", out=sb[1], in_=sb[1], func=AF.Sigmoid, scale=GELU_C, reads=[sb[0]], writes=[sb[0]])
    S.o("dve", "tensor_tensor", out=out_ap, in0=sb[1], in1=ps_ap, op=ALU.mult, reads=[sb[0], ps_t], writes=[out_t])


def merge_stage(S, C, dr, P):
    NT = C["NT"]
    NH = NT // 512
    NCH = NT // 128
    PB = C["PB"]
    hidR, AR, ring = C["hid"], C["A"], C["ring"]
    xn = AR.view(0, [128, DC, NT], BF16, "xn_m")
    yall = hidR.view(0, [128, 16, NT], BF16, "yall")
    merged = hidR.view(32 * NT, [128, 16, NT], BF16, "merged")
    vn = hidR.view(64 * NT, [128, NCH, 512], BF16, "vn")
    u = hidR.view(72 * NT, [128, 4, NT], BF16, "u_sgu")
    acc = hidR.view(80 * NT, [128, NT], F32, "acc")
    tmpa = C["hld"][0]
    f = lambda shape, name, dt=F32: S.sbuf(shape, dt, name)
    ab = 32 * NT
    gsc = []
    for i in range(2):
        tl = Tile(C["sil"][i][:, 0:512], f"gsc{i}")
        tl.r = C["sil"][i].r
        gsc.append(tl)
    ge = Tile(C["rstd"][:, 0:512], "ge")
    ge.r = C["rstd"].r
    lng = AR.view(ab, [128, 512], F32, "lng"); lnb = AR.view(ab + 2048, [128, 512], F32, "lnb")
    bsb = AR.view(ab + 4096, [128, 4, 128], F32, "bsb")
    wst = AR.view(ab + 6144, [128, 4, 128], F32, "wst"); cm = AR.view(ab + 8192, [128, 128], F32, "cm")
    wtm = AR.view(ab + 8704, [128, 4, 128], BF16, "wtm")
    st1 = f([128, 4], "st1")
    lneps = f([128, 1], "lneps")
    S.o("pool", "memset", lneps[:], LN_EPS, writes=[lneps])
    for c4 in range(4):
        S.dma(("sp", "act")[c4 % 2], xn[:, c4 * 4:(c4 + 1) * 4, :], dr["xnT"][c4].rearrange("(c p) t -> p c t", p=128), reads=[dr["xn_res"]], writes=[xn.res(c4)])
    xn_all = [xn.res(i) for i in range(4)]
    qm = P["qmask"]
    cand = [AR.view(ab + 12288 + i * 8192, [128, 4, NT], BF16, f"cand{i}") for i in range(2)]
    ya_ = dr["y_all"]
    for ci in range(12):
        if ci < 8:
            ysrc = ya_[ci % 2][(ci // 2) * 128:(ci // 2 + 1) * 128, :]
        else:
            ysrc = ya_[2][(ci - 8) * 128:(ci - 7) * 128, :]
        cd = cand[ci % 2]
        S.dma(("sp", "act")[ci % 2], cd[:], ysrc.rearrange("p (q t) -> p q t", q=4), reads=[dr["y_all_res"]], writes=[cd])
        yr_ = yall.res(ci)
        S.o("dve", "tensor_scalar_mul", out=yall[:, ci, :], in0=cd[:, 0, :], scalar1=qm[:, 0:1], reads=[cd, qm], writes=[yr_])
        for qq in range(1, 4):
            S.o("dve", "scalar_tensor_tensor", out=yall[:, ci, :], in0=cd[:, qq, :], scalar=qm[:, qq:qq + 1], in1=yall[:, ci, :], op0=ALU.mult, op1=ALU.add,
                reads=[cd, qm, yr_], writes=[yr_])
    S.dma("act", lng[:], dr["lng"], writes=[lng])
    S.dma("act", lnb[:], dr["lnb"], writes=[lnb])
    S.dma("act", bsb[:], dr["sgub"], writes=[bsb])
    S.dma("sp", wst[:], dr["wsT"], writes=[wst])
    S.dma("sp", cm[:], dr["cmask"], writes=[cm])
    S.o("dve", "tensor_tensor", out=wtm[:], in0=wst[:], in1=cm[:].unsqueeze(1).to_broadcast([128, 4, 128]), op=ALU.mult, reads=[wst, cm], writes=[wtm])
    wiv = dr["w_in"].rearrange("(c p) m -> p c m", p=128)
    wv_ = ring.view(0, [128, DC, 512], BF16, "w_sguv")
    wu_ = ring.view(16384, [128, DC, 512], BF16, "w_sguu")
    S.dma("pool", wv_[:], wiv[:, :, SGU0 + 512:SGU0 + 1024], writes=[wv_])
    S.dma("pool", wu_[:], wiv[:, :, SGU0:SGU0 + 512], writes=[wu_])
    for ch in range(NCH):
        ps = PB[ch % 2]
        for c in range(DC):
            S.o("pe", "matmul", ps[:], lhsT=xn[:, c, ch * 128:(ch + 1) * 128], rhs=wv_[:, c, :], start=(c == 0), stop=(c == DC - 1),
                reads=[xn.res(c // 4), wv_], writes=[ps], inc=(c == DC - 1))
        sc = gsc[ch % 2]
        gelu_from_psum(S, ps, ps[:], (sc, sc[:]), ge, ge[:])
        S.o("dve", "tensor_reduce", out=st1[:, 0:1], in_=ge[:], axis=AX.X, op=ALU.add, reads=[ge], writes=[st1])
        S.o("dve", "tensor_scalar_mul", out=st1[:, 0:1], in0=st1[:, 0:1], scalar1=-1.0 / 512, reads=[st1], writes=[st1])
        S.o("act", "activation", out=ge[:], in_=ge[:], func=AF.Identity, bias=st1[:, 0:1], reads=[ge, st1], writes=[ge])
        S.o("dve", "scalar_tensor_tensor", out=sc[:], in0=ge[:], scalar=1.0, in1=ge[:], op0=ALU.mult, op1=ALU.mult, accum_out=st1[:, 1:2],
            reads=[ge], writes=[sc, st1])
        S.o("act", "activation", out=st1[:, 2:3], in_=st1[:, 1:2], func=AF.Sqrt, bias=lneps[:, 0:1], scale=1.0 / 512, reads=[st1, lneps], writes=[st1])
        S.o("dve", "reciprocal", out=st1[:, 3:4], in_=st1[:, 2:3], reads=[st1], writes=[st1])
        S.o("dve", "scalar_tensor_tensor", out=ge[:], in0=ge[:], scalar=st1[:, 3:4], in1=lng[:], op0=ALU.mult, op1=ALU.mult, reads=[ge, st1, lng], writes=[ge])
        S.o("dve", "tensor_tensor", out=vn[:, ch, :], in0=ge[:], in1=lnb[:], op=ALU.add, reads=[ge, lnb], writes=[vn])
    for mtile in range(4):
        for hf in range(NH):
            ps = PB[2 + (mtile * NH + hf) % 2]
            for c in range(DC):
                S.o("pe", "matmul", ps[:], lhsT=wu_[:, c, mtile * 128:(mtile + 1) * 128], rhs=xn[:, c, hf * 512:(hf + 1) * 512], start=(c == 0), stop=(c == DC - 1),
                    reads=[xn.res(c // 4), wu_], writes=[ps], inc=(c == DC - 1))
            sc = gsc[(mtile * NH + hf) % 2]
            gelu_from_psum(S, ps, ps[:], (sc, sc[:]), u, u[:, mtile, hf * 512:(hf + 1) * 512])
    for ch in range(NCH):
        ps = PB[4 + ch % 2]
        for g in range(4):
            S.o("pe", "matmul", ps[:, g * 128:(g + 1) * 128], lhsT=vn[:, ch, g * 128:(g + 1) * 128], rhs=wtm[:, g, :], start=True, stop=True,
                reads=[vn, wtm], writes=[ps], inc=(g == 3))
        sc = gsc[ch % 2]
        S.o("dve", "tensor_tensor", out=sc[:], in0=ps[:], in1=bsb[:].rearrange("p g t -> p (g t)"), op=ALU.add, reads=[ps, bsb], writes=[sc])
        S.o("dve", "tensor_tensor", out=yall[:, 12:16, ch * 128:(ch + 1) * 128], in0=sc[:].rearrange("p (g t) -> p g t", g=4),
            in1=u[:, :, ch * 128:(ch + 1) * 128], op=ALU.mult, reads=[sc, u], writes=[yall.res(99)])
    wbv = [dr["w_b_rwkv"].rearrange("(c p) m -> p c m", p=128), dr["w_b_attn"].rearrange("(c p) m -> p c m", p=128),
           dr["w_b_sgu"].rearrange("(c p) m -> p c m", p=128)]
    kcs = [8, 4, 4]
    yoff = [0, 8, 12]
    yres = [[yall.res(i) for i in range(8)], [yall.res(8 + i) for i in range(4)], [yall.res(99)]]
    slabs = [ring.view(i * 16384, [128, 64, 128], BF16, f"mslab{i}") for i in range(2)]

    def load_slab(dt):
        sl = slabs[dt % 2]
        for br in range(3):
            c0 = GATE0 + br * 2048 + dt * 128
            S.dma("pool", sl[:, br * 16:(br + 1) * 16, :], wiv[:, :, c0:c0 + 128], writes=[sl.res(br)])
        o = 48
        for br in range(3):
            S.dma("pool", sl[:, o:o + kcs[br], :], wbv[br][:, :, dt * 128:(dt + 1) * 128], writes=[sl.res(3 + br)])
            o += kcs[br]
        return sl

    nxt = load_slab(0)
    for dt in range(DC):
        sl = nxt
        if dt + 1 < DC:
            nxt = load_slab(dt + 1)
        for br in range(3):
            i = dt * 3 + br
            s = i % 2
            G = PB[4 * s:4 * s + NH]
            Pj = PB[4 * s + 2:4 * s + 2 + NH]
            for c in range(DC):
                for hf in range(NH):
                    S.o("pe", "matmul", G[hf][:], lhsT=sl[:, br * 16 + c, :], rhs=xn[:, c, hf * 512:(hf + 1) * 512], start=(c == 0), stop=(c == DC - 1),
                        reads=[sl.res(br), xn.res(c // 4)], writes=[G[hf]], inc=(c == DC - 1))
            o = 48 + sum(kcs[:br])
            for kc in range(kcs[br]):
                for hf in range(NH):
                    S.o("pe", "matmul", Pj[hf][:], lhsT=sl[:, o + kc, :], rhs=yall[:, yoff[br] + kc, hf * 512:(hf + 1) * 512], start=(kc == 0), stop=(kc == kcs[br] - 1),
                        reads=[sl.res(3 + br)] + yres[br], writes=[Pj[hf]], inc=(kc == kcs[br] - 1))
            gs = C["sil"][s]
            for hf in range(NH):
                hs = slice(hf * 512, (hf + 1) * 512)
                S.o("act", "activation", out=gs[:, hs], in_=G[hf][:], func=AF.Sigmoid, reads=[G[hf]], writes=[gs])
                if br == 0:
                    S.o("dve", "tensor_tensor", out=acc[:, hs], in0=gs[:, hs], in1=Pj[hf][:], op=ALU.mult, reads=[gs, Pj[hf]], writes=[acc])
                elif br == 1:
                    S.o("dve", "tensor_tensor", out=tmpa[:, hs], in0=gs[:, hs], in1=Pj[hf][:], op=ALU.mult, reads=[gs, Pj[hf]], writes=[tmpa])
                    S.o("pool", "tensor_tensor", out=acc[:, hs], in0=acc[:, hs], in1=tmpa[:, hs], op=ALU.add, reads=[acc, tmpa], writes=[acc])
                else:
                    S.o("dve", "tensor_tensor", out=tmpa[:, hs], in0=gs[:, hs], in1=Pj[hf][:], op=ALU.mult, reads=[gs, Pj[hf]], writes=[tmpa])
                    S.o("pool", "tensor_tensor", out=merged[:, dt, hs], in0=acc[:, hs], in1=tmpa[:, hs], op=ALU.add, reads=[acc, tmpa], writes=[merged])
    yT = AR.view(0, [128, DC, NT], F32, "yT_m")
    wov = dr["w_out"].rearrange("(c p) m -> p c m", p=128)
    wo = [ring.view(i * 4096, [128, DC, 128], BF16, f"wo{i}") for i in range(2)]
    SS = PB[4:4 + NH]

    def load_wo(dt):
        S.dma("pool", wo[dt % 2][:], wov[:, :, dt * 128:(dt + 1) * 128], writes=[wo[dt % 2]])

    load_wo(0)
    for dt in range(DC):
        if dt + 1 < DC:
            load_wo(dt + 1)
        Y = PB[2 * (dt % 2):2 * (dt % 2) + NH]
        w = wo[dt % 2]
        for c in range(DC):
            for hf in range(NH):
                S.o("pe", "matmul", Y[hf][:], lhsT=w[:, c, :], rhs=merged[:, c, hf * 512:(hf + 1) * 512], start=(c == 0), stop=(c == DC - 1),
                    reads=[w, merged], writes=[Y[hf]], inc=(c == DC - 1))
        evac_y_tile(S, C, Y, dt, yT, SS, NH)
    return residual_epilogue(S, C, yT, SS, dr["hT"], P.get("hres"), P["post_g"], dr["hT_out"], P.get("hres_out"), P.get("next_g"), dr.get("xnT_out"))


import ml_dtypes
from concourse.bass_utils import run_bass_kernel_spmd

NCORES = 8
SEQ = 4096
NTOK = 1024
BF = ml_dtypes.bfloat16
DEPTH = 2
RGROUPS = [[0, 1, 2, 3], [4, 5, 6, 7]]
_PROG = {}


def _col16(v):
    return np.ascontiguousarray(np.asarray(v, np.float32).reshape(16, 128).T)


def build_fused():
    nc = bass.Bass("TRN2", target_bir_lowering=False)
    di = lambda n, sh, dt=F32: nc.dram_tensor(n, sh, dt, kind="ExternalInput").ap()
    dint = lambda n, sh, dt=F32: nc.dram_tensor(n, sh, dt).ap()
    L = DEPTH
    I = {}
    I["hT0"] = di("hT0", [D, NTOK])
    for nm in ("ffn1_w_gu", "ffn2_w_gu"):
        I[nm] = di(nm, [L, D, 2 * F])
    for nm in ("ffn1_w_down", "ffn2_w_down"):
        I[nm] = di(nm, [L, F, D])
    I["w_in"] = di("w_in", [L, D, 15136])
    I["w_b_rwkv"] = di("w_b_rwkv", [L, 1024, D]); I["w_b_attn"] = di("w_b_attn", [L, 512, D]); I["w_b_sgu"] = di("w_b_sgu", [L, 512, D])
    I["w_out"] = di("w_out", [L, D, D])
    I["gs"] = di("gs", [128, 6 * L, 16])
    I["Wkp"] = di("Wkp", [L, D, 2240])
    I["pc"] = di("pc", [L, 128, 2, 11]); I["lmu"] = di("lmu", [L, 128, 5])
    I["w2"] = di("w2", [L, 64, 256]); I["a2"] = di("a2", [L, 64, 256]); I["g2"] = di("g2", [L, 160, 256]); I["vw2"] = di("vw2", [32, 256])
    for nm, sh in (("c_sel", [64, 32, 128]), ("c_ident2", [128, 64]), ("c_bo", [128, 128]), ("c_hmask", [128, 2])):
        I[nm] = di(nm, sh)
    I["bias"] = di("bias", [128, 6, 256]); I["ident"] = di("ident", [128, 128], BF16)
    I["qmask"] = di("qmask", [128, 4])
    I["wsT"] = di("wsT", [L, 128, 4, 128]); I["cmask"] = di("cmask", [128, 128])
    I["lng"] = di("lng", [L, 128, 512]); I["lnb"] = di("lnb", [L, 128, 512]); I["sgub"] = di("sgub", [L, 128, 4, 128])
    out = nc.dram_tensor("hT_out", [D, NTOK], F32, kind="ExternalOutput").ap()

    S = Sched(nc, same_engine_sync=True)
    S.PB = [S.psum([128, 512], F32, f"pb{i}") for i in range(8)]
    gs = S.sbuf([128, 6 * L, 16], F32, "gs_sb")
    S.dma("sp", gs[:], I["gs"], writes=[gs])
    gt = []
    for i in range(6 * L):
        t = Tile(gs[:, i, :], f"g{i}")
        t.r = gs.r
        gt.append(t)
    gh = S.sbuf([128, 2 * L, 16], F32, "gs_half")
    ght = []
    for l in range(L):
        for j, src in enumerate((1, 5)):
            S.o("dve", "tensor_scalar_mul", out=gh[:, 2 * l + j, :], in0=gs[:, 6 * l + src, :], scalar1=0.5, reads=[gs], writes=[gh])
            t = Tile(gh[:, 2 * l + j, :], f"gh{l}{j}")
            t.r = gh.r
            ght.append(t)
    qm = S.sbuf([128, 4], F32, "qmask_sb")
    S.dma("sp", qm[:], I["qmask"], writes=[qm])

    hs = [dint(f"h_scr{i}", [D, NTOK]) for i in range(5)]
    hres = [[Res(f"h{i}_{d}") for d in range(16)] for i in range(5)]
    xn_own = [[dint(f"xn_own{l}_{k}", [512, NTOK], BF16) for k in range(4)] for l in range(L)]
    xn_own_res = [Res(f"xn_own_res{l}") for l in range(L)]
    xn_all = [[dint(f"xn_all{l}_{k}", [4 * 512, NTOK], BF16) for k in range(4)] for l in range(L)]
    xn_all_res = [Res(f"xn_all_res{l}") for l in range(L)]
    proj = [dint(f"proj{l}", [1088, SEQ]) for l in range(L)]
    qT = [dint(f"qT{l}", [384, SEQ], BF16) for l in range(L)]
    kT = [dint(f"kT{l}", [384, SEQ], BF16) for l in range(L)]
    vtok = [dint(f"vtok{l}", [SEQ, 384], BF16) for l in range(L)]
    og = [dint(f"og{l}", [3, SEQ, 128]) for l in range(L)]
    lse = [dint(f"lse{l}", [3, SEQ]) for l in range(L)]
    y_own = [[dint(f"y_own{l}_{k}", [128, SEQ], BF16) for k in range(3)] for l in range(L)]
    y_own_res = [Res(f"y_own_res{l}") for l in range(L)]
    y_all = [[dint(f"y_all{l}_{k}", [4 * 128, SEQ], BF16) for k in range(3)] for l in range(L)]
    y_all_res = [Res(f"y_all_res{l}") for l in range(L)]
    xv = dint("xv_first", [256, SEQ])
    xv_res = Res("xv_res")

    m0 = S.phase_mark()
    C = alloc_common(S, NTOK)
    ffn_stage(S, C, I["hT0"], I["ffn1_w_gu"][0], I["ffn1_w_down"][0], gt[0], ght[0], hs[0], gt[2], xn_own[0], hres=None, hres_out=hres[0], xres_out=xn_own_res[0])
    S.phase_release(m0)
    hcur = 0
    toks = None
    for l in range(L):
        layer1 = l > 0
        last = l == L - 1
        for k in range(4):
            S.collective("AllGather", ALU.bypass, RGROUPS, xn_own[l][k].opt(), xn_all[l][k].opt(), reads=[xn_own_res[l]], wadd=[xn_all_res[l]])
        res = {"proj": [Res(f"pr{l}_{t}") for t in range(8)], "q": [Res(f"q{l}_{t}") for t in range(8)],
               "k": [Res(f"k{l}_{t}") for t in range(8)], "v": [Res(f"v{l}_{t}") for t in range(8)]}
        m = S.phase_mark()
        proj_program(S, nc, SEQ, layer1, {"xn_all": xn_all[l], "xn_all_res": xn_all_res[l], "W": I["Wkp"][l], "proj": proj[l], "qT": qT[l], "kT": kT[l],
                                          "vtok": vtok[l], "res": res})
        S.phase_release(m)
        m = S.phase_mark()
        drr = {"proj": proj[l], "res": res, "pc": I["pc"][l], "lmu": I["lmu"][l], "w2": I["w2"][l], "a2": I["a2"][l], "g2": I["g2"][l],
               "c_sel": I["c_sel"], "c_ident2": I["c_ident2"], "c_bo": I["c_bo"], "c_hmask": I["c_hmask"],
               "y": y_own[l][0:2], "y_res": y_own_res[l], "xv": xv, "xv_res": xv_res, "vfirst": xv, "vw2": I["vw2"]}
        rwkv_program(S, nc, 2, SEQ, layer1, drr)
        S.phase_release(m)
        m = S.phase_mark()
        attn_program(S, nc, SEQ, {"qT": qT[l], "kT": kT[l], "vtok": vtok[l], "res": res, "bias": I["bias"], "ident": I["ident"], "og": og[l], "lse": lse[l],
                                  "yT": y_own[l][2], "y_res": y_own_res[l]})
        S.phase_release(m)
        for k in range(3):
            S.collective("AllGather", ALU.bypass, RGROUPS, y_own[l][k].opt(), y_all[l][k].opt(), reads=[y_own_res[l]], wadd=[y_all_res[l]])
        m = S.phase_mark()
        C = alloc_common(S, NTOK)
        drm = {"xnT": xn_own[l], "xn_res": xn_own_res[l], "hT": hs[hcur], "y_all": y_all[l], "y_all_res": y_all_res[l],
               "w_in": I["w_in"][l], "w_b_rwkv": I["w_b_rwkv"][l], "w_b_attn": I["w_b_attn"][l], "w_b_sgu": I["w_b_sgu"][l], "w_out": I["w_out"][l],
               "wsT": I["wsT"][l], "cmask": I["cmask"], "lng": I["lng"][l], "lnb": I["lnb"][l], "sgub": I["sgub"][l], "hT_out": hs[hcur + 1]}
        merge_stage(S, C, drm, {"post_g": gt[6 * l + 3], "hres": hres[hcur], "hres_out": hres[hcur + 1], "qmask": qm})
        hcur += 1
        if last:
            toks = ffn_stage(S, C, hs[hcur], I["ffn2_w_gu"][l], I["ffn2_w_down"][l], gt[6 * l + 4], ght[2 * l + 1], out, hres=hres[hcur])
        else:
            ffn_stage(S, C, hs[hcur], I["ffn2_w_gu"][l], I["ffn2_w_down"][l], gt[6 * l + 4], ght[2 * l + 1], hs[hcur + 1], hres=hres[hcur], hres_out=hres[hcur + 1])
            hcur += 1
            ffn_stage(S, C, hs[hcur], I["ffn1_w_gu"][l + 1], I["ffn1_w_down"][l + 1], gt[6 * (l + 1)], ght[2 * (l + 1)], hs[hcur + 1], gt[6 * (l + 1) + 2], xn_own[l + 1],
                      hres=hres[hcur], hres_out=hres[hcur + 1], xres_out=xn_own_res[l + 1])
            hcur += 1
        S.phase_release(m)
    S.wait_all("sp", toks)
    S.emit()
    S.close()
    return nc


def kernel(x, ffn1_pre_g, ffn1_w_gu, ffn1_w_down, ffn1_post_g, mix_pre_g, w_in, shift_mu,
           decay_w0, decay_w2, iclr_a0, iclr_a2, gate_g2, k_k, k_a, r_k, lnx_w, lnx_b,
           vres_w1, vres_mu, vres_v0, vres_w2, sgu_ln_g, sgu_ln_b, sgu_w_s, sgu_b,
           w_b_rwkv, w_b_attn, w_b_sgu, w_out, mix_post_g,
           ffn2_pre_g, ffn2_w_gu, ffn2_w_down, ffn2_post_g):
    f32 = lambda a: np.ascontiguousarray(np.asarray(a, dtype=np.float32))
    x = f32(x)
    L = DEPTH
    if "nc" not in _PROG:
        _PROG["nc"] = build_fused()
    nc = _PROG["nc"]
    shared = {"ffn1_w_gu": f32(ffn1_w_gu), "ffn2_w_gu": f32(ffn2_w_gu), "ffn1_w_down": f32(ffn1_w_down), "ffn2_w_down": f32(ffn2_w_down),
              "w_in": f32(w_in), "w_b_rwkv": f32(w_b_rwkv), "w_b_attn": f32(w_b_attn), "w_b_sgu": f32(w_b_sgu), "w_out": f32(w_out)}
    glist = []
    for l in range(L):
        glist += [ffn1_pre_g[l], ffn1_post_g[l], mix_pre_g[l], mix_post_g[l], ffn2_pre_g[l], ffn2_post_g[l]]
    shared["gs"] = np.ascontiguousarray(np.stack([_col16(g) for g in glist], axis=1))
    shared.update(rwkv_consts_host())
    shared["ident"] = np.eye(128, dtype=np.float32).astype(BF)
    shared["cmask"] = np.triu(np.ones((128, 128), np.float32))
    shared["wsT"] = np.ascontiguousarray(f32(sgu_w_s).transpose(0, 3, 1, 2))
    shared["lng"] = np.ascontiguousarray(np.broadcast_to(f32(sgu_ln_g)[:, None, :], (L, 128, 512)))
    shared["lnb"] = np.ascontiguousarray(np.broadcast_to(f32(sgu_ln_b)[:, None, :], (L, 128, 512)))
    shared["sgub"] = np.ascontiguousarray(np.broadcast_to(f32(sgu_b)[:, None, :, :], (L, 128, 4, 128)))
    wl = shared["w_in"]
    ims = []
    for c in range(NCORES):
        b, q = c // 4, c % 4
        im = dict(shared)
        im["hT0"] = np.ascontiguousarray(x[b, q * NTOK:(q + 1) * NTOK, :].T)
        fc = np.arange(256 * q, 256 * q + 256)
        heads = [q, 4 + q, 8 + q]
        cols = np.concatenate([fc, 1024 + fc, 2048 + fc, np.arange(3072, 3360)])
        Wkp = np.zeros((L, D, 2240), np.float32)
        pcs, lmus = [], []
        for l in range(L):
            Wkp[l, :, 0:1056] = wl[l][:, cols]
            if l > 0:
                Wkp[l, :, 1056:1088] = f32(vres_w1[l - 1])
            for gi, hd in enumerate(heads):
                Wkp[l, :, 1088 + gi * 128:1088 + (gi + 1) * 128] = wl[l][:, N_SHIFT + hd * 128:N_SHIFT + (hd + 1) * 128]
                Wkp[l, :, 1472 + gi * 128:1472 + (gi + 1) * 128] = wl[l][:, N_SHIFT + 1536 + hd * 128:N_SHIFT + 1536 + (hd + 1) * 128]
                Wkp[l, :, 1856 + gi * 128:1856 + (gi + 1) * 128] = wl[l][:, N_SHIFT + 3072 + hd * 128:N_SHIFT + 3072 + (hd + 1) * 128]
            sm = f32(shift_mu[l])
            pcl = [sm[fc], sm[1024 + fc], sm[2048 + fc], f32(decay_w0[l])[fc], f32(iclr_a0[l])[fc], f32(k_k[l])[fc], f32(k_a[l])[fc],
                   f32(r_k[l]).reshape(1024)[fc], f32(lnx_w[l])[fc], f32(lnx_b[l])[fc],
                   (f32(vres_v0[l - 1])[fc] if l > 0 else np.zeros(256, np.float32))]
            pcs.append(np.stack([p.reshape(2, 128).T for p in pcl], axis=-1))
            lmu = np.zeros((128, 5), np.float32)
            lmu[:64, 0] = sm[3072:3136]; lmu[:64, 1] = sm[3136:3200]; lmu[:128, 2] = sm[3200:3328]; lmu[:32, 3] = sm[3328:3360]
            if l > 0:
                lmu[:32, 4] = f32(vres_mu[l - 1])
            lmus.append(lmu)
        im["Wkp"] = Wkp
        im["pc"] = np.ascontiguousarray(np.stack(pcs, 0)); im["lmu"] = np.ascontiguousarray(np.stack(lmus, 0))
        im["w2"] = np.ascontiguousarray(f32(decay_w2)[:, :, fc]); im["a2"] = np.ascontiguousarray(f32(iclr_a2)[:, :, fc])
        im["g2"] = np.ascontiguousarray(f32(gate_g2)[:, :, fc]); im["vw2"] = np.ascontiguousarray(f32(vres_w2)[0][:, fc])
        im["bias"] = attn_bias_host(q)
        qmk = np.zeros((128, 4), np.float32); qmk[:, q] = 1.0
        im["qmask"] = qmk
        ims.append(im)
    res = run_bass_kernel_spmd(nc, ims, core_ids=list(range(NCORES)))
    out = np.zeros((2, SEQ, D), np.float32)
    for c in range(NCORES):
        b, q = c // 4, c % 4
        out[b, q * NTOK:(q + 1) * NTOK, :] = np.asarray(res.results[c]["hT_out"], np.float32).T
    return out
```
